# Optimizing a Trainium2 kernel written in Bass

```python
import math
import jax, jax.numpy as jnp
from jax import lax
import numpy as np

D_MODEL = 1024
BATCH = 4
SEQ = 4096
DEPTH = 2

GRID_W = 64
CTX_LEN = 256
N_MOD = 9
EPS = 1e-6
D_FF = 2816
FFN_RES = 0.5

HEAD_DIM = 64
ROPE_BASE = 10000.0

LRU_WIDTH = 256
LRU_BLOCKS = 4
LRU_BLOCK = LRU_WIDTH // LRU_BLOCKS
CONV_WIDTH = 4
CONV_LEFT = 2
LRU_C = 8.0

SWA_HEADS = 4
SWA_KV_HEADS = 2
SWA_GROUP = SWA_HEADS // SWA_KV_HEADS
WINDOW = 128
QBLK = 128

DIFF_HEADS = 4
DIFF_V_DIM = 2 * HEAD_DIM

IN_SIZES = (LRU_WIDTH, LRU_WIDTH,
            SWA_HEADS * HEAD_DIM, SWA_KV_HEADS * HEAD_DIM, SWA_KV_HEADS * HEAD_DIM,
            DIFF_HEADS * 2 * HEAD_DIM, DIFF_HEADS * 2 * HEAD_DIM, DIFF_HEADS * DIFF_V_DIM)
IN_WIDTH = sum(IN_SIZES)
IN_SPLITS = tuple(int(s) for s in np.cumsum(IN_SIZES)[:-1])
MIX_WIDTH = LRU_WIDTH + SWA_HEADS * HEAD_DIM + DIFF_HEADS * DIFF_V_DIM
NEG_INF = -1e30

kernel_name = 'hybrid_prefix_dit_block'


def rmsnorm(x, g):
    xf = x.astype(jnp.float32)
    y = xf * lax.rsqrt(jnp.mean(xf * xf, axis=-1, keepdims=True) + EPS)
    return (y * g.astype(jnp.float32)).astype(x.dtype)


def modulated_norm(x, g, mod, k):
    shift = mod[:, 3 * k][:, None]
    scale = mod[:, 3 * k + 1][:, None]
    return rmsnorm(x, g) * (1 + scale) + shift


def gate_of(mod, k):
    return mod[:, 3 * k + 2][:, None]


def swiglu(x, w_gu, w_down):
    g, u = jnp.split(x @ w_gu, 2, axis=-1)
    return (jax.nn.silu(g) * u) @ w_down


def ffn_sublayer(h, g, mod, k, w_gu, w_down):
    return h + FFN_RES * gate_of(mod, k) * swiglu(modulated_norm(h, g, mod, k), w_gu, w_down)


def axial_rope_tables(n, dtype):
    rows = n // GRID_W
    row = jnp.repeat(jnp.arange(rows, dtype=jnp.float32), GRID_W)
    col = jnp.tile(jnp.arange(GRID_W, dtype=jnp.float32), rows)
    n_freq = HEAD_DIM // 4
    inv_freq = ROPE_BASE ** (-jnp.arange(n_freq, dtype=jnp.float32) / n_freq)
    ang = jnp.concatenate([row[:, None] * inv_freq, col[:, None] * inv_freq], axis=-1)
    return jnp.cos(ang).astype(dtype), jnp.sin(ang).astype(dtype)


def apply_rope(x, cos, sin):
    x1, x2 = jnp.split(x, 2, axis=-1)
    c = cos[None, :, None]
    s = sin[None, :, None]
    return jnp.concatenate([x1 * c - x2 * s, x1 * s + x2 * c], axis=-1)


def depthwise_conv(x, w, b):
    n = x.shape[1]
    xp = jnp.pad(x, ((0, 0), (CONV_LEFT, CONV_WIDTH - 1 - CONV_LEFT), (0, 0)))
    y = b
    for k in range(CONV_WIDTH):
        y = y + xp[:, k:k + n] * w[k]
    return y


def rglru_coeffs(u, w_r, b_r, w_i, b_i, lam):
    B, n, W = u.shape
    uf = u.astype(jnp.float32)
    ub = uf.reshape(B, n, LRU_BLOCKS, LRU_BLOCK)
    r = jax.nn.sigmoid(jnp.einsum('bnkc,kcd->bnkd', ub, w_r.astype(jnp.float32)).reshape(B, n, W) + b_r.astype(jnp.float32))
    i = jax.nn.sigmoid(jnp.einsum('bnkc,kcd->bnkd', ub, w_i.astype(jnp.float32)).reshape(B, n, W) + b_i.astype(jnp.float32))
    log_a = -LRU_C * r * jax.nn.softplus(-lam.astype(jnp.float32))
    a = jnp.exp(log_a)
    mult = jnp.sqrt(-jnp.expm1(2.0 * log_a))
    return a, mult * i * uf


def linear_scan(a, b, h0, reverse):
    if h0 is not None:
        edge = -1 if reverse else 0
        b = b.at[:, edge].add(a[:, edge] * h0)

    def combine(left, right):
        a_l, b_l = left
        a_r, b_r = right
        return a_l * a_r, a_r * b_l + b_r

    _, h = lax.associative_scan(combine, (a, b), reverse=reverse, axis=1)
    return h


def sink_softmax(logits, sink):
    sink_b = jnp.broadcast_to(sink, logits.shape[:-1] + (1,))
    return jax.nn.softmax(jnp.concatenate([logits, sink_b], axis=-1), axis=-1)


def window_attention_latent(q, k, v, kc, vc, sink):
    B, S, _, d = q.shape
    nb = S // QBLK
    scale = d ** -0.5
    qb = q.reshape(B, nb, QBLK, SWA_KV_HEADS, SWA_GROUP, d)

    def band(t):
        tp = jnp.pad(t, ((0, 0), (QBLK, QBLK), (0, 0), (0, 0))).reshape(B, nb + 2, QBLK, SWA_KV_HEADS, d)
        return jnp.concatenate([tp[:, :-2], tp[:, 1:-1], tp[:, 2:]], axis=2)

    kband, vband = band(k), band(v)
    s_loc = jnp.einsum('bnqhgd,bnkhd->bnhgqk', qb, kband).astype(jnp.float32) * scale
    blk = jnp.arange(nb)[:, None, None]
    q_abs = blk * QBLK + jnp.arange(QBLK)[None, :, None]
    k_abs = (blk - 1) * QBLK + jnp.arange(3 * QBLK)[None, None, :]
    valid = (k_abs >= 0) & (k_abs < S) & (jnp.abs(q_abs - k_abs) <= WINDOW)
    s_loc = jnp.where(valid[None, :, None, None], s_loc, NEG_INF)
    s_ctx = jnp.einsum('bnqhgd,bchd->bnhgqc', qb, kc).astype(jnp.float32) * scale
    sink_r = sink.astype(jnp.float32).reshape(SWA_KV_HEADS, SWA_GROUP)[None, None, :, :, None, None]
    p = sink_softmax(jnp.concatenate([s_loc, s_ctx], axis=-1), sink_r)
    n_loc = 3 * QBLK
    p_loc = p[..., :n_loc].astype(v.dtype)
    p_ctx = p[..., n_loc:n_loc + kc.shape[1]].astype(v.dtype)
    o = jnp.einsum('bnhgqk,bnkhd->bnqhgd', p_loc, vband) + jnp.einsum('bnhgqc,bchd->bnqhgd', p_ctx, vc)
    return o.reshape(B, S, SWA_HEADS * d)


def window_attention_context(qc, kc, vc, sink):
    B, L, _, d = qc.shape
    qg = qc.reshape(B, L, SWA_KV_HEADS, SWA_GROUP, d)
    s = jnp.einsum('bqhgd,bkhd->bhgqk', qg, kc).astype(jnp.float32) * d ** -0.5
    sink_r = sink.astype(jnp.float32).reshape(SWA_KV_HEADS, SWA_GROUP)[None, :, :, None, None]
    p = sink_softmax(s, sink_r)[..., :-1].astype(vc.dtype)
    o = jnp.einsum('bhgqk,bkhd->bqhgd', p, vc)
    return o.reshape(B, L, SWA_HEADS * d)


def diff_attend(q, k, v, lam):
    d = q.shape[-1]
    s = jnp.einsum('bqhmd,bkhmd->bhmqk', q, k).astype(jnp.float32) * d ** -0.5
    p = jax.nn.softmax(s, axis=-1)
    w = (p[:, :, 0] - lam * p[:, :, 1]).astype(v.dtype)
    return jnp.einsum('bhqk,bkhe->bqhe', w, v)


def diff_attention_latent(q, k, v, kc, vc, lam):
    B, S, H, _, d = q.shape
    nb = S // QBLK
    k_all = jnp.concatenate([k, kc], axis=1)
    v_all = jnp.concatenate([v, vc], axis=1)
    qb = jnp.moveaxis(q.reshape(B, nb, QBLK, H, 2, d), 1, 0)
    o = lax.map(lambda qblk: diff_attend(qblk, k_all, v_all, lam), qb)
    return jnp.moveaxis(o, 0, 1).reshape(B, S, H, 2 * d)


def token_mixer(n, nc, w_in, w_out, conv_w, conv_b, lru_w_r, lru_b_r, lru_w_i, lru_b_i, lru_lambda,
                swa_sink, diff_lambda, diff_subln_g, lambda_init, with_ctx_out):
    B, S, _ = n.shape
    Lc = nc.shape[1]
    lx, lg, sq, sk, sv, dq, dk, dv = jnp.split(n @ w_in, IN_SPLITS, axis=-1)
    lxc, lgc, sqc, skc, svc, dqc, dkc, dvc = jnp.split(nc @ w_in, IN_SPLITS, axis=-1)
    cos, sin = axial_rope_tables(S, n.dtype)

    u = depthwise_conv(lx, conv_w, conv_b)
    uc = depthwise_conv(lxc, conv_w, conv_b)
    h_lat = 0.0
    h_ctx = 0.0
    for dirn, reverse in ((0, False), (1, True)):
        prm = (lru_w_r[dirn], lru_b_r[dirn], lru_w_i[dirn], lru_b_i[dirn], lru_lambda[dirn])
        a_c, b_c = rglru_coeffs(uc, *prm)
        hc = linear_scan(a_c, b_c, None, reverse)
        h0 = hc[:, 0] if reverse else hc[:, -1]
        a_l, b_l = rglru_coeffs(u, *prm)
        h_lat = h_lat + linear_scan(a_l, b_l, h0, reverse)
        if with_ctx_out:
            h_ctx = h_ctx + hc
    y_lru = h_lat.astype(n.dtype) * jax.nn.gelu(lg)

    q_s = apply_rope(sq.reshape(B, S, SWA_HEADS, HEAD_DIM), cos, sin)
    k_s = apply_rope(sk.reshape(B, S, SWA_KV_HEADS, HEAD_DIM), cos, sin)
    v_s = sv.reshape(B, S, SWA_KV_HEADS, HEAD_DIM)
    kc_s = skc.reshape(B, Lc, SWA_KV_HEADS, HEAD_DIM)
    vc_s = svc.reshape(B, Lc, SWA_KV_HEADS, HEAD_DIM)
    y_swa = window_attention_latent(q_s, k_s, v_s, kc_s, vc_s, swa_sink)

    lam_f = diff_lambda.astype(jnp.float32)
    lam = jnp.exp(jnp.sum(lam_f[0] * lam_f[1])) - jnp.exp(jnp.sum(lam_f[2] * lam_f[3])) + lambda_init
    q_d = apply_rope(dq.reshape(B, S, DIFF_HEADS * 2, HEAD_DIM), cos, sin).reshape(B, S, DIFF_HEADS, 2, HEAD_DIM)
    k_d = apply_rope(dk.reshape(B, S, DIFF_HEADS * 2, HEAD_DIM), cos, sin).reshape(B, S, DIFF_HEADS, 2, HEAD_DIM)
    v_d = dv.reshape(B, S, DIFF_HEADS, DIFF_V_DIM)
    kc_d = dkc.reshape(B, Lc, DIFF_HEADS, 2, HEAD_DIM)
    vc_d = dvc.reshape(B, Lc, DIFF_HEADS, DIFF_V_DIM)
    o_d = diff_attention_latent(q_d, k_d, v_d, kc_d, vc_d, lam)
    y_diff = (rmsnorm(o_d, diff_subln_g) * (1 - lambda_init)).reshape(B, S, DIFF_HEADS * DIFF_V_DIM)

    y = jnp.concatenate([y_lru, y_swa, y_diff], axis=-1) @ w_out
    if not with_ctx_out:
        return y, None

    yc_lru = h_ctx.astype(nc.dtype) * jax.nn.gelu(lgc)
    yc_swa = window_attention_context(sqc.reshape(B, Lc, SWA_HEADS, HEAD_DIM), kc_s, vc_s, swa_sink)
    oc_d = diff_attend(dqc.reshape(B, Lc, DIFF_HEADS, 2, HEAD_DIM), kc_d, vc_d, lam)
    yc_diff = (rmsnorm(oc_d, diff_subln_g) * (1 - lambda_init)).reshape(B, Lc, DIFF_HEADS * DIFF_V_DIM)
    yc = jnp.concatenate([yc_lru, yc_swa, yc_diff], axis=-1) @ w_out
    return y, yc


def setup_inputs(seed: int = 0) -> dict:
    key = jax.random.key(seed)
    ks = jax.random.split(key, 24)
    f32 = jnp.float32
    D = D_MODEL

    def nrm(k, shape, scale):
        return jax.random.normal(k, shape, f32) * scale

    u = jax.random.uniform(ks[19], (DEPTH, 2, LRU_WIDTH), f32, 0.9, 0.999)
    a0 = u ** (1.0 / LRU_C)
    return {
        'x': nrm(ks[0], (BATCH, SEQ, D), 1.0),
        'c': nrm(ks[1], (BATCH, D), 1.0),
        'ctx': nrm(ks[2], (BATCH, CTX_LEN, D), 1.0),
        'c_ctx': nrm(ks[3], (D,), 1.0),
        'w_ada': nrm(ks[4], (DEPTH, D, N_MOD * D), 0.5 * D ** -0.5),
        'b_ada': nrm(ks[5], (DEPTH, N_MOD * D), 0.02),
        'norm_g': 1.0 + nrm(ks[6], (DEPTH, 3, D), 0.02),
        'ffn1_w_gu': nrm(ks[7], (DEPTH, D, 2 * D_FF), D ** -0.5),
        'ffn1_w_down': nrm(ks[8], (DEPTH, D_FF, D), D_FF ** -0.5),
        'ffn2_w_gu': nrm(ks[9], (DEPTH, D, 2 * D_FF), D ** -0.5),
        'ffn2_w_down': nrm(ks[10], (DEPTH, D_FF, D), D_FF ** -0.5),
        'w_in': nrm(ks[11], (DEPTH, D, IN_WIDTH), D ** -0.5),
        'w_out': nrm(ks[12], (DEPTH, MIX_WIDTH, D), MIX_WIDTH ** -0.5),
        'conv_w': nrm(ks[13], (DEPTH, CONV_WIDTH, LRU_WIDTH), CONV_WIDTH ** -0.5),
        'conv_b': nrm(ks[14], (DEPTH, LRU_WIDTH), 0.02),
        'lru_w_r': nrm(ks[15], (DEPTH, 2, LRU_BLOCKS, LRU_BLOCK, LRU_BLOCK), LRU_BLOCK ** -0.5),
        'lru_b_r': nrm(ks[16], (DEPTH, 2, LRU_WIDTH), 0.02),
        'lru_w_i': nrm(ks[17], (DEPTH, 2, LRU_BLOCKS, LRU_BLOCK, LRU_BLOCK), LRU_BLOCK ** -0.5),
        'lru_b_i': nrm(ks[18], (DEPTH, 2, LRU_WIDTH), 0.02),
        'lru_lambda': jnp.log(a0) - jnp.log1p(-a0),
        'swa_sink': nrm(ks[20], (DEPTH, SWA_HEADS), 0.5),
        'diff_lambda': nrm(ks[21], (DEPTH, 4, HEAD_DIM), 0.1),
        'diff_subln_g': 1.0 + nrm(ks[22], (DEPTH, DIFF_V_DIM), 0.02),
        'final_g': 1.0 + nrm(ks[23], (D,), 0.02),
    }


def reference(x, c, ctx, c_ctx, w_ada, b_ada, norm_g, ffn1_w_gu, ffn1_w_down, ffn2_w_gu, ffn2_w_down,
              w_in, w_out, conv_w, conv_b, lru_w_r, lru_b_r, lru_w_i, lru_b_i, lru_lambda,
              swa_sink, diff_lambda, diff_subln_g, final_g):
    B = x.shape[0]
    h, hc = x, ctx
    s_c = jax.nn.silu(c)
    s_cc = jax.nn.silu(c_ctx)
    for l in range(DEPTH):
        last = l == DEPTH - 1
        mod = (s_c @ w_ada[l] + b_ada[l]).reshape(B, N_MOD, D_MODEL)
        mod_c = (s_cc @ w_ada[l] + b_ada[l]).reshape(1, N_MOD, D_MODEL)
        lambda_init = 0.8 - 0.6 * math.exp(-0.3 * l)
        h = ffn_sublayer(h, norm_g[l, 0], mod, 0, ffn1_w_gu[l], ffn1_w_down[l])
        hc = ffn_sublayer(hc, norm_g[l, 0], mod_c, 0, ffn1_w_gu[l], ffn1_w_down[l])
        y, yc = token_mixer(modulated_norm(h, norm_g[l, 1], mod, 1), modulated_norm(hc, norm_g[l, 1], mod_c, 1),
                            w_in[l], w_out[l], conv_w[l], conv_b[l], lru_w_r[l], lru_b_r[l], lru_w_i[l], lru_b_i[l],
                            lru_lambda[l], swa_sink[l], diff_lambda[l], diff_subln_g[l], lambda_init, not last)
        h = h + gate_of(mod, 1) * y
        h = ffn_sublayer(h, norm_g[l, 2], mod, 2, ffn2_w_gu[l], ffn2_w_down[l])
        if not last:
            hc = hc + gate_of(mod_c, 1) * yc
            hc = ffn_sublayer(hc, norm_g[l, 2], mod_c, 2, ffn2_w_gu[l], ffn2_w_down[l])
    return rmsnorm(h, final_g)
```

```python
import math
import numpy as np
import concourse.bass as bass
import concourse.mybir as mybir
from concourse.bass_utils import run_bass_kernel_spmd
from contextlib import ExitStack

F32 = mybir.dt.float32
BF16 = mybir.dt.bfloat16
AF = mybir.ActivationFunctionType
ALU = mybir.AluOpType

D = 1024
NCH = 8
DFF = 2816
NJ = 22
SEQ = 4096
NCTX = 256
DEPTH = 2
EPS = 1e-6
NMOD = 9
GRID_W = 64
O_LX, O_LG, O_SQ, O_SK, O_SV, O_DQ, O_DK, O_DV = 0, 256, 512, 768, 896, 1024, 1536, 2048
INW = 2560
GELU_K = 2.0 * math.sqrt(2.0 / math.pi)


class Buf:
    __slots__ = ("name", "w", "r", "t")

    def __init__(self, name, t=None):
        self.name = name
        self.w = None
        self.r = {}
        self.t = t

    def __getitem__(self, k):
        return self.t[k]


class Sched:
    ENG = ("pe", "act", "dve", "pool", "sp")

    def __init__(self, nc, es):
        self.nc = nc
        self.es = es
        self.E = {"pe": nc.tensor, "act": nc.scalar, "dve": nc.vector, "pool": nc.gpsimd, "sp": nc.sync}
        self.sems = {}
        self.cnt = {}
        self.seen = {e: {} for e in self.ENG}
        for e in self.ENG:
            self.sems[e] = es.enter_context(nc.semaphore("sem_" + e))
            self.cnt[e] = 0
        self.dsem = {}
        self.nwait = 0
        self.nins = 0

    def sb(self, name, shape, dt, es=None):
        self.uid = getattr(self, "uid", 0) + 1
        t = (es or self.es).enter_context(self.nc.sbuf_tensor(f"{name}_{self.uid}", list(shape), dt))
        return Buf(name, t)

    def ps(self, name, shape, dt=F32, es=None):
        t = (es or self.es).enter_context(self.nc.psum_tensor(name, list(shape), dt))
        return Buf(name, t)

    def _wait(self, eng, ev):
        if ev is None:
            return
        key, val = ev
        if self.seen[eng].get(key, 0) >= val:
            return
        self.E[eng].wait_ge(self.sems[key], val)
        self.seen[eng][key] = val
        self.nwait += 1

    def _deps(self, eng, reads, writes, skip_self=False):
        for b in reads:
            if b.w is not None:
                self._wait(eng, b.w)
        for b in writes:
            if b.w is not None and not (skip_self and b.w[0] == eng):
                self._wait(eng, b.w)
            for k, v in b.r.items():
                if not (skip_self and k == eng):
                    self._wait(eng, (k, v))

    def _mark(self, ev, reads, writes):
        k, v = ev
        for b in reads:
            if b.r.get(k, 0) < v:
                b.r[k] = v
        for b in writes:
            b.w = ev
            b.r = {}

    def op(self, eng, fn, reads=(), writes=(), skip_self=False):
        self._deps(eng, reads, writes, skip_self)
        ins = fn(self.E[eng])
        self.cnt[eng] += 1
        ins.then_inc(self.sems[eng], 1)
        self._mark((eng, self.cnt[eng]), reads, writes)
        self.nins += 1
        return ins

    def mm(self, fns, reads=(), writes=()):
        eng = "pe"
        self._deps(eng, reads, writes, skip_self=True)
        ins = None
        for fn in fns:
            ins = fn(self.E[eng])
            self.nins += 1
        self.cnt[eng] += 1
        ins.then_inc(self.sems[eng], 1)
        self._mark((eng, self.cnt[eng]), reads, writes)
        return ins

    def dma(self, q, out, in_, reads=(), writes=(), sembuf=None, **kw):
        self._deps(q, reads, writes)
        name = sembuf.name
        if name not in self.dsem:
            sem = self.es.enter_context(self.nc.semaphore("ds_" + name))
            self.dsem[name] = [sem, 0]
            self.sems[("d", name)] = sem
        ent = self.dsem[name]
        ins = self.E[q].dma_start(out=out, in_=in_, **kw)
        ins.then_inc(ent[0], 16)
        ent[1] += 16
        self._mark((("d", name), ent[1]), reads, writes)
        self.nins += 1
        return ins

    def dma_group(self, q, pairs, reads=(), writes=(), sembuf=None, **kw):
        self._deps(q, reads, writes)
        name = sembuf.name
        if name not in self.dsem:
            sem = self.es.enter_context(self.nc.semaphore("ds_" + name))
            self.dsem[name] = [sem, 0]
            self.sems[("d", name)] = sem
        ent = self.dsem[name]
        for (o, i) in pairs:
            self.E[q].dma_start(out=o, in_=i, **kw).then_inc(ent[0], 16)
            ent[1] += 16
            self.nins += 1
        self._mark((("d", name), ent[1]), reads, writes)

    def barrier(self):
        for e in self.ENG:
            for e2 in self.ENG:
                if e2 != e and self.cnt[e2] > 0:
                    self._wait(e, (e2, self.cnt[e2]))
            for name, ent in self.dsem.items():
                if ent[1] > 0:
                    self._wait(e, (("d", name), ent[1]))


class Kern:
    def __init__(self, NQ, debug=None):
        self.NQ = NQ
        self.T = NQ + NCTX
        self.TK = SEQ + NCTX
        self.debug = debug or set()
        self.tiles = [(i * 512, 512, 0) for i in range(NQ // 512)] + [(NQ, NCTX, 1)]
        self.supers = []
        cur = []
        tot = 0
        for t in self.tiles:
            if tot + t[1] > 2304:
                self.supers.append(cur)
                cur, tot = [], 0
            cur.append(t)
            tot += t[1]
        self.supers.append(cur)
        self.STMAX = max(sum(t[1] for t in s) for s in self.supers)

    def build(self):
        nc = bass.Bass("TRN2", target_bir_lowering=False)
        self.nc = nc
        NQ, T, TK = self.NQ, self.T, self.TK

        def inp(name, shape, dt=F32):
            return nc.dram_tensor(name, list(shape), dt, kind="ExternalInput").ap()

        def scratch(name, shape, dt):
            return nc.dram_tensor(name, list(shape), dt, kind="Internal").ap()

        self.x = inp("x", [NQ, D])
        self.ctx = inp("ctx", [NCTX, D])
        self.cvec = inp("cvec", [2, D])
        self.ropec = inp("ropec", [128, NQ])
        self.ropes = inp("ropes", [128, NQ])
        self.w_ada = inp("w_ada", [DEPTH, D, NMOD * D])
        self.b_ada = inp("b_ada", [DEPTH, NMOD * D])
        self.norm_g = inp("norm_g", [DEPTH, 3, D])
        self.ffn_w_gu = [inp("ffn1_w_gu", [DEPTH, D, 2 * DFF]), inp("ffn2_w_gu", [DEPTH, D, 2 * DFF])]
        self.ffn_w_down = [inp("ffn1_w_down", [DEPTH, DFF, D]), inp("ffn2_w_down", [DEPTH, DFF, D])]
        self.w_in = inp("w_in", [DEPTH, D, INW])
        self.w_out = inp("w_out", [DEPTH, D, D])
        self.conv_w = inp("conv_w", [DEPTH, 4, 256])
        self.conv_b = inp("conv_b", [DEPTH, 256])
        self.lru_w_r = inp("lru_w_r", [DEPTH, 2, 4, 64, 64])
        self.lru_b_r = inp("lru_b_r", [DEPTH, 2, 256])
        self.lru_w_i = inp("lru_w_i", [DEPTH, 2, 4, 64, 64])
        self.lru_b_i = inp("lru_b_i", [DEPTH, 2, 256])
        self.lru_lambda = inp("lru_lambda", [DEPTH, 2, 256])
        self.swa_sink = inp("swa_sink", [DEPTH, 4])
        self.diff_lambda = inp("diff_lambda", [DEPTH, 4, 64])
        self.diff_subln_g = inp("diff_subln_g", [DEPTH, 128])
        self.final_g = inp("final_g", [D])
        self.stopping = any(k.startswith("stop_") for k in self.debug)
        self.out = nc.dram_tensor("out", [128 if self.stopping else NQ, D], F32, kind="ExternalOutput").ap()

        self.hs = scratch("hs", [NCH, 128, T], F32)
        self.lxs = scratch("lxs", [2, 128, TK], F32)
        self.lgs = scratch("lgs", [2, 128, T], F32)
        self.sqs = scratch("sqs", [2, 128, T], BF16)
        self.sks = scratch("sks", [2, 128, TK], BF16)
        self.dqs = scratch("dqs", [4, 128, T], BF16)
        self.dks = scratch("dks", [4, 128, TK], BF16)
        self.svs = scratch("svs", [TK, 128], BF16)
        self.dvs = scratch("dvs", [TK, 512], BF16)
        self.yms = scratch("yms", [NCH, 128, T], BF16)

        with ExitStack() as es:
            self.es = es
            S = Sched(nc, es)
            self.S = S
            self.PB = [S.ps(f"pb{i}", [128, 512], F32) for i in range(8)]
            self.consts()
            self.mods()
            self.phase0()
            if "modT" in self.debug:
                md = nc.dram_tensor("modT_o", [128, DEPTH * 72 * 2], F32, kind="ExternalOutput").ap()
                S.dma("sp", md[:, :], self.modT[:].rearrange("p l i j -> p (l i j)"), reads=[self.modT], sembuf=self.modT)
                S.barrier()
            for l in range(DEPTH):
                if "stop_p0" in self.debug:
                    break
                last = l == DEPTH - 1
                self.ffn(l, 0, self.tiles)
                if "h1" in self.debug and l == 0:
                    self.dump_ct("h1", self.hs, F32)
                if "stop_ffn1" in self.debug and l == 0:
                    break
                self.proj(l)
                if "proj" in self.debug and l == 0:
                    for nm, tn, dt in (("lxs", self.lxs, F32), ("lgs", self.lgs, F32), ("sqs", self.sqs, BF16), ("sks", self.sks, BF16), ("dqs", self.dqs, BF16), ("dks", self.dks, BF16)):
                        self.dump_ct(nm, tn, dt)
                    self.dump_tm("svs", self.svs, BF16)
                    self.dump_tm("dvs", self.dvs, BF16)
                if "stop_proj" in self.debug and l == 0:
                    break
                with ExitStack() as esm:
                    ym = S.sb("ymix", [128, NCH, T], BF16, esm)
                    if "no_lru" not in self.debug:
                        self.lru(l, ym, not last)
                    if "no_swa" not in self.debug:
                        self.swa(l, ym, not last)
                    if "no_diff" not in self.debug:
                        self.diff(l, ym, not last)
                    S.barrier()
                    if "yms" in self.debug and l == 0:
                        S.dma("sp", self.hsv(self.yms, 0, T), ym[:, :, :], reads=[ym], sembuf=ym)
                        S.barrier()
                        self.dump_ct("yms", self.yms, BF16)
                    tl = self.tiles if not last else [t for t in self.tiles if t[2] == 0]
                    self.down(l, ym, NCH, self.w_out[l], 5, 1.0, tl, [t[0] for t in tl])
                if "h2" in self.debug and l == 0:
                    self.dump_ct("h2", self.hs, F32)
                if "stop_mix" in self.debug and l == 0:
                    break
                tl = self.tiles if not last else [t for t in self.tiles if t[2] == 0]
                self.ffn(l, 2, tl)
                if "h3" in self.debug and l == 0:
                    self.dump_ct("h3", self.hs, F32)
                if "stop_l0" in self.debug and l == 0:
                    break
            if not self.stopping:
                self.final()
            S.barrier()
            print("instructions", S.nins, "waits", S.nwait, "sems", len(S.dsem) + 5)
        return nc

    def dump_ct(self, name, ten, dt):
        S, nc = self.S, self.nc
        C = ten.shape[0]
        o = nc.dram_tensor("d_" + name, [C, 128, 512], dt, kind="ExternalOutput").ap()
        b = Buf("dbgdump")
        S.dma_group("sp", [(o[:, :, d0:d0 + n], ten[:, :, a:a + n]) for (a, n, d0) in ((0, 128, 0), (self.NQ - 128, 128, 128), (self.NQ, 256, 256))], sembuf=b)
        S.barrier()

    def dump_tm(self, name, ten, dt):
        S, nc = self.S, self.nc
        E = ten.shape[1]
        o = nc.dram_tensor("d_" + name, [512, E], dt, kind="ExternalOutput").ap()
        b = Buf("dbgdump")
        S.dma_group("sp", [(o[d0:d0 + n, :], ten[a:a + n, :]) for (a, n, d0) in ((0, 128, 0), (SEQ - 128, 128, 128), (SEQ, 256, 256))], sembuf=b)
        S.barrier()

    def hsv(self, ten, off, n):
        return ten[:, :, off:off + n].rearrange("c p t -> p c t")

    def consts(self):
        S, nc, es = self.S, self.nc, self.es
        K = dict(allow_slow_non_contiguous=True)
        self.cbuf = Buf("cbuf")

        cbufs = []

        def cl(name, shape, src, dt=F32):
            b = S.sb(name, shape, dt)
            S.dma("sp", b[:], src, sembuf=self.cbuf, **K)
            cbufs.append(b)
            return b

        def cs(name, shape):
            return S.sb(name, shape, F32)

        def ld(b, dst, src):
            S.dma("sp", dst, src, sembuf=self.cbuf, **K)
            cbufs.append(b)

        self.sT = cs("sT", [128, NCH, 2])
        for j in range(2):
            ld(self.sT, self.sT[:, :, j], self.cvec[j].rearrange("(c p) -> p c", p=128))
        self.bT = cs("bT", [128, DEPTH, 72])
        self.NG = cs("NG", [128, DEPTH, 3, NCH])
        self.CW = cs("CW", [128, DEPTH, 4, 2])
        self.CB = cs("CB", [128, DEPTH, 2])
        self.LBR = cs("LBR", [128, DEPTH, 2, 2])
        self.LBI = cs("LBI", [128, DEPTH, 2, 2])
        self.LAM = cs("LAM", [128, DEPTH, 2, 2])
        self.SUBG = cs("SUBG", [128, DEPTH])
        for l in range(DEPTH):
            ld(self.bT, self.bT[:, l, :], self.b_ada[l].rearrange("(i p) -> p i", p=128))
            for k in range(3):
                ld(self.NG, self.NG[:, l, k, :], self.norm_g[l, k].rearrange("(c p) -> p c", p=128))
            for k in range(4):
                ld(self.CW, self.CW[:, l, k, :], self.conv_w[l, k].rearrange("(i p) -> p i", p=128))
            ld(self.CB, self.CB[:, l, :], self.conv_b[l].rearrange("(i p) -> p i", p=128))
            for d in range(2):
                ld(self.LBR, self.LBR[:, l, d, :], self.lru_b_r[l, d].rearrange("(i p) -> p i", p=128))
                ld(self.LBI, self.LBI[:, l, d, :], self.lru_b_i[l, d].rearrange("(i p) -> p i", p=128))
                ld(self.LAM, self.LAM[:, l, d, :], self.lru_lambda[l, d].rearrange("(i p) -> p i", p=128))
            ld(self.SUBG, self.SUBG[:, l:l + 1], self.diff_subln_g[l].rearrange("(p o) -> p o", o=1))
        self.SINK = cl("SINK", [128, DEPTH * 4], self.swa_sink.rearrange("l h -> (l h)").partition_broadcast(128))
        self.DLAM = cl("DLAM", [128, DEPTH * 4 * 64], self.diff_lambda.rearrange("l a d -> (l a d)").partition_broadcast(128))
        self.FG = cl("FG", [128, D], self.final_g.partition_broadcast(128))
        for b in cbufs:
            b.w = (("d", "cbuf"), S.dsem["cbuf"][1])

        self.ident = S.sb("ident", [128, 128], F32)
        S.op("pool", lambda e: e.memset(self.ident[:], 0.0), writes=[self.ident])
        S.op("pool", lambda e: e.affine_select(out=self.ident[:], in_=self.ident[:], pattern=[[-1, 128]], compare_op=ALU.not_equal, fill=1.0, base=0, channel_multiplier=1), reads=[self.ident], writes=[self.ident])
        self.ones_b = S.sb("ones_b", [128, 128], BF16)
        S.op("pool", lambda e: e.memset(self.ones_b[:], 1.0), writes=[self.ones_b])
        self.ones_f = S.sb("ones_f", [128, 128], F32)
        S.op("pool", lambda e: e.memset(self.ones_f[:], 1.0), writes=[self.ones_f])
        self.onesL = S.sb("onesL", [128, 128], BF16)
        self.onesR = S.sb("onesR", [128, 128], BF16)
        S.op("pool", lambda e: e.memset(self.onesL[:], 0.0), writes=[self.onesL])
        S.op("pool", lambda e: e.memset(self.onesL[:, 0:64], 1.0), writes=[self.onesL])
        S.op("pool", lambda e: e.memset(self.onesR[:], 0.0), writes=[self.onesR])
        S.op("pool", lambda e: e.memset(self.onesR[:, 64:128], 1.0), writes=[self.onesR])
        rt = S.sb("rm_tmp", [128, 2, 128], F32)
        S.op("pool", lambda e: e.memset(rt[:], 0.0), writes=[rt])
        S.op("pool", lambda e: e.affine_select(out=rt[:, 0, :], in_=rt[:, 0, :], pattern=[[-1, 128]], compare_op=ALU.not_equal, fill=1.0, base=32, channel_multiplier=1), reads=[rt], writes=[rt])
        S.op("pool", lambda e: e.affine_select(out=rt[:, 1, :], in_=rt[:, 1, :], pattern=[[-1, 128]], compare_op=ALU.not_equal, fill=-1.0, base=-32, channel_multiplier=1), reads=[rt], writes=[rt])
        for a in (0, 64):
            S.op("pool", lambda e: e.memset(rt[:, 0, a:a + 32], 0.0), reads=[rt], writes=[rt])
            S.op("pool", lambda e: e.memset(rt[:, 1, a + 32:a + 64], 0.0), reads=[rt], writes=[rt])
        self.Rm = S.sb("Rm", [128, 128], BF16)
        S.op("pool", lambda e: e.tensor_tensor(out=self.Rm[:], in0=rt[:, 0, :], in1=rt[:, 1, :], op=ALU.add), reads=[rt], writes=[self.Rm])
        mt = S.sb("mask_tmp", [128, 2, 128], F32)
        S.op("pool", lambda e: e.memset(mt[:], 1.0), writes=[mt])
        S.op("pool", lambda e: e.affine_select(out=mt[:, 0, :], in_=mt[:, 0, :], pattern=[[-1, 128]], compare_op=ALU.is_ge, fill=0.0, base=0, channel_multiplier=1), reads=[mt], writes=[mt])
        S.op("pool", lambda e: e.affine_select(out=mt[:, 1, :], in_=mt[:, 1, :], pattern=[[1, 128]], compare_op=ALU.is_ge, fill=0.0, base=0, channel_multiplier=-1), reads=[mt], writes=[mt])
        self.masks = S.sb("masks", [128, 2, 128], BF16)
        S.op("pool", lambda e: e.tensor_copy(out=self.masks[:], in_=mt[:]), reads=[mt], writes=[self.masks])
        self.eps_t = S.sb("eps_t", [128, 1], F32)
        S.op("pool", lambda e: e.memset(self.eps_t[:], EPS), writes=[self.eps_t])
        self.one_t = S.sb("one_t", [128, 1], F32)
        S.op("pool", lambda e: e.memset(self.one_t[:], 1.0), writes=[self.one_t])

    def mods(self):
        S = self.S
        S.op("act", lambda e: e.activation(out=self.sT[:], in_=self.sT[:], func=AF.Silu), reads=[self.sT], writes=[self.sT])
        self.modT = S.sb("modT", [128, DEPTH, 72, 2], F32)
        self.GS = S.sb("GS", [128, DEPTH, 3, NCH, 2], F32)
        with ExitStack() as esl:
            wa = [S.sb(f"wa{i}", [128, NCH, 512], F32, esl) for i in range(2)]
            for l in range(DEPTH):
                pm = self.PB[l]
                pmv = pm[:, 0:144].rearrange("p (i j) -> p i j", j=2)
                for cb in range(18):
                    w = wa[cb % 2]
                    S.dma("sp", w[:], self.w_ada[l][:, cb * 512:(cb + 1) * 512].rearrange("(c p) n -> p c n", p=128), writes=[w], sembuf=w)
                    fns = []
                    for sub in range(4):
                        idx = cb * 4 + sub
                        for c in range(NCH):
                            fns.append(lambda e, idx=idx, sub=sub, c=c: e.matmul(pmv[:, idx, :], w[:, c, sub * 128:(sub + 1) * 128], self.sT[:, c, :], start=(c == 0), stop=(c == NCH - 1)))
                    S.mm(fns, reads=[w, self.sT], writes=[pm])
                S.op("dve", lambda e: e.tensor_tensor(out=self.modT[:, l], in0=pmv, in1=self.bT[:, l, :].unsqueeze(2).to_broadcast([128, 72, 2]), op=ALU.add), reads=[pm, self.bT], writes=[self.modT])
                for k in range(3):
                    sc = self.modT[:, l, (3 * k + 1) * 8:(3 * k + 2) * 8, :]
                    S.op("dve", lambda e: e.scalar_tensor_tensor(out=self.GS[:, l, k], in0=sc, scalar=1.0, in1=self.NG[:, l, k, :].unsqueeze(2).to_broadcast([128, NCH, 2]), op0=ALU.add, op1=ALU.mult), reads=[self.modT, self.NG], writes=[self.GS])
            S.barrier()

    def mod_ap(self, l, m, c, j):
        return self.modT[:, l, m * 8 + c, j:j + 1]

    def phase0(self):
        S = self.S
        with ExitStack() as esl:
            xin = [S.sb(f"xin{i}", [128, D], F32, esl) for i in range(3)]
            hst = [S.sb(f"hst{i}", [128, NCH, 512], F32, esl) for i in range(2)]
            cnt = 0
            for ti, (off, n, j) in enumerate(self.tiles):
                hb = hst[ti % 2]
                for sub in range(n // 128):
                    xb = xin[cnt % 3]
                    src = self.x[off + sub * 128: off + (sub + 1) * 128, :] if j == 0 else self.ctx[sub * 128:(sub + 1) * 128, :]
                    S.dma("sp", xb[:], src, writes=[xb], sembuf=xb)
                    for half in range(2):
                        pt = self.PB[(cnt * 2 + half) % 8]
                        ptv = pt[:, :].rearrange("p (c t) -> p c t", c=4)
                        S.mm([lambda e, cc=cc: e.transpose(ptv[:, cc, :], xb[:, (half * 4 + cc) * 128:(half * 4 + cc + 1) * 128], self.ident[:]) for cc in range(4)], reads=[xb, self.ident], writes=[pt])
                        eng = "act" if half == 0 else "dve"
                        dst = hb[:, half * 4:(half + 1) * 4, sub * 128:(sub + 1) * 128]
                        if eng == "act":
                            S.op("act", lambda e: e.activation(out=dst, in_=ptv, func=AF.Copy), reads=[pt], writes=[hb])
                        else:
                            S.op("dve", lambda e: e.tensor_copy(out=dst, in_=ptv), reads=[pt], writes=[hb])
                    cnt += 1
                S.dma("sp", self.hsv(self.hs, off, n), hb[:, :, :n], reads=[hb], sembuf=hb)
            S.barrier()

    def norm_pass(self, l, k, tiles, offs, hn, hn_b):
        S = self.S
        with ExitStack() as esl:
            hbs = [S.sb(f"hb{i}", [128, NCH, 512], F32, esl) for i in range(2)]
            sq = S.sb("sq", [128, NCH, 512], BF16, esl)
            rs = [S.sb(f"rs{i}", [128, 512], F32, esl) for i in range(2)]
            for ti, (off, n, j) in enumerate(tiles):
                b = hbs[ti % 2]
                r = rs[ti % 2]
                S.dma("sp", b[:, :, :n], self.hsv(self.hs, off, n), writes=[b], sembuf=b)
                S.op("act", lambda e: e.activation(out=sq[:, :, :n], in_=b[:, :, :n], func=AF.Square), reads=[b], writes=[sq])
                ss = self.PB[4 + ti % 2]
                S.mm([lambda e, c=c: e.matmul(ss[:, :n], self.ones_b[:], sq[:, c, :n], start=(c == 0), stop=(c == NCH - 1)) for c in range(NCH)], reads=[sq, self.ones_b], writes=[ss])
                S.op("act", lambda e: e.activation(out=r[:, :n], in_=ss[:, :n], func=AF.Sqrt, scale=1.0 / D, bias=self.eps_t[:]), reads=[ss, self.eps_t], writes=[r])
                S.op("dve", lambda e: e.reciprocal(out=r[:, :n], in_=r[:, :n]), reads=[r], writes=[r])
                S.op("dve", lambda e: e.tensor_tensor(out=b[:, :, :n], in0=b[:, :, :n], in1=r[:, :n].unsqueeze(1).to_broadcast([128, NCH, n]), op=ALU.mult), reads=[b, r], writes=[b])
                o = offs[ti]
                for c in range(NCH):
                    S.op("act", lambda e, c=c: e.activation(out=hn[:, c, o:o + n], in_=b[:, c, :n], func=AF.Identity, scale=self.GS[:, l, k, c, j:j + 1], bias=self.mod_ap(l, 3 * k, c, j)), reads=[b, self.GS, self.modT], writes=[hn_b[ti]])
            S.barrier()

    def down(self, l, act, nk, wdram, gate_m, coef, tiles, offs, act_b=None):
        S = self.S
        with ExitStack() as esl:
            wd = [S.sb(f"wd{i}", [128, nk, 512], BF16, esl) for i in range(2)]
            hbs = [S.sb(f"hbd{i}", [128, NCH, 512], F32, esl) for i in range(2)]
            gt = S.sb("gt", [128, NCH, 2], F32, esl)
            S.op("pool", lambda e: e.tensor_scalar(out=gt[:], in0=self.modT[:, l, gate_m * 8:(gate_m + 1) * 8, :], scalar1=float(coef), scalar2=0.0, op0=ALU.mult, op1=ALU.add), reads=[self.modT], writes=[gt])
            for half in range(2):
                S.dma("pool", wd[half][:], wdram[:, half * 512:(half + 1) * 512].rearrange("(j p) n -> p j n", p=128), writes=[wd[half]], sembuf=wd[half])
            for ti, (off, n, j) in enumerate(tiles):
                b = hbs[ti % 2]
                o = offs[ti]
                rd = [act_b[ti]] if act_b is not None else [act]
                S.dma("sp", b[:, :, :n], self.hsv(self.hs, off, n), writes=[b], sembuf=b)
                for oc in range(NCH):
                    py = self.PB[4 + oc % 4]
                    w = wd[oc // 4]
                    cs = (oc % 4) * 128
                    S.mm([lambda e, jj=jj: e.matmul(py[:, :n], w[:, jj, cs:cs + 128], act[:, jj, o:o + n], start=(jj == 0), stop=(jj == nk - 1)) for jj in range(nk)], reads=[w] + rd, writes=[py])
                    S.op("dve", lambda e: e.scalar_tensor_tensor(out=b[:, oc, :n], in0=py[:, :n], scalar=gt[:, oc, j:j + 1], in1=b[:, oc, :n], op0=ALU.mult, op1=ALU.add), reads=[py, gt] + ([b] if oc == 0 else []), writes=[b], skip_self=True)
                S.dma("sp", self.hsv(self.hs, off, n), b[:, :, :n], reads=[b], sembuf=b)
            S.barrier()

    def ffn(self, l, k, tiles):
        S = self.S
        wgu = self.ffn_w_gu[0 if k == 0 else 1][l]
        wdn = self.ffn_w_down[0 if k == 0 else 1][l]
        supers = []
        cur, tot = [], 0
        for t in tiles:
            if tot + t[1] > 2304:
                supers.append(cur)
                cur, tot = [], 0
            cur.append(t)
            tot += t[1]
        supers.append(cur)
        for st in supers:
            offs = []
            o = 0
            for t in st:
                offs.append(o)
                o += t[1]
            with ExitStack() as esf:
                act = S.sb("act", [128, NJ, self.STMAX], BF16, esf)
                act_b = [Buf(f"act_b{i}") for i in range(len(st))]
                with ExitStack() as esa:
                    hn = S.sb("hn", [128, NCH, self.STMAX], BF16, esa)
                    hn_b = [Buf(f"hn_b{i}") for i in range(len(st))]
                    self.norm_pass(l, k, st, offs, hn, hn_b)
                    with ExitStack() as es2:
                        wg = [S.sb(f"wg{i}", [128, NCH, 256], BF16, es2) for i in range(2)]
                        wu = [S.sb(f"wu{i}", [128, NCH, 256], BF16, es2) for i in range(2)]
                        sg = [S.sb(f"sg{i}", [128, 512], BF16, es2) for i in range(2)]
                        cnt = 0
                        for jg in range(NJ // 2):
                            s = jg % 2
                            S.dma("pool", wg[s][:], wgu[:, jg * 256:(jg + 1) * 256].rearrange("(c p) n -> p c n", p=128), writes=[wg[s]], sembuf=wg[s])
                            S.dma("pool", wu[s][:], wgu[:, DFF + jg * 256:DFF + (jg + 1) * 256].rearrange("(c p) n -> p c n", p=128), writes=[wu[s]], sembuf=wu[s])
                            for jj in range(2):
                                jx = jg * 2 + jj
                                for ti, (off, n, j) in enumerate(st):
                                    o = offs[ti]
                                    pg = self.PB[(cnt % 2) * 2]
                                    pu = self.PB[(cnt % 2) * 2 + 1]
                                    sgb = sg[cnt % 2]
                                    S.mm([lambda e, c=c: e.matmul(pg[:, :n], wg[s][:, c, jj * 128:(jj + 1) * 128], hn[:, c, o:o + n], start=(c == 0), stop=(c == NCH - 1)) for c in range(NCH)], reads=[wg[s], hn_b[ti]], writes=[pg])
                                    S.mm([lambda e, c=c: e.matmul(pu[:, :n], wu[s][:, c, jj * 128:(jj + 1) * 128], hn[:, c, o:o + n], start=(c == 0), stop=(c == NCH - 1)) for c in range(NCH)], reads=[wu[s], hn_b[ti]], writes=[pu])
                                    S.op("act", lambda e: e.activation(out=sgb[:, :n], in_=pg[:, :n], func=AF.Silu), reads=[pg], writes=[sgb])
                                    S.op("dve", lambda e: e.tensor_tensor(out=act[:, jx, o:o + n], in0=sgb[:, :n], in1=pu[:, :n], op=ALU.mult), reads=[sgb, pu], writes=[act_b[ti]])
                                    cnt += 1
                    S.barrier()
                self.down(l, act, NJ, wdn, 3 * k + 2, 0.5, st, offs, act_b)

    def proj(self, l):
        S = self.S
        NQ = self.NQ
        for st in self.supers:
            offs = []
            o = 0
            for t in st:
                offs.append(o)
                o += t[1]
            with ExitStack() as esa:
                hn = S.sb("hn", [128, NCH, self.STMAX], BF16, esa)
                hn_b = [Buf(f"hn_b{i}") for i in range(len(st))]
                self.norm_pass(l, 1, st, offs, hn, hn_b)
                win = S.sb("win", [128, NCH, INW], BF16, esa)
                wkd = S.sb("wkd", [128, NCH, 256], BF16, esa)
                S.dma_group("pool", [(win[:, :, i * 512:(i + 1) * 512], self.w_in[l][:, i * 512:(i + 1) * 512].rearrange("(c p) n -> p c n", p=128)) for i in range(5)], writes=[win], sembuf=win)
                S.dma_group("pool", [(wkd[:, :, g * 128 + r * 64:g * 128 + r * 64 + 64], self.w_in[l][:, O_SK + 64 * g:O_SK + 64 * g + 64].rearrange("(c p) n -> p c n", p=128)) for g in range(2) for r in range(2)], writes=[wkd], sembuf=wkd)
                ct = [S.sb(f"ropc{i}", [128, 512], F32, esa) for i in range(2)]
                stt = [S.sb(f"rops{i}", [128, 512], F32, esa) for i in range(2)]
                f32st = [S.sb(f"pst{i}", [128, 512], F32, esa) for i in range(3)]
                bfst = [S.sb(f"pbs{i}", [128, 512], BF16, esa) for i in range(4)]
                qb = [S.sb(f"qb{i}", [128, 512], BF16, esa) for i in range(2)]
                t1 = [S.sb(f"rt1_{i}", [128, 512], F32, esa) for i in range(2)]
                t2 = [S.sb(f"rt2_{i}", [128, 512], F32, esa) for i in range(2)]
                vst = [S.sb(f"vst{i}", [128, 640], BF16, esa) for i in range(2)]
                c32 = cbf = crp = cv = 0
                chunks = []
                for i in range(2):
                    chunks.append(("f32", win, O_LX + 128 * i, self.lxs, i, True))
                    chunks.append(("f32", win, O_LG + 128 * i, self.lgs, i, False))
                for g in range(2):
                    chunks.append(("rope", win, O_SQ + 128 * g, self.sqs, g, False))
                    chunks.append(("rope", wkd, 128 * g, self.sks, g, True))
                for h in range(4):
                    chunks.append(("rope", win, O_DQ + 128 * h, self.dqs, h, False))
                    chunks.append(("rope", win, O_DK + 128 * h, self.dks, h, True))
                for ti, (off, n, j) in enumerate(st):
                    o = offs[ti]
                    koff = off if j == 0 else SEQ
                    if j == 0:
                        rc, rsn = ct[ti % 2], stt[ti % 2]
                        S.dma("sp", rc[:, :n], self.ropec[:, off:off + n], writes=[rc], sembuf=rc)
                        S.dma("sp", rsn[:, :n], self.ropes[:, off:off + n], writes=[rsn], sembuf=rsn)
                    for ci, (kind, wt, c0, dst, di, keysp) in enumerate(chunks):
                        pp = self.PB[ci % 4]
                        S.mm([lambda e, c=c: e.matmul(pp[:, :n], wt[:, c, c0:c0 + 128], hn[:, c, o:o + n], start=(c == 0), stop=(c == NCH - 1)) for c in range(NCH)], reads=[wt, hn_b[ti]], writes=[pp])
                        dcol = koff if keysp else off
                        if kind == "f32":
                            sb_ = f32st[c32 % 3]
                            c32 += 1
                            S.op("act", lambda e: e.activation(out=sb_[:, :n], in_=pp[:, :n], func=AF.Copy), reads=[pp], writes=[sb_])
                            S.dma("sp", dst[di, :, dcol:dcol + n], sb_[:, :n], reads=[sb_], sembuf=sb_)
                        elif j == 1:
                            sb_ = bfst[cbf % 4]
                            cbf += 1
                            S.op("act", lambda e: e.activation(out=sb_[:, :n], in_=pp[:, :n], func=AF.Copy), reads=[pp], writes=[sb_])
                            S.dma("sp", dst[di, :, dcol:dcol + n], sb_[:, :n], reads=[sb_], sembuf=sb_)
                        else:
                            q_ = qb[crp % 2]
                            a1 = t1[crp % 2]
                            a2 = t2[crp % 2]
                            pr = self.PB[4 + crp % 2]
                            crp += 1
                            sb_ = bfst[cbf % 4]
                            cbf += 1
                            S.op("act", lambda e: e.activation(out=q_[:, :n], in_=pp[:, :n], func=AF.Copy), reads=[pp], writes=[q_])
                            S.mm([lambda e: e.matmul(pr[:, :n], self.Rm[:], q_[:, :n], start=True, stop=True)], reads=[self.Rm, q_], writes=[pr])
                            S.op("pool", lambda e: e.tensor_tensor(out=a1[:, :n], in0=q_[:, :n], in1=rc[:, :n], op=ALU.mult), reads=[q_, rc], writes=[a1])
                            S.op("dve", lambda e: e.tensor_tensor(out=a2[:, :n], in0=pr[:, :n], in1=rsn[:, :n], op=ALU.mult), reads=[pr, rsn], writes=[a2])
                            S.op("dve", lambda e: e.tensor_tensor(out=sb_[:, :n], in0=a1[:, :n], in1=a2[:, :n], op=ALU.add), reads=[a1, a2], writes=[sb_])
                            S.dma("sp", dst[di, :, dcol:dcol + n], sb_[:, :n], reads=[sb_], sembuf=sb_)
                    for sub in range(n // 128):
                        pv1 = self.PB[6]
                        pv2 = self.PB[7]
                        vs_ = vst[cv % 2]
                        cv += 1
                        S.mm([lambda e, c=c: e.matmul(pv1[:, :512], hn[:, c, o + sub * 128:o + (sub + 1) * 128], win[:, c, O_DV:O_DV + 512], start=(c == 0), stop=(c == NCH - 1)) for c in range(NCH)], reads=[win, hn_b[ti]], writes=[pv1])
                        S.mm([lambda e, c=c: e.matmul(pv2[:, :128], hn[:, c, o + sub * 128:o + (sub + 1) * 128], win[:, c, O_SV:O_SV + 128], start=(c == 0), stop=(c == NCH - 1)) for c in range(NCH)], reads=[win, hn_b[ti]], writes=[pv2])
                        S.op("act", lambda e: e.activation(out=vs_[:, 0:512], in_=pv1[:, :512], func=AF.Copy), reads=[pv1], writes=[vs_])
                        S.op("dve", lambda e: e.tensor_copy(out=vs_[:, 512:640], in_=pv2[:, :128]), reads=[pv2], writes=[vs_])
                        r0 = koff + sub * 128
                        S.dma_group("sp", [(self.dvs[r0:r0 + 128, :], vs_[:, 0:512]), (self.svs[r0:r0 + 128, :], vs_[:, 512:640])], reads=[vs_], sembuf=vs_)
                S.barrier()

    def lru(self, l, ym, with_ctx):
        S = self.S
        NQ, T, TK = self.NQ, self.T, self.TK
        K = dict(allow_slow_non_contiguous=True)
        with ExitStack() as esl:
            XH = S.sb("lru_xh", [128, TK], F32, esl)
            U = S.sb("lru_u", [128, TK], F32, esl)
            A = S.sb("lru_a", [128, TK], F32, esl)
            Bv = S.sb("lru_b", [128, TK], F32, esl)
            M = S.sb("lru_m", [128, TK], F32, esl)
            HS = S.sb("lru_hs", [128, TK], F32, esl)
            LG = S.sb("lru_lg", [128, T], F32, esl)
            WR = S.sb("lru_wr", [128, 2, 2, 128], F32, esl)
            WI = S.sb("lru_wi", [128, 2, 2, 128], F32, esl)
            c8 = S.sb("lru_c8", [128, 2, 2], F32, esl)
            c16 = S.sb("lru_c16", [128, 2, 2], F32, esl)
            S.op("pool", lambda e: e.memset(WR[:], 0.0), writes=[WR])
            S.op("pool", lambda e: e.memset(WI[:], 0.0), writes=[WI])
            idx = [(d, i, bb) for d in range(2) for i in range(2) for bb in range(2)]
            S.dma_group("sp", [(WR[bb * 64:(bb + 1) * 64, d, i, bb * 64:(bb + 1) * 64], self.lru_w_r[l, d, 2 * i + bb]) for (d, i, bb) in idx], writes=[WR], sembuf=WR)
            S.dma_group("sp", [(WI[bb * 64:(bb + 1) * 64, d, i, bb * 64:(bb + 1) * 64], self.lru_w_i[l, d, 2 * i + bb]) for (d, i, bb) in idx], writes=[WI], sembuf=WI)
            S.op("act", lambda e: e.activation(out=c8[:], in_=self.LAM[:, l], func=AF.Exp, scale=-1.0), reads=[self.LAM], writes=[c8])
            S.op("act", lambda e: e.activation(out=c8[:], in_=c8[:], func=AF.Ln, scale=1.0, bias=self.one_t[:]), reads=[c8, self.one_t], writes=[c8])
            S.op("dve", lambda e: e.tensor_scalar(out=c16[:], in0=c8[:], scalar1=-16.0, scalar2=0.0, op0=ALU.mult, op1=ALU.add), reads=[c8], writes=[c16])
            S.op("dve", lambda e: e.tensor_scalar(out=c8[:], in0=c8[:], scalar1=-8.0, scalar2=0.0, op0=ALU.mult, op1=ALU.add), reads=[c8], writes=[c8])
            segs = [(0, SEQ), (SEQ, NCTX)]
            for i in range(2):
                S.dma("sp", XH[:, :], self.lxs[i, :, :], writes=[XH], sembuf=XH)
                S.dma("sp", LG[:, :], self.lgs[i, :, :], writes=[LG], sembuf=LG)
                for (s0, sn) in segs:
                    S.op("dve", lambda e: e.tensor_scalar(out=U[:, s0:s0 + sn], in0=XH[:, s0:s0 + sn], scalar1=self.CW[:, l, 2, i:i + 1], scalar2=self.CB[:, l, i:i + 1], op0=ALU.mult, op1=ALU.add), reads=[XH, self.CW, self.CB], writes=[U])
                    for (kk, sh) in ((0, -2), (1, -1), (3, 1)):
                        if sh < 0:
                            oa, ob, ia, ib = s0 - sh, s0 + sn, s0, s0 + sn + sh
                        else:
                            oa, ob, ia, ib = s0, s0 + sn - sh, s0 + sh, s0 + sn
                        S.op("dve", lambda e: e.scalar_tensor_tensor(out=U[:, oa:ob], in0=XH[:, ia:ib], scalar=self.CW[:, l, kk, i:i + 1], in1=U[:, oa:ob], op0=ALU.mult, op1=ALU.add), reads=[XH, U, self.CW], writes=[U])
                for d in range(2):
                    nblk = (TK + 511) // 512
                    for bk in range(nblk):
                        b0 = bk * 512
                        bn = min(512, TK - b0)
                        pr = self.PB[(bk % 2) * 2]
                        pi = self.PB[(bk % 2) * 2 + 1]
                        S.mm([lambda e: e.matmul(pr[:, :bn], WR[:, d, i, :], U[:, b0:b0 + bn], start=True, stop=True)], reads=[WR, U], writes=[pr])
                        S.mm([lambda e: e.matmul(pi[:, :bn], WI[:, d, i, :], U[:, b0:b0 + bn], start=True, stop=True)], reads=[WI, U], writes=[pi])
                        S.op("act", lambda e: e.activation(out=A[:, b0:b0 + bn], in_=pr[:, :bn], func=AF.Sigmoid, bias=self.LBR[:, l, d, i:i + 1], scale=1.0), reads=[pr, self.LBR], writes=[A])
                        S.op("act", lambda e: e.activation(out=Bv[:, b0:b0 + bn], in_=pi[:, :bn], func=AF.Sigmoid, bias=self.LBI[:, l, d, i:i + 1], scale=1.0), reads=[pi, self.LBI], writes=[Bv])
                    S.op("act", lambda e: e.activation(out=M[:], in_=A[:], func=AF.Exp, scale=c16[:, d, i:i + 1]), reads=[A, c16], writes=[M])
                    S.op("act", lambda e: e.activation(out=A[:], in_=A[:], func=AF.Exp, scale=c8[:, d, i:i + 1]), reads=[A, c8], writes=[A])
                    S.op("act", lambda e: e.activation(out=M[:], in_=M[:], func=AF.Sqrt, scale=-1.0, bias=self.one_t[:]), reads=[M, self.one_t], writes=[M])
                    S.op("pool", lambda e: e.tensor_tensor(out=Bv[:], in0=Bv[:], in1=U[:], op=ALU.mult), reads=[Bv, U], writes=[Bv])
                    S.op("dve", lambda e: e.tensor_tensor(out=Bv[:], in0=Bv[:], in1=M[:], op=ALU.mult), reads=[Bv, M], writes=[Bv])
                    if d == 0:
                        S.op("dve", lambda e: e.tensor_tensor_scan(out=XH[:, SEQ:TK], data0=A[:, SEQ:TK], data1=Bv[:, SEQ:TK], initial=0.0, op0=ALU.mult, op1=ALU.add), reads=[A, Bv], writes=[XH])
                        S.op("dve", lambda e: e.tensor_tensor_scan(out=XH[:, 0:SEQ], data0=A[:, 0:SEQ], data1=Bv[:, 0:SEQ], initial=XH[:, TK - 1:TK], op0=ALU.mult, op1=ALU.add), reads=[A, Bv, XH], writes=[XH])
                        S.op("pool", lambda e: e.tensor_copy(out=HS[:], in_=XH[:]), reads=[XH], writes=[HS])
                    else:
                        S.op("dve", lambda e: e.tensor_tensor_scan(out=XH[:, SEQ:TK][:, ::-1], data0=A[:, SEQ:TK][:, ::-1], data1=Bv[:, SEQ:TK][:, ::-1], initial=0.0, op0=ALU.mult, op1=ALU.add), reads=[A, Bv], writes=[XH])
                        S.op("dve", lambda e: e.tensor_tensor_scan(out=XH[:, 0:SEQ][:, ::-1], data0=A[:, 0:SEQ][:, ::-1], data1=Bv[:, 0:SEQ][:, ::-1], initial=XH[:, SEQ:SEQ + 1], op0=ALU.mult, op1=ALU.add), reads=[A, Bv, XH], writes=[XH])
                        S.op("pool", lambda e: e.tensor_tensor(out=HS[:], in0=HS[:], in1=XH[:], op=ALU.add), reads=[HS, XH], writes=[HS])
                nt = T if with_ctx else NQ
                S.op("pool", lambda e: e.tensor_tensor(out=M[:, :nt], in0=LG[:, :nt], in1=LG[:, :nt], op=ALU.mult), reads=[LG], writes=[M])
                S.op("dve", lambda e: e.tensor_scalar(out=M[:, :nt], in0=M[:, :nt], scalar1=0.044715, scalar2=1.0, op0=ALU.mult, op1=ALU.add), reads=[M], writes=[M])
                S.op("pool", lambda e: e.tensor_tensor(out=M[:, :nt], in0=M[:, :nt], in1=LG[:, :nt], op=ALU.mult), reads=[M, LG], writes=[M])
                S.op("act", lambda e: e.activation(out=M[:, :nt], in_=M[:, :nt], func=AF.Sigmoid, scale=GELU_K), reads=[M], writes=[M])
                S.op("dve", lambda e: e.tensor_tensor(out=M[:, :nt], in0=M[:, :nt], in1=LG[:, :nt], op=ALU.mult), reads=[M, LG], writes=[M])
                S.op("dve", lambda e: e.tensor_tensor(out=ym[:, i, 0:NQ], in0=M[:, 0:NQ], in1=HS[:, 0:NQ], op=ALU.mult), reads=[M, HS], writes=[ym], skip_self=True)
                if with_ctx:
                    S.op("dve", lambda e: e.tensor_tensor(out=ym[:, i, NQ:T], in0=M[:, NQ:T], in1=HS[:, SEQ:TK], op=ALU.mult), reads=[M, HS], writes=[ym], skip_self=True)
            S.barrier()

    def swa(self, l, ym, with_ctx):
        S = self.S
        NQ, T, TK = self.NQ, self.T, self.TK
        nkt = TK // 128
        with ExitStack() as esl:
            SK = S.sb("swa_k", [128, TK], BF16, esl)
            SQ = S.sb("swa_q", [128, T], BF16, esl)
            VS = S.sb("swa_v", [128, nkt, 128], BF16, esl)
            VP = [S.sb(f"swa_vp{i}", [128, nkt, 128], BF16, esl) for i in range(2)]
            PT = [S.sb(f"swa_pt{i}", [128, 256], BF16, esl) for i in range(3)]
            es_ = S.sb("swa_es", [128, 4], F32, esl)
            zz = [S.sb(f"swa_zz{i}", [128, 128], F32, esl) for i in range(2)]
            S.op("act", lambda e: e.activation(out=es_[:], in_=self.SINK[:, l * 4:(l + 1) * 4], func=AF.Exp), reads=[self.SINK], writes=[es_])
            S.dma("sp", VS[:], self.svs.rearrange("(k p) e -> p k e", p=128), writes=[VS], sembuf=VS)
            cnt = 0
            qn = 0
            for g in range(2):
                esc = S.sb(f"swa_esc{g}", [128, 1], F32, esl)
                S.op("dve", lambda e: e.tensor_copy(out=esc[0:64, :], in_=es_[0:64, 2 * g:2 * g + 1]), reads=[es_], writes=[esc])
                S.op("dve", lambda e: e.tensor_copy(out=esc[64:128, :], in_=es_[64:128, 2 * g + 1:2 * g + 2]), reads=[es_], writes=[esc])
                S.dma("sp", SK[:], self.sks[g], writes=[SK], sembuf=SK)
                S.dma("sp", SQ[:], self.sqs[g], writes=[SQ], sembuf=SQ)
                S.op("pool", lambda e: e.memset(VP[0][:], 0.0), writes=[VP[0]])
                S.op("pool", lambda e: e.memset(VP[1][:], 0.0), writes=[VP[1]])
                S.op("pool", lambda e: e.tensor_copy(out=VP[0][:, :, 0:64], in_=VS[:, :, 64 * g:64 * g + 64]), reads=[VS], writes=[VP[0]])
                S.op("pool", lambda e: e.tensor_copy(out=VP[1][:, :, 64:128], in_=VS[:, :, 64 * g:64 * g + 64]), reads=[VS], writes=[VP[1]])
                nbl = NQ // 128
                qblocks = [(n, n * 128) for n in range(nbl)]
                if with_ctx:
                    qblocks += [(-1, NQ), (-1, NQ + 128)]
                for (n, qc) in qblocks:
                    if n >= 0:
                        kts = ([(n - 1, 0)] if n > 0 else []) + [(n, None)] + ([(n + 1, 1)] if n < SEQ // 128 - 1 else []) + [(SEQ // 128, None), (SEQ // 128 + 1, None)]
                    else:
                        kts = [(SEQ // 128, None), (SEQ // 128 + 1, None)]
                    po = self.PB[4 + (qn % 2) * 2]
                    pz = self.PB[5 + (qn % 2) * 2]
                    z = zz[qn % 2]
                    qn += 1
                    for ki, (kt, m) in enumerate(kts):
                        pscs = (self.PB[(cnt % 2) * 2], self.PB[(cnt % 2) * 2 + 1])
                        pt = PT[cnt % 3]
                        cnt += 1
                        for hh in range(2):
                            S.mm([lambda e: e.matmul(pscs[hh][:, 0:128], SK[hh * 64:(hh + 1) * 64, kt * 128:(kt + 1) * 128], SQ[hh * 64:(hh + 1) * 64, qc:qc + 128], start=True, stop=True)], reads=[SK, SQ], writes=[pscs[hh]])
                        S.op("act", lambda e: e.activation(out=pt[:, 0:128], in_=pscs[0][:, 0:128], func=AF.Exp, scale=0.125), reads=[pscs[0]], writes=[pt])
                        S.op("act", lambda e: e.activation(out=pt[:, 128:256], in_=pscs[1][:, 0:128], func=AF.Exp, scale=0.125), reads=[pscs[1]], writes=[pt], skip_self=True)
                        if m is not None:
                            S.op("pool", lambda e: e.tensor_tensor(out=pt[:].rearrange("p (h q) -> p h q", h=2), in0=pt[:].rearrange("p (h q) -> p h q", h=2), in1=self.masks[:, m, :].unsqueeze(1).to_broadcast([128, 2, 128]), op=ALU.mult), reads=[pt, self.masks], writes=[pt])
                        first = ki == 0
                        lastk = ki == len(kts) - 1
                        S.mm([lambda e: e.matmul(po[:, 0:128], VP[0][:, kt, :], pt[:, 0:128], start=first, stop=False),
                              lambda e: e.matmul(po[:, 0:128], VP[1][:, kt, :], pt[:, 128:256], start=False, stop=lastk)], reads=[VP[0], VP[1], pt], writes=[po])
                        S.mm([lambda e: e.matmul(pz[:, 0:128], self.onesL[:], pt[:, 0:128], start=first, stop=False),
                              lambda e: e.matmul(pz[:, 0:128], self.onesR[:], pt[:, 128:256], start=False, stop=lastk)], reads=[self.onesL, self.onesR, pt], writes=[pz])
                    S.op("dve", lambda e: e.tensor_scalar(out=z[:], in0=pz[:, 0:128], scalar1=esc[:, 0:1], scalar2=0.0, op0=ALU.add, op1=ALU.add), reads=[pz, esc], writes=[z])
                    S.op("dve", lambda e: e.reciprocal(out=z[:], in_=z[:]), reads=[z], writes=[z])
                    S.op("dve", lambda e: e.tensor_tensor(out=ym[:, 2 + g, qc:qc + 128], in0=po[:, 0:128], in1=z[:], op=ALU.mult), reads=[po, z], writes=[ym], skip_self=True)
            S.barrier()

    def diff(self, l, ym, with_ctx):
        S = self.S
        NQ, T, TK = self.NQ, self.T, self.TK
        nkt = TK // 128
        lam_init = 0.8 - 0.6 * math.exp(-0.3 * l)
        with ExitStack() as esl:
            DK = [S.sb(f"df_k{i}", [128, TK], BF16, esl) for i in range(2)]
            DQ = [S.sb(f"df_q{i}", [128, T], BF16, esl) for i in range(2)]
            DV = [S.sb(f"df_v{i}", [128, nkt, 128], BF16, esl) for i in range(2)]
            PT = [S.sb(f"df_pt{i}", [128, 1024], BF16, esl) for i in range(3)]
            r1 = S.sb("df_r1", [128, 512], F32, esl)
            r2 = S.sb("df_r2", [128, 512], F32, esl)
            o1 = S.sb("df_o1", [128, 512], F32, esl)
            o2 = S.sb("df_o2", [128, 512], F32, esl)
            osq = S.sb("df_osq", [128, 512], F32, esl)
            lt = S.sb("df_lt", [128, 2, 64], F32, esl)
            ls = S.sb("df_ls", [128, 2], F32, esl)
            nlam = S.sb("df_nlam", [128, 1], F32, esl)
            gsl = S.sb("df_gsl", [128, 1], F32, esl)
            dl = self.DLAM[:, l * 256:(l + 1) * 256].rearrange("p (a d) -> p a d", a=4)
            S.op("dve", lambda e: e.tensor_tensor(out=lt[:, 0, :], in0=dl[:, 0, :], in1=dl[:, 1, :], op=ALU.mult), reads=[self.DLAM], writes=[lt])
            S.op("dve", lambda e: e.tensor_tensor(out=lt[:, 1, :], in0=dl[:, 2, :], in1=dl[:, 3, :], op=ALU.mult), reads=[self.DLAM], writes=[lt], skip_self=True)
            S.op("dve", lambda e: e.tensor_reduce(out=ls[:], in_=lt[:], axis=mybir.AxisListType.X, op=ALU.add), reads=[lt], writes=[ls])
            S.op("act", lambda e: e.activation(out=ls[:], in_=ls[:], func=AF.Exp), reads=[ls], writes=[ls])
            S.op("dve", lambda e: e.tensor_tensor(out=nlam[:], in0=ls[:, 1:2], in1=ls[:, 0:1], op=ALU.subtract), reads=[ls], writes=[nlam])
            S.op("dve", lambda e: e.tensor_scalar(out=nlam[:], in0=nlam[:], scalar1=-lam_init, scalar2=0.0, op0=ALU.add, op1=ALU.add), reads=[nlam], writes=[nlam])
            S.op("dve", lambda e: e.tensor_scalar(out=gsl[:], in0=self.SUBG[:, l:l + 1], scalar1=1.0 - lam_init, scalar2=0.0, op0=ALU.mult, op1=ALU.add), reads=[self.SUBG], writes=[gsl])
            cnt = 0
            for h in range(4):
                dk, dq, dv = DK[h % 2], DQ[h % 2], DV[h % 2]
                S.dma("sp", dk[:], self.dks[h], writes=[dk], sembuf=dk)
                S.dma("sp", dq[:], self.dqs[h], writes=[dq], sembuf=dq)
                S.dma("sp", dv[:], self.dvs[:, 128 * h:128 * (h + 1)].rearrange("(k p) e -> p k e", p=128), writes=[dv], sembuf=dv)
                qtiles = [(i * 512, 512, list(range(nkt))) for i in range(NQ // 512)]
                if with_ctx:
                    qtiles.append((NQ, NCTX, [SEQ // 128, SEQ // 128 + 1]))
                O1, O2, Z1, Z2 = self.PB[4], self.PB[5], self.PB[6], self.PB[7]
                for (qc, n, kts) in qtiles:
                    for ki, kt in enumerate(kts):
                        pa = self.PB[(cnt % 2) * 2]
                        pb = self.PB[(cnt % 2) * 2 + 1]
                        pt = PT[cnt % 3]
                        cnt += 1
                        S.mm([lambda e: e.matmul(pa[:, :n], dk[0:64, kt * 128:(kt + 1) * 128], dq[0:64, qc:qc + n], start=True, stop=True)], reads=[dk, dq], writes=[pa])
                        S.mm([lambda e: e.matmul(pb[:, :n], dk[64:128, kt * 128:(kt + 1) * 128], dq[64:128, qc:qc + n], start=True, stop=True)], reads=[dk, dq], writes=[pb])
                        S.op("act", lambda e: e.activation(out=pt[:, 0:n], in_=pa[:, :n], func=AF.Exp, scale=0.125), reads=[pa], writes=[pt])
                        S.op("act", lambda e: e.activation(out=pt[:, 512:512 + n], in_=pb[:, :n], func=AF.Exp, scale=0.125), reads=[pb], writes=[pt], skip_self=True)
                        first = ki == 0
                        lastk = ki == len(kts) - 1
                        S.mm([lambda e: e.matmul(O1[:, :n], dv[:, kt, :], pt[:, 0:n], start=first, stop=lastk)], reads=[dv, pt], writes=[O1])
                        S.mm([lambda e: e.matmul(O2[:, :n], dv[:, kt, :], pt[:, 512:512 + n], start=first, stop=lastk)], reads=[dv, pt], writes=[O2])
                        S.mm([lambda e: e.matmul(Z1[:, :n], self.ones_b[:], pt[:, 0:n], start=first, stop=lastk)], reads=[self.ones_b, pt], writes=[Z1])
                        S.mm([lambda e: e.matmul(Z2[:, :n], self.ones_b[:], pt[:, 512:512 + n], start=first, stop=lastk)], reads=[self.ones_b, pt], writes=[Z2])
                    S.op("dve", lambda e: e.reciprocal(out=r1[:, :n], in_=Z1[:, :n]), reads=[Z1], writes=[r1])
                    S.op("dve", lambda e: e.reciprocal(out=r2[:, :n], in_=Z2[:, :n]), reads=[Z2], writes=[r2])
                    S.op("dve", lambda e: e.tensor_tensor(out=o1[:, :n], in0=O1[:, :n], in1=r1[:, :n], op=ALU.mult), reads=[O1, r1], writes=[o1])
                    S.op("dve", lambda e: e.tensor_tensor(out=o2[:, :n], in0=O2[:, :n], in1=r2[:, :n], op=ALU.mult), reads=[O2, r2], writes=[o2])
                    S.op("dve", lambda e: e.scalar_tensor_tensor(out=o1[:, :n], in0=o2[:, :n], scalar=nlam[:, 0:1], in1=o1[:, :n], op0=ALU.mult, op1=ALU.add), reads=[o1, o2, nlam], writes=[o1])
                    S.op("act", lambda e: e.activation(out=osq[:, :n], in_=o1[:, :n], func=AF.Square), reads=[o1], writes=[osq])
                    S.mm([lambda e: e.matmul(Z1[:, :n], self.ones_f[:], osq[:, :n], start=True, stop=True)], reads=[self.ones_f, osq], writes=[Z1])
                    S.op("act", lambda e: e.activation(out=r1[:, :n], in_=Z1[:, :n], func=AF.Sqrt, scale=1.0 / 128, bias=self.eps_t[:]), reads=[Z1, self.eps_t], writes=[r1])
                    S.op("dve", lambda e: e.reciprocal(out=r1[:, :n], in_=r1[:, :n]), reads=[r1], writes=[r1])
                    S.op("dve", lambda e: e.scalar_tensor_tensor(out=ym[:, 4 + h, qc:qc + n], in0=o1[:, :n], scalar=gsl[:, 0:1], in1=r1[:, :n], op0=ALU.mult, op1=ALU.mult), reads=[o1, r1, gsl], writes=[ym], skip_self=True)
            S.barrier()

    def final(self):
        S = self.S
        with ExitStack() as esl:
            hbs = [S.sb(f"hbf{i}", [128, NCH, 512], F32, esl) for i in range(2)]
            junk = S.sb("fjunk", [128, D], BF16, esl)
            ssq = [S.sb(f"fssq{i}", [128, 1], F32, esl) for i in range(2)]
            ot = [S.sb(f"fot{i}", [128, D], F32, esl) for i in range(2)]
            ss2 = [S.sb(f"fss2_{i}", [128, 2], F32, esl) for i in range(2)]
            cnt = 0
            for ti, (off, n, j) in enumerate(self.tiles):
                if j == 1:
                    continue
                b = hbs[ti % 2]
                S.dma("sp", b[:, :, :n], self.hsv(self.hs, off, n), writes=[b], sembuf=b)
                for sub in range(n // 128):
                    pa = self.PB[(cnt % 2) * 2]
                    pb = self.PB[(cnt % 2) * 2 + 1]
                    s_ = ssq[cnt % 2]
                    o_ = ot[cnt % 2]
                    cnt += 1
                    for half, pp in enumerate((pa, pb)):
                        ppv = pp[:, :].rearrange("p (c t) -> p c t", c=4)
                        S.mm([lambda e, cc=cc: e.transpose(ppv[:, cc, :], b[:, half * 4 + cc, sub * 128:(sub + 1) * 128], self.ident[:]) for cc in range(4)], reads=[b, self.ident], writes=[pp])
                    s2 = ss2[cnt % 2]
                    S.op("act", lambda e: e.activation(out=junk[:, 0:512], in_=pa[:, :], func=AF.Square, accum_out=s2[:, 0:1]), reads=[pa], writes=[junk, s2])
                    S.op("act", lambda e: e.activation(out=junk[:, 512:1024], in_=pb[:, :], func=AF.Square, accum_out=s2[:, 1:2]), reads=[pb], writes=[junk, s2])
                    S.op("dve", lambda e: e.tensor_tensor(out=s_[:], in0=s2[:, 0:1], in1=s2[:, 1:2], op=ALU.add), reads=[s2], writes=[s_])
                    S.op("act", lambda e: e.activation(out=s_[:], in_=s_[:], func=AF.Sqrt, scale=1.0 / D, bias=self.eps_t[:]), reads=[s_, self.eps_t], writes=[s_])
                    S.op("dve", lambda e: e.reciprocal(out=s_[:], in_=s_[:]), reads=[s_], writes=[s_])
                    S.op("dve", lambda e: e.scalar_tensor_tensor(out=o_[:, 0:512], in0=pa[:, :], scalar=s_[:, 0:1], in1=self.FG[:, 0:512], op0=ALU.mult, op1=ALU.mult), reads=[pa, s_, self.FG], writes=[o_])
                    S.op("dve", lambda e: e.scalar_tensor_tensor(out=o_[:, 512:1024], in0=pb[:, :], scalar=s_[:, 0:1], in1=self.FG[:, 512:1024], op0=ALU.mult, op1=ALU.mult), reads=[pb, s_, self.FG], writes=[o_], skip_self=True)
                    r0 = off + sub * 128
                    S.dma("sp", self.out[r0:r0 + 128, :], o_[:], reads=[o_], sembuf=o_)


def rope_tables(n0, n):
    t = np.arange(n0, n0 + n)
    row = (t // GRID_W).astype(np.float32)
    col = (t % GRID_W).astype(np.float32)
    nf = 16
    inv = (10000.0 ** (-np.arange(nf, dtype=np.float32) / nf)).astype(np.float32)
    ang = np.concatenate([row[:, None] * inv, col[:, None] * inv], axis=-1).astype(np.float32)
    idx = np.arange(128) % 32
    c = np.cos(ang).astype(np.float32)[:, idx].T
    s = np.sin(ang).astype(np.float32)[:, idx].T
    return np.ascontiguousarray(c), np.ascontiguousarray(s)


_NC_CACHE = {}


def kernel(**inputs):
    NQ = SEQ
    key = ("main", NQ)
    if key not in _NC_CACHE:
        _NC_CACHE[key] = Kern(NQ).build()
    nc = _NC_CACHE[key]
    f = lambda a: np.ascontiguousarray(np.asarray(a, dtype=np.float32))
    rc, rs = rope_tables(0, NQ)
    shared = {k: f(inputs[k]) for k in ("w_ada", "b_ada", "norm_g", "ffn1_w_gu", "ffn1_w_down", "ffn2_w_gu", "ffn2_w_down",
                                        "w_in", "w_out", "conv_w", "conv_b", "lru_w_r", "lru_b_r", "lru_w_i", "lru_b_i",
                                        "lru_lambda", "swa_sink", "diff_lambda", "diff_subln_g", "final_g")}
    x = f(inputs["x"])
    ctx = f(inputs["ctx"])
    c = f(inputs["c"])
    cc = f(inputs["c_ctx"])
    in_maps = []
    for core in range(8):
        b = core % 4
        m = dict(shared)
        m["x"] = x[b]
        m["ctx"] = ctx[b]
        m["cvec"] = np.ascontiguousarray(np.stack([c[b], cc], axis=0))
        m["ropec"] = rc
        m["ropes"] = rs
        in_maps.append(m)
    res = run_bass_kernel_spmd(nc, in_maps, core_ids=list(range(8)))
    out = np.stack([np.asarray(res.results[b]["out"], dtype=np.float32) for b in range(4)], axis=0)
    return out
```

```python
import math
import numpy as np
import concourse.bass as bass
import concourse.mybir as mybir
from concourse.bass_utils import run_bass_kernel_spmd
from contextlib import ExitStack

F32 = mybir.dt.float32
BF16 = mybir.dt.bfloat16
AF = mybir.ActivationFunctionType
ALU = mybir.AluOpType

D = 1024
NCH = 8
DFF = 2816
NJ = 22
SEQ = 4096
NCTX = 256
DEPTH = 2
EPS = 1e-6
NMOD = 9
GRID_W = 64
O_LX, O_LG, O_SQ, O_SK, O_SV, O_DQ, O_DK, O_DV = 0, 256, 512, 768, 896, 1024, 1536, 2048
INW = 2560
GELU_K = 2.0 * math.sqrt(2.0 / math.pi)


class Buf:
    __slots__ = ("name", "w", "r", "t")

    def __init__(self, name, t=None):
        self.name = name
        self.w = None
        self.r = {}
        self.t = t

    def __getitem__(self, k):
        return self.t[k]


class Sched:
    ENG = ("pe", "act", "dve", "pool", "sp")

    def __init__(self, nc, es):
        self.nc = nc
        self.es = es
        self.E = {"pe": nc.tensor, "act": nc.scalar, "dve": nc.vector, "pool": nc.gpsimd, "sp": nc.sync}
        self.sems = {}
        self.cnt = {}
        self.seen = {e: {} for e in self.ENG}
        for e in self.ENG:
            self.sems[e] = es.enter_context(nc.semaphore("sem_" + e))
            self.cnt[e] = 0
        self.dsem = {}
        self.nwait = 0
        self.nins = 0

    def sb(self, name, shape, dt, es=None):
        self.uid = getattr(self, "uid", 0) + 1
        t = (es or self.es).enter_context(self.nc.sbuf_tensor(f"{name}_{self.uid}", list(shape), dt))
        return Buf(name, t)

    def ps(self, name, shape, dt=F32, es=None):
        t = (es or self.es).enter_context(self.nc.psum_tensor(name, list(shape), dt))
        return Buf(name, t)

    def _wait(self, eng, ev):
        if ev is None:
            return
        key, val = ev
        if self.seen[eng].get(key, 0) >= val:
            return
        self.E[eng].wait_ge(self.sems[key], val)
        self.seen[eng][key] = val
        self.nwait += 1

    def _deps(self, eng, reads, writes, skip_self=False):
        for b in reads:
            if b.w is not None:
                self._wait(eng, b.w)
        for b in writes:
            if b.w is not None and not (skip_self and b.w[0] == eng):
                self._wait(eng, b.w)
            for k, v in b.r.items():
                if not (skip_self and k == eng):
                    self._wait(eng, (k, v))

    def _mark(self, ev, reads, writes):
        k, v = ev
        for b in reads:
            if b.r.get(k, 0) < v:
                b.r[k] = v
        for b in writes:
            b.w = ev
            b.r = {}

    def op(self, eng, fn, reads=(), writes=(), skip_self=False):
        self._deps(eng, reads, writes, skip_self)
        ins = fn(self.E[eng])
        self.cnt[eng] += 1
        ins.then_inc(self.sems[eng], 1)
        self._mark((eng, self.cnt[eng]), reads, writes)
        self.nins += 1
        return ins

    def mm(self, fns, reads=(), writes=()):
        eng = "pe"
        self._deps(eng, reads, writes, skip_self=True)
        ins = None
        for fn in fns:
            ins = fn(self.E[eng])
            self.nins += 1
        self.cnt[eng] += 1
        ins.then_inc(self.sems[eng], 1)
        self._mark((eng, self.cnt[eng]), reads, writes)
        return ins

    def dma(self, q, out, in_, reads=(), writes=(), sembuf=None, **kw):
        self._deps(q, reads, writes)
        name = sembuf.name
        if name not in self.dsem:
            sem = self.es.enter_context(self.nc.semaphore("ds_" + name))
            self.dsem[name] = [sem, 0]
            self.sems[("d", name)] = sem
        ent = self.dsem[name]
        ins = self.E[q].dma_start(out=out, in_=in_, **kw)
        ins.then_inc(ent[0], 16)
        ent[1] += 16
        self._mark((("d", name), ent[1]), reads, writes)
        self.nins += 1
        return ins

    def dma_group(self, q, pairs, reads=(), writes=(), sembuf=None, **kw):
        self._deps(q, reads, writes)
        name = sembuf.name
        if name not in self.dsem:
            sem = self.es.enter_context(self.nc.semaphore("ds_" + name))
            self.dsem[name] = [sem, 0]
            self.sems[("d", name)] = sem
        ent = self.dsem[name]
        for (o, i) in pairs:
            self.E[q].dma_start(out=o, in_=i, **kw).then_inc(ent[0], 16)
            ent[1] += 16
            self.nins += 1
        self._mark((("d", name), ent[1]), reads, writes)

    def barrier(self):
        for e in self.ENG:
            for e2 in self.ENG:
                if e2 != e and self.cnt[e2] > 0:
                    self._wait(e, (e2, self.cnt[e2]))
            for name, ent in self.dsem.items():
                if ent[1] > 0:
                    self._wait(e, (("d", name), ent[1]))


class Kern:
    def __init__(self, NQ, debug=None):
        self.NQ = NQ
        self.T = NQ + NCTX
        self.TK = SEQ + NCTX
        self.debug = debug or set()
        self.tiles = [(i * 512, 512, 0) for i in range(NQ // 512)] + [(NQ, NCTX, 1)]
        self.supers = []
        cur = []
        tot = 0
        for t in self.tiles:
            if tot + t[1] > 2304:
                self.supers.append(cur)
                cur, tot = [], 0
            cur.append(t)
            tot += t[1]
        self.supers.append(cur)
        self.STMAX = max(sum(t[1] for t in s) for s in self.supers)

    def build(self):
        nc = bass.Bass("TRN2", target_bir_lowering=False)
        self.nc = nc
        NQ, T, TK = self.NQ, self.T, self.TK

        def inp(name, shape, dt=F32):
            return nc.dram_tensor(name, list(shape), dt, kind="ExternalInput").ap()

        def scratch(name, shape, dt):
            return nc.dram_tensor(name, list(shape), dt, kind="Internal").ap()

        self.x = inp("x", [NQ, D])
        self.ctx = inp("ctx", [NCTX, D])
        self.cvec = inp("cvec", [2, D])
        self.ropec = inp("ropec", [128, NQ])
        self.ropes = inp("ropes", [128, NQ])
        self.w_ada = inp("w_ada", [DEPTH, D, NMOD * D])
        self.b_ada = inp("b_ada", [DEPTH, NMOD * D])
        self.norm_g = inp("norm_g", [DEPTH, 3, D])
        self.ffn_w_gu = [inp("ffn1_w_gu", [DEPTH, D, 2 * DFF]), inp("ffn2_w_gu", [DEPTH, D, 2 * DFF])]
        self.ffn_w_down = [inp("ffn1_w_down", [DEPTH, DFF, D]), inp("ffn2_w_down", [DEPTH, DFF, D])]
        self.w_in = inp("w_in", [DEPTH, D, INW])
        self.w_out = inp("w_out", [DEPTH, D, D])
        self.conv_w = inp("conv_w", [DEPTH, 4, 256])
        self.conv_b = inp("conv_b", [DEPTH, 256])
        self.lru_w_r = inp("lru_w_r", [DEPTH, 2, 4, 64, 64])
        self.lru_b_r = inp("lru_b_r", [DEPTH, 2, 256])
        self.lru_w_i = inp("lru_w_i", [DEPTH, 2, 4, 64, 64])
        self.lru_b_i = inp("lru_b_i", [DEPTH, 2, 256])
        self.lru_lambda = inp("lru_lambda", [DEPTH, 2, 256])
        self.swa_sink = inp("swa_sink", [DEPTH, 4])
        self.diff_lambda = inp("diff_lambda", [DEPTH, 4, 64])
        self.diff_subln_g = inp("diff_subln_g", [DEPTH, 128])
        self.final_g = inp("final_g", [D])
        self.stopping = any(k.startswith("stop_") for k in self.debug)
        self.out = nc.dram_tensor("out", [128 if self.stopping else NQ, D], F32, kind="ExternalOutput").ap()

        self.hs = scratch("hs", [NCH, 128, T], F32)
        self.lxs = scratch("lxs", [2, 128, TK], F32)
        self.lgs = scratch("lgs", [2, 128, T], F32)
        self.sqs = scratch("sqs", [2, 128, T], BF16)
        self.sks = scratch("sks", [2, 128, TK], BF16)
        self.dqs = scratch("dqs", [4, 128, T], BF16)
        self.dks = scratch("dks", [4, 128, TK], BF16)
        self.svs = scratch("svs", [TK, 128], BF16)
        self.dvs = scratch("dvs", [TK, 512], BF16)
        self.yms = scratch("yms", [NCH, 128, T], BF16)

        with ExitStack() as es:
            self.es = es
            S = Sched(nc, es)
            self.S = S
            self.PB = [S.ps(f"pb{i}", [128, 512], F32) for i in range(8)]
            self.consts()
            self.mods()
            self.phase0()
            if "modT" in self.debug:
                md = nc.dram_tensor("modT_o", [128, DEPTH * 72 * 2], F32, kind="ExternalOutput").ap()
                S.dma("sp", md[:, :], self.modT[:].rearrange("p l i j -> p (l i j)"), reads=[self.modT], sembuf=self.modT)
                S.barrier()
            for l in range(DEPTH):
                if "stop_p0" in self.debug:
                    break
                last = l == DEPTH - 1
                self.ffn(l, 0, self.tiles)
                if "h1" in self.debug and l == 0:
                    self.dump_ct("h1", self.hs, F32)
                if "stop_ffn1" in self.debug and l == 0:
                    break
                self.proj(l)
                if "proj" in self.debug and l == 0:
                    for nm, tn, dt in (("lxs", self.lxs, F32), ("lgs", self.lgs, F32), ("sqs", self.sqs, BF16), ("sks", self.sks, BF16), ("dqs", self.dqs, BF16), ("dks", self.dks, BF16)):
                        self.dump_ct(nm, tn, dt)
                    self.dump_tm("svs", self.svs, BF16)
                    self.dump_tm("dvs", self.dvs, BF16)
                if "stop_proj" in self.debug and l == 0:
                    break
                with ExitStack() as esm:
                    ym = S.sb("ymix", [128, NCH, T], BF16, esm)
                    if "no_lru" not in self.debug:
                        self.lru(l, ym, not last)
                    if "no_swa" not in self.debug:
                        self.swa(l, ym, not last)
                    if "no_diff" not in self.debug:
                        self.diff(l, ym, not last)
                    S.barrier()
                    if "yms" in self.debug and l == 0:
                        S.dma("sp", self.hsv(self.yms, 0, T), ym[:, :, :], reads=[ym], sembuf=ym)
                        S.barrier()
                        self.dump_ct("yms", self.yms, BF16)
                    tl = self.tiles if not last else [t for t in self.tiles if t[2] == 0]
                    self.down(l, ym, NCH, self.w_out[l], 5, 1.0, tl, [t[0] for t in tl])
                if "h2" in self.debug and l == 0:
                    self.dump_ct("h2", self.hs, F32)
                if "stop_mix" in self.debug and l == 0:
                    break
                tl = self.tiles if not last else [t for t in self.tiles if t[2] == 0]
                self.ffn(l, 2, tl)
                if "h3" in self.debug and l == 0:
                    self.dump_ct("h3", self.hs, F32)
                if "stop_l0" in self.debug and l == 0:
                    break
            if not self.stopping:
                self.final()
            S.barrier()
            print("instructions", S.nins, "waits", S.nwait, "sems", len(S.dsem) + 5)
        return nc

    def dump_ct(self, name, ten, dt):
        S, nc = self.S, self.nc
        C = ten.shape[0]
        o = nc.dram_tensor("d_" + name, [C, 128, 512], dt, kind="ExternalOutput").ap()
        b = Buf("dbgdump")
        S.dma_group("sp", [(o[:, :, d0:d0 + n], ten[:, :, a:a + n]) for (a, n, d0) in ((0, 128, 0), (self.NQ - 128, 128, 128), (self.NQ, 256, 256))], sembuf=b)
        S.barrier()

    def dump_tm(self, name, ten, dt):
        S, nc = self.S, self.nc
        E = ten.shape[1]
        o = nc.dram_tensor("d_" + name, [512, E], dt, kind="ExternalOutput").ap()
        b = Buf("dbgdump")
        S.dma_group("sp", [(o[d0:d0 + n, :], ten[a:a + n, :]) for (a, n, d0) in ((0, 128, 0), (SEQ - 128, 128, 128), (SEQ, 256, 256))], sembuf=b)
        S.barrier()

    def hsv(self, ten, off, n):
        return ten[:, :, off:off + n].rearrange("c p t -> p c t")

    def consts(self):
        S, nc, es = self.S, self.nc, self.es
        K = dict(allow_slow_non_contiguous=True)
        self.cbuf = Buf("cbuf")

        cbufs = []

        def cl(name, shape, src, dt=F32):
            b = S.sb(name, shape, dt)
            S.dma("sp", b[:], src, sembuf=self.cbuf, **K)
            cbufs.append(b)
            return b

        def cs(name, shape):
            return S.sb(name, shape, F32)

        def ld(b, dst, src):
            S.dma("sp", dst, src, sembuf=self.cbuf, **K)
            cbufs.append(b)

        self.sT = cs("sT", [128, NCH, 2])
        for j in range(2):
            ld(self.sT, self.sT[:, :, j], self.cvec[j].rearrange("(c p) -> p c", p=128))
        self.bT = cs("bT", [128, DEPTH, 72])
        self.NG = cs("NG", [128, DEPTH, 3, NCH])
        self.CW = cs("CW", [128, DEPTH, 4, 2])
        self.CB = cs("CB", [128, DEPTH, 2])
        self.LBR = cs("LBR", [128, DEPTH, 2, 2])
        self.LBI = cs("LBI", [128, DEPTH, 2, 2])
        self.LAM = cs("LAM", [128, DEPTH, 2, 2])
        self.SUBG = cs("SUBG", [128, DEPTH])
        for l in range(DEPTH):
            ld(self.bT, self.bT[:, l, :], self.b_ada[l].rearrange("(i p) -> p i", p=128))
            for k in range(3):
                ld(self.NG, self.NG[:, l, k, :], self.norm_g[l, k].rearrange("(c p) -> p c", p=128))
            for k in range(4):
                ld(self.CW, self.CW[:, l, k, :], self.conv_w[l, k].rearrange("(i p) -> p i", p=128))
            ld(self.CB, self.CB[:, l, :], self.conv_b[l].rearrange("(i p) -> p i", p=128))
            for d in range(2):
                ld(self.LBR, self.LBR[:, l, d, :], self.lru_b_r[l, d].rearrange("(i p) -> p i", p=128))
                ld(self.LBI, self.LBI[:, l, d, :], self.lru_b_i[l, d].rearrange("(i p) -> p i", p=128))
                ld(self.LAM, self.LAM[:, l, d, :], self.lru_lambda[l, d].rearrange("(i p) -> p i", p=128))
            ld(self.SUBG, self.SUBG[:, l:l + 1], self.diff_subln_g[l].rearrange("(p o) -> p o", o=1))
        self.SINK = cl("SINK", [128, DEPTH * 4], self.swa_sink.rearrange("l h -> (l h)").partition_broadcast(128))
        self.DLAM = cl("DLAM", [128, DEPTH * 4 * 64], self.diff_lambda.rearrange("l a d -> (l a d)").partition_broadcast(128))
        self.FG = cl("FG", [128, D], self.final_g.partition_broadcast(128))
        for b in cbufs:
            b.w = (("d", "cbuf"), S.dsem["cbuf"][1])

        self.ident = S.sb("ident", [128, 128], F32)
        S.op("pool", lambda e: e.memset(self.ident[:], 0.0), writes=[self.ident])
        S.op("pool", lambda e: e.affine_select(out=self.ident[:], in_=self.ident[:], pattern=[[-1, 128]], compare_op=ALU.not_equal, fill=1.0, base=0, channel_multiplier=1), reads=[self.ident], writes=[self.ident])
        self.ones_b = S.sb("ones_b", [128, 128], BF16)
        S.op("pool", lambda e: e.memset(self.ones_b[:], 1.0), writes=[self.ones_b])
        self.ones_f = S.sb("ones_f", [128, 128], F32)
        S.op("pool", lambda e: e.memset(self.ones_f[:], 1.0), writes=[self.ones_f])
        self.onesL = S.sb("onesL", [128, 128], BF16)
        self.onesR = S.sb("onesR", [128, 128], BF16)
        S.op("pool", lambda e: e.memset(self.onesL[:], 0.0), writes=[self.onesL])
        S.op("pool", lambda e: e.memset(self.onesL[:, 0:64], 1.0), writes=[self.onesL])
        S.op("pool", lambda e: e.memset(self.onesR[:], 0.0), writes=[self.onesR])
        S.op("pool", lambda e: e.memset(self.onesR[:, 64:128], 1.0), writes=[self.onesR])
        rt = S.sb("rm_tmp", [128, 2, 128], F32)
        S.op("pool", lambda e: e.memset(rt[:], 0.0), writes=[rt])
        S.op("pool", lambda e: e.affine_select(out=rt[:, 0, :], in_=rt[:, 0, :], pattern=[[-1, 128]], compare_op=ALU.not_equal, fill=1.0, base=32, channel_multiplier=1), reads=[rt], writes=[rt])
        S.op("pool", lambda e: e.affine_select(out=rt[:, 1, :], in_=rt[:, 1, :], pattern=[[-1, 128]], compare_op=ALU.not_equal, fill=-1.0, base=-32, channel_multiplier=1), reads=[rt], writes=[rt])
        for a in (0, 64):
            S.op("pool", lambda e: e.memset(rt[:, 0, a:a + 32], 0.0), reads=[rt], writes=[rt])
            S.op("pool", lambda e: e.memset(rt[:, 1, a + 32:a + 64], 0.0), reads=[rt], writes=[rt])
        self.Rm = S.sb("Rm", [128, 128], BF16)
        S.op("pool", lambda e: e.tensor_tensor(out=self.Rm[:], in0=rt[:, 0, :], in1=rt[:, 1, :], op=ALU.add), reads=[rt], writes=[self.Rm])
        mt = S.sb("mask_tmp", [128, 2, 128], F32)
        S.op("pool", lambda e: e.memset(mt[:], 1.0), writes=[mt])
        S.op("pool", lambda e: e.affine_select(out=mt[:, 0, :], in_=mt[:, 0, :], pattern=[[-1, 128]], compare_op=ALU.is_ge, fill=0.0, base=0, channel_multiplier=1), reads=[mt], writes=[mt])
        S.op("pool", lambda e: e.affine_select(out=mt[:, 1, :], in_=mt[:, 1, :], pattern=[[1, 128]], compare_op=ALU.is_ge, fill=0.0, base=0, channel_multiplier=-1), reads=[mt], writes=[mt])
        self.masks = S.sb("masks", [128, 2, 128], BF16)
        S.op("pool", lambda e: e.tensor_copy(out=self.masks[:], in_=mt[:]), reads=[mt], writes=[self.masks])
        self.eps_t = S.sb("eps_t", [128, 1], F32)
        S.op("pool", lambda e: e.memset(self.eps_t[:], EPS), writes=[self.eps_t])
        self.one_t = S.sb("one_t", [128, 1], F32)
        S.op("pool", lambda e: e.memset(self.one_t[:], 1.0), writes=[self.one_t])

    def mods(self):
        S = self.S
        S.op("act", lambda e: e.activation(out=self.sT[:], in_=self.sT[:], func=AF.Silu), reads=[self.sT], writes=[self.sT])
        self.modT = S.sb("modT", [128, DEPTH, 72, 2], F32)
        self.GS = S.sb("GS", [128, DEPTH, 3, NCH, 2], F32)
        with ExitStack() as esl:
            wa = [S.sb(f"wa{i}", [128, NCH, 512], F32, esl) for i in range(2)]
            for l in range(DEPTH):
                pm = self.PB[l]
                pmv = pm[:, 0:144].rearrange("p (i j) -> p i j", j=2)
                for cb in range(18):
                    w = wa[cb % 2]
                    S.dma("sp", w[:], self.w_ada[l][:, cb * 512:(cb + 1) * 512].rearrange("(c p) n -> p c n", p=128), writes=[w], sembuf=w)
                    fns = []
                    for sub in range(4):
                        idx = cb * 4 + sub
                        for c in range(NCH):
                            fns.append(lambda e, idx=idx, sub=sub, c=c: e.matmul(pmv[:, idx, :], w[:, c, sub * 128:(sub + 1) * 128], self.sT[:, c, :], start=(c == 0), stop=(c == NCH - 1)))
                    S.mm(fns, reads=[w, self.sT], writes=[pm])
                S.op("dve", lambda e: e.tensor_tensor(out=self.modT[:, l], in0=pmv, in1=self.bT[:, l, :].unsqueeze(2).to_broadcast([128, 72, 2]), op=ALU.add), reads=[pm, self.bT], writes=[self.modT])
                for k in range(3):
                    sc = self.modT[:, l, (3 * k + 1) * 8:(3 * k + 2) * 8, :]
                    S.op("dve", lambda e: e.scalar_tensor_tensor(out=self.GS[:, l, k], in0=sc, scalar=1.0, in1=self.NG[:, l, k, :].unsqueeze(2).to_broadcast([128, NCH, 2]), op0=ALU.add, op1=ALU.mult), reads=[self.modT, self.NG], writes=[self.GS])
            S.barrier()

    def mod_ap(self, l, m, c, j):
        return self.modT[:, l, m * 8 + c, j:j + 1]

    def phase0(self):
        S = self.S
        with ExitStack() as esl:
            xin = [S.sb(f"xin{i}", [128, D], F32, esl) for i in range(3)]
            hst = [S.sb(f"hst{i}", [128, NCH, 512], F32, esl) for i in range(2)]
            cnt = 0
            for ti, (off, n, j) in enumerate(self.tiles):
                hb = hst[ti % 2]
                for sub in range(n // 128):
                    xb = xin[cnt % 3]
                    src = self.x[off + sub * 128: off + (sub + 1) * 128, :] if j == 0 else self.ctx[sub * 128:(sub + 1) * 128, :]
                    S.dma("sp", xb[:], src, writes=[xb], sembuf=xb)
                    for half in range(2):
                        pt = self.PB[(cnt * 2 + half) % 8]
                        ptv = pt[:, :].rearrange("p (c t) -> p c t", c=4)
                        S.mm([lambda e, cc=cc: e.transpose(ptv[:, cc, :], xb[:, (half * 4 + cc) * 128:(half * 4 + cc + 1) * 128], self.ident[:]) for cc in range(4)], reads=[xb, self.ident], writes=[pt])
                        eng = "act" if half == 0 else "dve"
                        dst = hb[:, half * 4:(half + 1) * 4, sub * 128:(sub + 1) * 128]
                        if eng == "act":
                            S.op("act", lambda e: e.activation(out=dst, in_=ptv, func=AF.Copy), reads=[pt], writes=[hb])
                        else:
                            S.op("dve", lambda e: e.tensor_copy(out=dst, in_=ptv), reads=[pt], writes=[hb])
                    cnt += 1
                S.dma("sp", self.hsv(self.hs, off, n), hb[:, :, :n], reads=[hb], sembuf=hb)
            S.barrier()

    def norm_pass(self, l, k, tiles, offs, hn, hn_b):
        S = self.S
        with ExitStack() as esl:
            hbs = [S.sb(f"hb{i}", [128, NCH, 512], F32, esl) for i in range(2)]
            sq = S.sb("sq", [128, NCH, 512], BF16, esl)
            rs = [S.sb(f"rs{i}", [128, 512], F32, esl) for i in range(2)]
            for ti, (off, n, j) in enumerate(tiles):
                b = hbs[ti % 2]
                r = rs[ti % 2]
                S.dma("sp", b[:, :, :n], self.hsv(self.hs, off, n), writes=[b], sembuf=b)
                S.op("act", lambda e: e.activation(out=sq[:, :, :n], in_=b[:, :, :n], func=AF.Square), reads=[b], writes=[sq])
                ss = self.PB[4 + ti % 2]
                S.mm([lambda e, c=c: e.matmul(ss[:, :n], self.ones_b[:], sq[:, c, :n], start=(c == 0), stop=(c == NCH - 1)) for c in range(NCH)], reads=[sq, self.ones_b], writes=[ss])
                S.op("act", lambda e: e.activation(out=r[:, :n], in_=ss[:, :n], func=AF.Sqrt, scale=1.0 / D, bias=self.eps_t[:]), reads=[ss, self.eps_t], writes=[r])
                S.op("dve", lambda e: e.reciprocal(out=r[:, :n], in_=r[:, :n]), reads=[r], writes=[r])
                S.op("dve", lambda e: e.tensor_tensor(out=b[:, :, :n], in0=b[:, :, :n], in1=r[:, :n].unsqueeze(1).to_broadcast([128, NCH, n]), op=ALU.mult), reads=[b, r], writes=[b])
                o = offs[ti]
                for c in range(NCH):
                    S.op("act", lambda e, c=c: e.activation(out=hn[:, c, o:o + n], in_=b[:, c, :n], func=AF.Identity, scale=self.GS[:, l, k, c, j:j + 1], bias=self.mod_ap(l, 3 * k, c, j)), reads=[b, self.GS, self.modT], writes=[hn_b[ti]])
            S.barrier()

    def down(self, l, act, nk, wdram, gate_m, coef, tiles, offs, act_b=None):
        S = self.S
        with ExitStack() as esl:
            wd = [S.sb(f"wd{i}", [128, nk, 512], BF16, esl) for i in range(2)]
            hbs = [S.sb(f"hbd{i}", [128, NCH, 512], F32, esl) for i in range(2)]
            gt = S.sb("gt", [128, NCH, 2], F32, esl)
            S.op("pool", lambda e: e.tensor_scalar(out=gt[:], in0=self.modT[:, l, gate_m * 8:(gate_m + 1) * 8, :], scalar1=float(coef), scalar2=0.0, op0=ALU.mult, op1=ALU.add), reads=[self.modT], writes=[gt])
            for half in range(2):
                S.dma("pool", wd[half][:], wdram[:, half * 512:(half + 1) * 512].rearrange("(j p) n -> p j n", p=128), writes=[wd[half]], sembuf=wd[half])
            for ti, (off, n, j) in enumerate(tiles):
                b = hbs[ti % 2]
                o = offs[ti]
                rd = [act_b[ti]] if act_b is not None else [act]
                S.dma("sp", b[:, :, :n], self.hsv(self.hs, off, n), writes=[b], sembuf=b)
                for oc in range(NCH):
                    py = self.PB[4 + oc % 4]
                    w = wd[oc // 4]
                    cs = (oc % 4) * 128
                    S.mm([lambda e, jj=jj: e.matmul(py[:, :n], w[:, jj, cs:cs + 128], act[:, jj, o:o + n], start=(jj == 0), stop=(jj == nk - 1)) for jj in range(nk)], reads=[w] + rd, writes=[py])
                    S.op("dve", lambda e: e.scalar_tensor_tensor(out=b[:, oc, :n], in0=py[:, :n], scalar=gt[:, oc, j:j + 1], in1=b[:, oc, :n], op0=ALU.mult, op1=ALU.add), reads=[py, gt] + ([b] if oc == 0 else []), writes=[b], skip_self=True)
                S.dma("sp", self.hsv(self.hs, off, n), b[:, :, :n], reads=[b], sembuf=b)
            S.barrier()

    def ffn(self, l, k, tiles):
        S = self.S
        wgu = self.ffn_w_gu[0 if k == 0 else 1][l]
        wdn = self.ffn_w_down[0 if k == 0 else 1][l]
        supers = []
        cur, tot = [], 0
        for t in tiles:
            if tot + t[1] > 2304:
                supers.append(cur)
                cur, tot = [], 0
            cur.append(t)
            tot += t[1]
        supers.append(cur)
        for st in supers:
            offs = []
            o = 0
            for t in st:
                offs.append(o)
                o += t[1]
            with ExitStack() as esf:
                act = S.sb("act", [128, NJ, self.STMAX], BF16, esf)
                act_b = [Buf(f"act_b{i}") for i in range(len(st))]
                with ExitStack() as esa:
                    hn = S.sb("hn", [128, NCH, self.STMAX], BF16, esa)
                    hn_b = [Buf(f"hn_b{i}") for i in range(len(st))]
                    self.norm_pass(l, k, st, offs, hn, hn_b)
                    with ExitStack() as es2:
                        wg = [S.sb(f"wg{i}", [128, NCH, 256], BF16, es2) for i in range(2)]
                        wu = [S.sb(f"wu{i}", [128, NCH, 256], BF16, es2) for i in range(2)]
                        sg = [S.sb(f"sg{i}", [128, 512], BF16, es2) for i in range(2)]
                        cnt = 0
                        for jg in range(NJ // 2):
                            s = jg % 2
                            S.dma("pool", wg[s][:], wgu[:, jg * 256:(jg + 1) * 256].rearrange("(c p) n -> p c n", p=128), writes=[wg[s]], sembuf=wg[s])
                            S.dma("pool", wu[s][:], wgu[:, DFF + jg * 256:DFF + (jg + 1) * 256].rearrange("(c p) n -> p c n", p=128), writes=[wu[s]], sembuf=wu[s])
                            for jj in range(2):
                                jx = jg * 2 + jj
                                for ti, (off, n, j) in enumerate(st):
                                    o = offs[ti]
                                    pg = self.PB[(cnt % 2) * 2]
                                    pu = self.PB[(cnt % 2) * 2 + 1]
                                    sgb = sg[cnt % 2]
                                    S.mm([lambda e, c=c: e.matmul(pg[:, :n], wg[s][:, c, jj * 128:(jj + 1) * 128], hn[:, c, o:o + n], start=(c == 0), stop=(c == NCH - 1)) for c in range(NCH)], reads=[wg[s], hn_b[ti]], writes=[pg])
                                    S.mm([lambda e, c=c: e.matmul(pu[:, :n], wu[s][:, c, jj * 128:(jj + 1) * 128], hn[:, c, o:o + n], start=(c == 0), stop=(c == NCH - 1)) for c in range(NCH)], reads=[wu[s], hn_b[ti]], writes=[pu])
                                    S.op("act", lambda e: e.activation(out=sgb[:, :n], in_=pg[:, :n], func=AF.Silu), reads=[pg], writes=[sgb])
                                    S.op("dve", lambda e: e.tensor_tensor(out=act[:, jx, o:o + n], in0=sgb[:, :n], in1=pu[:, :n], op=ALU.mult), reads=[sgb, pu], writes=[act_b[ti]])
                                    cnt += 1
                    S.barrier()
                self.down(l, act, NJ, wdn, 3 * k + 2, 0.5, st, offs, act_b)

    def proj(self, l):
        S = self.S
        NQ = self.NQ
        for st in self.supers:
            offs = []
            o = 0
            for t in st:
                offs.append(o)
                o += t[1]
            with ExitStack() as esa:
                hn = S.sb("hn", [128, NCH, self.STMAX], BF16, esa)
                hn_b = [Buf(f"hn_b{i}") for i in range(len(st))]
                self.norm_pass(l, 1, st, offs, hn, hn_b)
                win = S.sb("win", [128, NCH, INW], BF16, esa)
                wkd = S.sb("wkd", [128, NCH, 256], BF16, esa)
                S.dma_group("pool", [(win[:, :, i * 512:(i + 1) * 512], self.w_in[l][:, i * 512:(i + 1) * 512].rearrange("(c p) n -> p c n", p=128)) for i in range(5)], writes=[win], sembuf=win)
                S.dma_group("pool", [(wkd[:, :, g * 128 + r * 64:g * 128 + r * 64 + 64], self.w_in[l][:, O_SK + 64 * g:O_SK + 64 * g + 64].rearrange("(c p) n -> p c n", p=128)) for g in range(2) for r in range(2)], writes=[wkd], sembuf=wkd)
                ct = [S.sb(f"ropc{i}", [128, 512], F32, esa) for i in range(2)]
                stt = [S.sb(f"rops{i}", [128, 512], F32, esa) for i in range(2)]
                f32st = [S.sb(f"pst{i}", [128, 512], F32, esa) for i in range(3)]
                bfst = [S.sb(f"pbs{i}", [128, 512], BF16, esa) for i in range(4)]
                qb = [S.sb(f"qb{i}", [128, 512], BF16, esa) for i in range(2)]
                t1 = [S.sb(f"rt1_{i}", [128, 512], F32, esa) for i in range(2)]
                t2 = [S.sb(f"rt2_{i}", [128, 512], F32, esa) for i in range(2)]
                vst = [S.sb(f"vst{i}", [128, 640], BF16, esa) for i in range(2)]
                c32 = cbf = crp = cv = 0
                chunks = []
                for i in range(2):
                    chunks.append(("f32", win, O_LX + 128 * i, self.lxs, i, True))
                    chunks.append(("f32", win, O_LG + 128 * i, self.lgs, i, False))
                for g in range(2):
                    chunks.append(("rope", win, O_SQ + 128 * g, self.sqs, g, False))
                    chunks.append(("rope", wkd, 128 * g, self.sks, g, True))
                for h in range(4):
                    chunks.append(("rope", win, O_DQ + 128 * h, self.dqs, h, False))
                    chunks.append(("rope", win, O_DK + 128 * h, self.dks, h, True))
                for ti, (off, n, j) in enumerate(st):
                    o = offs[ti]
                    koff = off if j == 0 else SEQ
                    if j == 0:
                        rc, rsn = ct[ti % 2], stt[ti % 2]
                        S.dma("sp", rc[:, :n], self.ropec[:, off:off + n], writes=[rc], sembuf=rc)
                        S.dma("sp", rsn[:, :n], self.ropes[:, off:off + n], writes=[rsn], sembuf=rsn)
                    def pmm(ci):
                        kind, wt, c0, dst, di, keysp = chunks[ci]
                        pp = self.PB[ci % 4]
                        S.mm([lambda e, c=c: e.matmul(pp[:, :n], wt[:, c, c0:c0 + 128], hn[:, c, o:o + n], start=(c == 0), stop=(c == NCH - 1)) for c in range(NCH)], reads=[wt, hn_b[ti]], writes=[pp])

                    pmm(0)
                    for ci, (kind, wt, c0, dst, di, keysp) in enumerate(chunks):
                        pp = self.PB[ci % 4]
                        if ci + 1 < len(chunks):
                            pmm(ci + 1)
                        dcol = koff if keysp else off
                        if kind == "f32":
                            sb_ = f32st[c32 % 3]
                            c32 += 1
                            S.op("act", lambda e: e.activation(out=sb_[:, :n], in_=pp[:, :n], func=AF.Copy), reads=[pp], writes=[sb_])
                            S.dma("sp", dst[di, :, dcol:dcol + n], sb_[:, :n], reads=[sb_], sembuf=sb_)
                        elif j == 1:
                            sb_ = bfst[cbf % 4]
                            cbf += 1
                            S.op("act", lambda e: e.activation(out=sb_[:, :n], in_=pp[:, :n], func=AF.Copy), reads=[pp], writes=[sb_])
                            S.dma("sp", dst[di, :, dcol:dcol + n], sb_[:, :n], reads=[sb_], sembuf=sb_)
                        else:
                            q_ = qb[crp % 2]
                            a1 = t1[crp % 2]
                            a2 = t2[crp % 2]
                            pr = self.PB[4 + crp % 2]
                            crp += 1
                            sb_ = bfst[cbf % 4]
                            cbf += 1
                            S.op("act", lambda e: e.activation(out=q_[:, :n], in_=pp[:, :n], func=AF.Copy), reads=[pp], writes=[q_])
                            S.mm([lambda e: e.matmul(pr[:, :n], self.Rm[:], q_[:, :n], start=True, stop=True)], reads=[self.Rm, q_], writes=[pr])
                            S.op("pool", lambda e: e.tensor_tensor(out=a1[:, :n], in0=q_[:, :n], in1=rc[:, :n], op=ALU.mult), reads=[q_, rc], writes=[a1])
                            S.op("dve", lambda e: e.tensor_tensor(out=a2[:, :n], in0=pr[:, :n], in1=rsn[:, :n], op=ALU.mult), reads=[pr, rsn], writes=[a2])
                            S.op("dve", lambda e: e.tensor_tensor(out=sb_[:, :n], in0=a1[:, :n], in1=a2[:, :n], op=ALU.add), reads=[a1, a2], writes=[sb_])
                            S.dma("sp", dst[di, :, dcol:dcol + n], sb_[:, :n], reads=[sb_], sembuf=sb_)
                    for sub in range(n // 128):
                        pv1 = self.PB[6]
                        pv2 = self.PB[7]
                        vs_ = vst[cv % 2]
                        cv += 1
                        S.mm([lambda e, c=c: e.matmul(pv1[:, :512], hn[:, c, o + sub * 128:o + (sub + 1) * 128], win[:, c, O_DV:O_DV + 512], start=(c == 0), stop=(c == NCH - 1)) for c in range(NCH)], reads=[win, hn_b[ti]], writes=[pv1])
                        S.mm([lambda e, c=c: e.matmul(pv2[:, :128], hn[:, c, o + sub * 128:o + (sub + 1) * 128], win[:, c, O_SV:O_SV + 128], start=(c == 0), stop=(c == NCH - 1)) for c in range(NCH)], reads=[win, hn_b[ti]], writes=[pv2])
                        S.op("act", lambda e: e.activation(out=vs_[:, 0:512], in_=pv1[:, :512], func=AF.Copy), reads=[pv1], writes=[vs_])
                        S.op("dve", lambda e: e.tensor_copy(out=vs_[:, 512:640], in_=pv2[:, :128]), reads=[pv2], writes=[vs_])
                        r0 = koff + sub * 128
                        S.dma_group("sp", [(self.dvs[r0:r0 + 128, :], vs_[:, 0:512]), (self.svs[r0:r0 + 128, :], vs_[:, 512:640])], reads=[vs_], sembuf=vs_)
                S.barrier()

    def lru(self, l, ym, with_ctx):
        S = self.S
        NQ, T, TK = self.NQ, self.T, self.TK
        K = dict(allow_slow_non_contiguous=True)
        with ExitStack() as esl:
            XH = S.sb("lru_xh", [128, TK], F32, esl)
            U = S.sb("lru_u", [128, TK], F32, esl)
            A = S.sb("lru_a", [128, TK], F32, esl)
            Bv = S.sb("lru_b", [128, TK], F32, esl)
            M = S.sb("lru_m", [128, TK], F32, esl)
            HS = S.sb("lru_hs", [128, TK], F32, esl)
            LG = S.sb("lru_lg", [128, T], F32, esl)
            WR = S.sb("lru_wr", [128, 2, 2, 128], F32, esl)
            WI = S.sb("lru_wi", [128, 2, 2, 128], F32, esl)
            c8 = S.sb("lru_c8", [128, 2, 2], F32, esl)
            c16 = S.sb("lru_c16", [128, 2, 2], F32, esl)
            S.op("pool", lambda e: e.memset(WR[:], 0.0), writes=[WR])
            S.op("pool", lambda e: e.memset(WI[:], 0.0), writes=[WI])
            idx = [(d, i, bb) for d in range(2) for i in range(2) for bb in range(2)]
            S.dma_group("sp", [(WR[bb * 64:(bb + 1) * 64, d, i, bb * 64:(bb + 1) * 64], self.lru_w_r[l, d, 2 * i + bb]) for (d, i, bb) in idx], writes=[WR], sembuf=WR)
            S.dma_group("sp", [(WI[bb * 64:(bb + 1) * 64, d, i, bb * 64:(bb + 1) * 64], self.lru_w_i[l, d, 2 * i + bb]) for (d, i, bb) in idx], writes=[WI], sembuf=WI)
            S.op("act", lambda e: e.activation(out=c8[:], in_=self.LAM[:, l], func=AF.Exp, scale=-1.0), reads=[self.LAM], writes=[c8])
            S.op("act", lambda e: e.activation(out=c8[:], in_=c8[:], func=AF.Ln, scale=1.0, bias=self.one_t[:]), reads=[c8, self.one_t], writes=[c8])
            S.op("dve", lambda e: e.tensor_scalar(out=c16[:], in0=c8[:], scalar1=-16.0, scalar2=0.0, op0=ALU.mult, op1=ALU.add), reads=[c8], writes=[c16])
            S.op("dve", lambda e: e.tensor_scalar(out=c8[:], in0=c8[:], scalar1=-8.0, scalar2=0.0, op0=ALU.mult, op1=ALU.add), reads=[c8], writes=[c8])
            segs = [(0, SEQ), (SEQ, NCTX)]
            for i in range(2):
                S.dma("sp", XH[:, :], self.lxs[i, :, :], writes=[XH], sembuf=XH)
                S.dma("sp", LG[:, :], self.lgs[i, :, :], writes=[LG], sembuf=LG)
                for (s0, sn) in segs:
                    S.op("dve", lambda e: e.tensor_scalar(out=U[:, s0:s0 + sn], in0=XH[:, s0:s0 + sn], scalar1=self.CW[:, l, 2, i:i + 1], scalar2=self.CB[:, l, i:i + 1], op0=ALU.mult, op1=ALU.add), reads=[XH, self.CW, self.CB], writes=[U])
                    for (kk, sh) in ((0, -2), (1, -1), (3, 1)):
                        if sh < 0:
                            oa, ob, ia, ib = s0 - sh, s0 + sn, s0, s0 + sn + sh
                        else:
                            oa, ob, ia, ib = s0, s0 + sn - sh, s0 + sh, s0 + sn
                        S.op("dve", lambda e: e.scalar_tensor_tensor(out=U[:, oa:ob], in0=XH[:, ia:ib], scalar=self.CW[:, l, kk, i:i + 1], in1=U[:, oa:ob], op0=ALU.mult, op1=ALU.add), reads=[XH, U, self.CW], writes=[U])
                for d in range(2):
                    nblk = (TK + 511) // 512
                    for bk in range(nblk):
                        b0 = bk * 512
                        bn = min(512, TK - b0)
                        pr = self.PB[(bk % 2) * 2]
                        pi = self.PB[(bk % 2) * 2 + 1]
                        S.mm([lambda e: e.matmul(pr[:, :bn], WR[:, d, i, :], U[:, b0:b0 + bn], start=True, stop=True)], reads=[WR, U], writes=[pr])
                        S.mm([lambda e: e.matmul(pi[:, :bn], WI[:, d, i, :], U[:, b0:b0 + bn], start=True, stop=True)], reads=[WI, U], writes=[pi])
                        S.op("act", lambda e: e.activation(out=A[:, b0:b0 + bn], in_=pr[:, :bn], func=AF.Sigmoid, bias=self.LBR[:, l, d, i:i + 1], scale=1.0), reads=[pr, self.LBR], writes=[A])
                        S.op("act", lambda e: e.activation(out=Bv[:, b0:b0 + bn], in_=pi[:, :bn], func=AF.Sigmoid, bias=self.LBI[:, l, d, i:i + 1], scale=1.0), reads=[pi, self.LBI], writes=[Bv])
                    S.op("act", lambda e: e.activation(out=M[:], in_=A[:], func=AF.Exp, scale=c16[:, d, i:i + 1]), reads=[A, c16], writes=[M])
                    S.op("act", lambda e: e.activation(out=A[:], in_=A[:], func=AF.Exp, scale=c8[:, d, i:i + 1]), reads=[A, c8], writes=[A])
                    S.op("act", lambda e: e.activation(out=M[:], in_=M[:], func=AF.Sqrt, scale=-1.0, bias=self.one_t[:]), reads=[M, self.one_t], writes=[M])
                    S.op("pool", lambda e: e.tensor_tensor(out=Bv[:], in0=Bv[:], in1=U[:], op=ALU.mult), reads=[Bv, U], writes=[Bv])
                    S.op("dve", lambda e: e.tensor_tensor(out=Bv[:], in0=Bv[:], in1=M[:], op=ALU.mult), reads=[Bv, M], writes=[Bv])
                    if d == 0:
                        S.op("dve", lambda e: e.tensor_tensor_scan(out=XH[:, SEQ:TK], data0=A[:, SEQ:TK], data1=Bv[:, SEQ:TK], initial=0.0, op0=ALU.mult, op1=ALU.add), reads=[A, Bv], writes=[XH])
                        S.op("dve", lambda e: e.tensor_tensor_scan(out=XH[:, 0:SEQ], data0=A[:, 0:SEQ], data1=Bv[:, 0:SEQ], initial=XH[:, TK - 1:TK], op0=ALU.mult, op1=ALU.add), reads=[A, Bv, XH], writes=[XH])
                        S.op("pool", lambda e: e.tensor_copy(out=HS[:], in_=XH[:]), reads=[XH], writes=[HS])
                    else:
                        S.op("dve", lambda e: e.tensor_tensor_scan(out=XH[:, SEQ:TK][:, ::-1], data0=A[:, SEQ:TK][:, ::-1], data1=Bv[:, SEQ:TK][:, ::-1], initial=0.0, op0=ALU.mult, op1=ALU.add), reads=[A, Bv], writes=[XH])
                        S.op("dve", lambda e: e.tensor_tensor_scan(out=XH[:, 0:SEQ][:, ::-1], data0=A[:, 0:SEQ][:, ::-1], data1=Bv[:, 0:SEQ][:, ::-1], initial=XH[:, SEQ:SEQ + 1], op0=ALU.mult, op1=ALU.add), reads=[A, Bv, XH], writes=[XH])
                        S.op("pool", lambda e: e.tensor_tensor(out=HS[:], in0=HS[:], in1=XH[:], op=ALU.add), reads=[HS, XH], writes=[HS])
                nt = T if with_ctx else NQ
                S.op("pool", lambda e: e.tensor_tensor(out=M[:, :nt], in0=LG[:, :nt], in1=LG[:, :nt], op=ALU.mult), reads=[LG], writes=[M])
                S.op("dve", lambda e: e.tensor_scalar(out=M[:, :nt], in0=M[:, :nt], scalar1=0.044715, scalar2=1.0, op0=ALU.mult, op1=ALU.add), reads=[M], writes=[M])
                S.op("pool", lambda e: e.tensor_tensor(out=M[:, :nt], in0=M[:, :nt], in1=LG[:, :nt], op=ALU.mult), reads=[M, LG], writes=[M])
                S.op("act", lambda e: e.activation(out=M[:, :nt], in_=M[:, :nt], func=AF.Sigmoid, scale=GELU_K), reads=[M], writes=[M])
                S.op("dve", lambda e: e.tensor_tensor(out=M[:, :nt], in0=M[:, :nt], in1=LG[:, :nt], op=ALU.mult), reads=[M, LG], writes=[M])
                S.op("dve", lambda e: e.tensor_tensor(out=ym[:, i, 0:NQ], in0=M[:, 0:NQ], in1=HS[:, 0:NQ], op=ALU.mult), reads=[M, HS], writes=[ym], skip_self=True)
                if with_ctx:
                    S.op("dve", lambda e: e.tensor_tensor(out=ym[:, i, NQ:T], in0=M[:, NQ:T], in1=HS[:, SEQ:TK], op=ALU.mult), reads=[M, HS], writes=[ym], skip_self=True)
            S.barrier()

    def swa(self, l, ym, with_ctx):
        S = self.S
        NQ, T, TK = self.NQ, self.T, self.TK
        nkt = TK // 128
        with ExitStack() as esl:
            SK = S.sb("swa_k", [128, TK], BF16, esl)
            SQ = S.sb("swa_q", [128, T], BF16, esl)
            VS = S.sb("swa_v", [128, nkt, 128], BF16, esl)
            VP = [S.sb(f"swa_vp{i}", [128, nkt, 128], BF16, esl) for i in range(2)]
            PT = [S.sb(f"swa_pt{i}", [128, 256], BF16, esl) for i in range(3)]
            es_ = S.sb("swa_es", [128, 4], F32, esl)
            zz = [S.sb(f"swa_zz{i}", [128, 128], F32, esl) for i in range(2)]
            S.op("act", lambda e: e.activation(out=es_[:], in_=self.SINK[:, l * 4:(l + 1) * 4], func=AF.Exp), reads=[self.SINK], writes=[es_])
            S.dma("sp", VS[:], self.svs.rearrange("(k p) e -> p k e", p=128), writes=[VS], sembuf=VS)
            cnt = 0
            qn = 0
            for g in range(2):
                esc = S.sb(f"swa_esc{g}", [128, 1], F32, esl)
                S.op("dve", lambda e: e.tensor_copy(out=esc[0:64, :], in_=es_[0:64, 2 * g:2 * g + 1]), reads=[es_], writes=[esc])
                S.op("dve", lambda e: e.tensor_copy(out=esc[64:128, :], in_=es_[64:128, 2 * g + 1:2 * g + 2]), reads=[es_], writes=[esc])
                S.dma("sp", SK[:], self.sks[g], writes=[SK], sembuf=SK)
                S.dma("sp", SQ[:], self.sqs[g], writes=[SQ], sembuf=SQ)
                S.op("pool", lambda e: e.memset(VP[0][:], 0.0), writes=[VP[0]])
                S.op("pool", lambda e: e.memset(VP[1][:], 0.0), writes=[VP[1]])
                S.op("pool", lambda e: e.tensor_copy(out=VP[0][:, :, 0:64], in_=VS[:, :, 64 * g:64 * g + 64]), reads=[VS], writes=[VP[0]])
                S.op("pool", lambda e: e.tensor_copy(out=VP[1][:, :, 64:128], in_=VS[:, :, 64 * g:64 * g + 64]), reads=[VS], writes=[VP[1]])
                nbl = NQ // 128
                qblocks = [(n, n * 128) for n in range(nbl)]
                if with_ctx:
                    qblocks += [(-1, NQ), (-1, NQ + 128)]
                units = []
                for (n, qc) in qblocks:
                    if n >= 0:
                        kts = ([(n - 1, 0)] if n > 0 else []) + [(n, None)] + ([(n + 1, 1)] if n < SEQ // 128 - 1 else []) + [(SEQ // 128, None), (SEQ // 128 + 1, None)]
                    else:
                        kts = [(SEQ // 128, None), (SEQ // 128 + 1, None)]
                    for ki, (kt, m) in enumerate(kts):
                        units.append((qc, kt, m, ki == 0, ki == len(kts) - 1, qn))
                    qn += 1

                def scores(u, idx):
                    qc, kt, m, first, lastk, q_ = u
                    pscs = (self.PB[(idx % 2) * 2], self.PB[(idx % 2) * 2 + 1])
                    for hh in range(2):
                        S.mm([lambda e: e.matmul(pscs[hh][:, 0:128], SK[hh * 64:(hh + 1) * 64, kt * 128:(kt + 1) * 128], SQ[hh * 64:(hh + 1) * 64, qc:qc + 128], start=True, stop=True)], reads=[SK, SQ], writes=[pscs[hh]])

                def rest(u, idx):
                    qc, kt, m, first, lastk, q_ = u
                    pscs = (self.PB[(idx % 2) * 2], self.PB[(idx % 2) * 2 + 1])
                    pt = PT[idx % 3]
                    po = self.PB[4 + (q_ % 2) * 2]
                    pz = self.PB[5 + (q_ % 2) * 2]
                    z = zz[q_ % 2]
                    S.op("act", lambda e: e.activation(out=pt[:, 0:128], in_=pscs[0][:, 0:128], func=AF.Exp, scale=0.125), reads=[pscs[0]], writes=[pt])
                    S.op("act", lambda e: e.activation(out=pt[:, 128:256], in_=pscs[1][:, 0:128], func=AF.Exp, scale=0.125), reads=[pscs[1]], writes=[pt], skip_self=True)
                    if m is not None:
                        S.op("pool", lambda e: e.tensor_tensor(out=pt[:].rearrange("p (h q) -> p h q", h=2), in0=pt[:].rearrange("p (h q) -> p h q", h=2), in1=self.masks[:, m, :].unsqueeze(1).to_broadcast([128, 2, 128]), op=ALU.mult), reads=[pt, self.masks], writes=[pt])
                    S.mm([lambda e: e.matmul(po[:, 0:128], VP[0][:, kt, :], pt[:, 0:128], start=first, stop=False),
                          lambda e: e.matmul(po[:, 0:128], VP[1][:, kt, :], pt[:, 128:256], start=False, stop=lastk)], reads=[VP[0], VP[1], pt], writes=[po])
                    S.mm([lambda e: e.matmul(pz[:, 0:128], self.onesL[:], pt[:, 0:128], start=first, stop=False),
                          lambda e: e.matmul(pz[:, 0:128], self.onesR[:], pt[:, 128:256], start=False, stop=lastk)], reads=[self.onesL, self.onesR, pt], writes=[pz])
                    if lastk:
                        S.op("dve", lambda e: e.tensor_scalar(out=z[:], in0=pz[:, 0:128], scalar1=esc[:, 0:1], scalar2=0.0, op0=ALU.add, op1=ALU.add), reads=[pz, esc], writes=[z])
                        S.op("dve", lambda e: e.reciprocal(out=z[:], in_=z[:]), reads=[z], writes=[z])
                        S.op("dve", lambda e: e.tensor_tensor(out=ym[:, 2 + g, qc:qc + 128], in0=po[:, 0:128], in1=z[:], op=ALU.mult), reads=[po, z], writes=[ym], skip_self=True)

                scores(units[0], cnt)
                for ui, u in enumerate(units):
                    if ui + 1 < len(units):
                        scores(units[ui + 1], cnt + ui + 1)
                    rest(u, cnt + ui)
                cnt += len(units)
            S.barrier()

    def diff(self, l, ym, with_ctx):
        S = self.S
        NQ, T, TK = self.NQ, self.T, self.TK
        nkt = TK // 128
        lam_init = 0.8 - 0.6 * math.exp(-0.3 * l)
        with ExitStack() as esl:
            DK = [S.sb(f"df_k{i}", [128, TK], BF16, esl) for i in range(2)]
            DQ = [S.sb(f"df_q{i}", [128, T], BF16, esl) for i in range(2)]
            DV = [S.sb(f"df_v{i}", [128, nkt, 128], BF16, esl) for i in range(2)]
            PT = [S.sb(f"df_pt{i}", [128, 1024], BF16, esl) for i in range(3)]
            r1 = S.sb("df_r1", [128, 512], F32, esl)
            r2 = S.sb("df_r2", [128, 512], F32, esl)
            o1 = S.sb("df_o1", [128, 512], F32, esl)
            o2 = S.sb("df_o2", [128, 512], F32, esl)
            osq = S.sb("df_osq", [128, 512], F32, esl)
            lt = S.sb("df_lt", [128, 2, 64], F32, esl)
            ls = S.sb("df_ls", [128, 2], F32, esl)
            nlam = S.sb("df_nlam", [128, 1], F32, esl)
            gsl = S.sb("df_gsl", [128, 1], F32, esl)
            dl = self.DLAM[:, l * 256:(l + 1) * 256].rearrange("p (a d) -> p a d", a=4)
            S.op("dve", lambda e: e.tensor_tensor(out=lt[:, 0, :], in0=dl[:, 0, :], in1=dl[:, 1, :], op=ALU.mult), reads=[self.DLAM], writes=[lt])
            S.op("dve", lambda e: e.tensor_tensor(out=lt[:, 1, :], in0=dl[:, 2, :], in1=dl[:, 3, :], op=ALU.mult), reads=[self.DLAM], writes=[lt], skip_self=True)
            S.op("dve", lambda e: e.tensor_reduce(out=ls[:], in_=lt[:], axis=mybir.AxisListType.X, op=ALU.add), reads=[lt], writes=[ls])
            S.op("act", lambda e: e.activation(out=ls[:], in_=ls[:], func=AF.Exp), reads=[ls], writes=[ls])
            S.op("dve", lambda e: e.tensor_tensor(out=nlam[:], in0=ls[:, 1:2], in1=ls[:, 0:1], op=ALU.subtract), reads=[ls], writes=[nlam])
            S.op("dve", lambda e: e.tensor_scalar(out=nlam[:], in0=nlam[:], scalar1=-lam_init, scalar2=0.0, op0=ALU.add, op1=ALU.add), reads=[nlam], writes=[nlam])
            S.op("dve", lambda e: e.tensor_scalar(out=gsl[:], in0=self.SUBG[:, l:l + 1], scalar1=1.0 - lam_init, scalar2=0.0, op0=ALU.mult, op1=ALU.add), reads=[self.SUBG], writes=[gsl])
            qtiles = [(i * 512, 512, list(range(nkt))) for i in range(NQ // 512)]
            if with_ctx:
                qtiles.append((NQ, NCTX, [SEQ // 128, SEQ // 128 + 1]))
            O1, O2, Z1, Z2 = self.PB[4], self.PB[5], self.PB[6], self.PB[7]

            def load(h):
                dk, dq, dv = DK[h % 2], DQ[h % 2], DV[h % 2]
                S.dma("sp", dk[:], self.dks[h], writes=[dk], sembuf=dk)
                S.dma("sp", dq[:], self.dqs[h], writes=[dq], sembuf=dq)
                S.dma("sp", dv[:], self.dvs[:, 128 * h:128 * (h + 1)].rearrange("(k p) e -> p k e", p=128), writes=[dv], sembuf=dv)

            units = []
            for h in range(4):
                for (qc, n, kts) in qtiles:
                    for ki, kt in enumerate(kts):
                        units.append((h, qc, n, kt, ki == 0, ki == len(kts) - 1))

            def scores(u, idx):
                h, qc, n, kt, first, lastk = u
                dk, dq = DK[h % 2], DQ[h % 2]
                pa = self.PB[(idx % 2) * 2]
                pb = self.PB[(idx % 2) * 2 + 1]
                S.mm([lambda e: e.matmul(pa[:, :n], dk[0:64, kt * 128:(kt + 1) * 128], dq[0:64, qc:qc + n], start=True, stop=True)], reads=[dk, dq], writes=[pa])
                S.mm([lambda e: e.matmul(pb[:, :n], dk[64:128, kt * 128:(kt + 1) * 128], dq[64:128, qc:qc + n], start=True, stop=True)], reads=[dk, dq], writes=[pb])

            def rest(u, idx):
                h, qc, n, kt, first, lastk = u
                dv = DV[h % 2]
                pa = self.PB[(idx % 2) * 2]
                pb = self.PB[(idx % 2) * 2 + 1]
                pt = PT[idx % 3]
                S.op("act", lambda e: e.activation(out=pt[:, 0:n], in_=pa[:, :n], func=AF.Exp, scale=0.125), reads=[pa], writes=[pt])
                S.op("act", lambda e: e.activation(out=pt[:, 512:512 + n], in_=pb[:, :n], func=AF.Exp, scale=0.125), reads=[pb], writes=[pt], skip_self=True)
                S.mm([lambda e: e.matmul(O1[:, :n], dv[:, kt, :], pt[:, 0:n], start=first, stop=lastk)], reads=[dv, pt], writes=[O1])
                S.mm([lambda e: e.matmul(O2[:, :n], dv[:, kt, :], pt[:, 512:512 + n], start=first, stop=lastk)], reads=[dv, pt], writes=[O2])
                S.mm([lambda e: e.matmul(Z1[:, :n], self.ones_b[:], pt[:, 0:n], start=first, stop=lastk)], reads=[self.ones_b, pt], writes=[Z1])
                S.mm([lambda e: e.matmul(Z2[:, :n], self.ones_b[:], pt[:, 512:512 + n], start=first, stop=lastk)], reads=[self.ones_b, pt], writes=[Z2])

            def finalize(u):
                h, qc, n, kt, first, lastk = u
                S.op("dve", lambda e: e.reciprocal(out=r1[:, :n], in_=Z1[:, :n]), reads=[Z1], writes=[r1])
                S.op("dve", lambda e: e.reciprocal(out=r2[:, :n], in_=Z2[:, :n]), reads=[Z2], writes=[r2])
                S.op("dve", lambda e: e.tensor_tensor(out=o1[:, :n], in0=O1[:, :n], in1=r1[:, :n], op=ALU.mult), reads=[O1, r1], writes=[o1])
                S.op("dve", lambda e: e.tensor_tensor(out=o2[:, :n], in0=O2[:, :n], in1=r2[:, :n], op=ALU.mult), reads=[O2, r2], writes=[o2])
                S.op("dve", lambda e: e.scalar_tensor_tensor(out=o1[:, :n], in0=o2[:, :n], scalar=nlam[:, 0:1], in1=o1[:, :n], op0=ALU.mult, op1=ALU.add), reads=[o1, o2, nlam], writes=[o1])
                S.op("act", lambda e: e.activation(out=osq[:, :n], in_=o1[:, :n], func=AF.Square), reads=[o1], writes=[osq])
                S.mm([lambda e: e.matmul(Z1[:, :n], self.ones_f[:], osq[:, :n], start=True, stop=True)], reads=[self.ones_f, osq], writes=[Z1])
                S.op("act", lambda e: e.activation(out=r1[:, :n], in_=Z1[:, :n], func=AF.Sqrt, scale=1.0 / 128, bias=self.eps_t[:]), reads=[Z1, self.eps_t], writes=[r1])
                S.op("dve", lambda e: e.reciprocal(out=r1[:, :n], in_=r1[:, :n]), reads=[r1], writes=[r1])
                S.op("dve", lambda e: e.scalar_tensor_tensor(out=ym[:, 4 + h, qc:qc + n], in0=o1[:, :n], scalar=gsl[:, 0:1], in1=r1[:, :n], op0=ALU.mult, op1=ALU.mult), reads=[o1, r1, gsl], writes=[ym], skip_self=True)

            load(0)
            load(1)
            scores(units[0], 0)
            for idx, u in enumerate(units):
                if idx + 1 < len(units):
                    scores(units[idx + 1], idx + 1)
                rest(u, idx)
                if u[5]:
                    finalize(u)
                    if (idx + 1 == len(units) or units[idx + 1][0] != u[0]) and u[0] + 2 < 4:
                        load(u[0] + 2)
            S.barrier()

    def final(self):
        S = self.S
        with ExitStack() as esl:
            hbs = [S.sb(f"hbf{i}", [128, NCH, 512], F32, esl) for i in range(2)]
            junk = S.sb("fjunk", [128, D], BF16, esl)
            ssq = [S.sb(f"fssq{i}", [128, 1], F32, esl) for i in range(2)]
            ot = [S.sb(f"fot{i}", [128, D], F32, esl) for i in range(2)]
            ss2 = [S.sb(f"fss2_{i}", [128, 2], F32, esl) for i in range(2)]
            cnt = 0
            for ti, (off, n, j) in enumerate(self.tiles):
                if j == 1:
                    continue
                b = hbs[ti % 2]
                S.dma("sp", b[:, :, :n], self.hsv(self.hs, off, n), writes=[b], sembuf=b)
                for sub in range(n // 128):
                    pa = self.PB[(cnt % 2) * 2]
                    pb = self.PB[(cnt % 2) * 2 + 1]
                    s_ = ssq[cnt % 2]
                    o_ = ot[cnt % 2]
                    cnt += 1
                    for half, pp in enumerate((pa, pb)):
                        ppv = pp[:, :].rearrange("p (c t) -> p c t", c=4)
                        S.mm([lambda e, cc=cc: e.transpose(ppv[:, cc, :], b[:, half * 4 + cc, sub * 128:(sub + 1) * 128], self.ident[:]) for cc in range(4)], reads=[b, self.ident], writes=[pp])
                    s2 = ss2[cnt % 2]
                    S.op("act", lambda e: e.activation(out=junk[:, 0:512], in_=pa[:, :], func=AF.Square, accum_out=s2[:, 0:1]), reads=[pa], writes=[junk, s2])
                    S.op("act", lambda e: e.activation(out=junk[:, 512:1024], in_=pb[:, :], func=AF.Square, accum_out=s2[:, 1:2]), reads=[pb], writes=[junk, s2])
                    S.op("dve", lambda e: e.tensor_tensor(out=s_[:], in0=s2[:, 0:1], in1=s2[:, 1:2], op=ALU.add), reads=[s2], writes=[s_])
                    S.op("act", lambda e: e.activation(out=s_[:], in_=s_[:], func=AF.Sqrt, scale=1.0 / D, bias=self.eps_t[:]), reads=[s_, self.eps_t], writes=[s_])
                    S.op("dve", lambda e: e.reciprocal(out=s_[:], in_=s_[:]), reads=[s_], writes=[s_])
                    S.op("dve", lambda e: e.scalar_tensor_tensor(out=o_[:, 0:512], in0=pa[:, :], scalar=s_[:, 0:1], in1=self.FG[:, 0:512], op0=ALU.mult, op1=ALU.mult), reads=[pa, s_, self.FG], writes=[o_])
                    S.op("dve", lambda e: e.scalar_tensor_tensor(out=o_[:, 512:1024], in0=pb[:, :], scalar=s_[:, 0:1], in1=self.FG[:, 512:1024], op0=ALU.mult, op1=ALU.mult), reads=[pb, s_, self.FG], writes=[o_], skip_self=True)
                    r0 = off + sub * 128
                    S.dma("sp", self.out[r0:r0 + 128, :], o_[:], reads=[o_], sembuf=o_)


def rope_tables(n0, n):
    t = np.arange(n0, n0 + n)
    row = (t // GRID_W).astype(np.float32)
    col = (t % GRID_W).astype(np.float32)
    nf = 16
    inv = (10000.0 ** (-np.arange(nf, dtype=np.float32) / nf)).astype(np.float32)
    ang = np.concatenate([row[:, None] * inv, col[:, None] * inv], axis=-1).astype(np.float32)
    idx = np.arange(128) % 32
    c = np.cos(ang).astype(np.float32)[:, idx].T
    s = np.sin(ang).astype(np.float32)[:, idx].T
    return np.ascontiguousarray(c), np.ascontiguousarray(s)


_NC_CACHE = {}


def kernel(**inputs):
    NQ = SEQ
    key = ("main", NQ)
    if key not in _NC_CACHE:
        _NC_CACHE[key] = Kern(NQ).build()
    nc = _NC_CACHE[key]
    f = lambda a: np.ascontiguousarray(np.asarray(a, dtype=np.float32))
    rc, rs = rope_tables(0, NQ)
    shared = {k: f(inputs[k]) for k in ("w_ada", "b_ada", "norm_g", "ffn1_w_gu", "ffn1_w_down", "ffn2_w_gu", "ffn2_w_down",
                                        "w_in", "w_out", "conv_w", "conv_b", "lru_w_r", "lru_b_r", "lru_w_i", "lru_b_i",
                                        "lru_lambda", "swa_sink", "diff_lambda", "diff_subln_g", "final_g")}
    x = f(inputs["x"])
    ctx = f(inputs["ctx"])
    c = f(inputs["c"])
    cc = f(inputs["c_ctx"])
    in_maps = []
    for core in range(8):
        b = core % 4
        m = dict(shared)
        m["x"] = x[b]
        m["ctx"] = ctx[b]
        m["cvec"] = np.ascontiguousarray(np.stack([c[b], cc], axis=0))
        m["ropec"] = rc
        m["ropes"] = rs
        in_maps.append(m)
    res = run_bass_kernel_spmd(nc, in_maps, core_ids=list(range(8)))
    out = np.stack([np.asarray(res.results[b]["out"], dtype=np.float32) for b in range(4)], axis=0)
    return out
```

```python
import math
import numpy as np
import concourse.bass as bass
import concourse.mybir as mybir
from concourse.bass_utils import run_bass_kernel_spmd
from contextlib import ExitStack

F32 = mybir.dt.float32
BF16 = mybir.dt.bfloat16
AF = mybir.ActivationFunctionType
ALU = mybir.AluOpType

D = 1024
NCH = 8
DFF = 2816
NJ = 22
SEQ = 4096
NCTX = 256
DEPTH = 2
EPS = 1e-6
NMOD = 9
GRID_W = 64
O_LX, O_LG, O_SQ, O_SK, O_SV, O_DQ, O_DK, O_DV = 0, 256, 512, 768, 896, 1024, 1536, 2048
INW = 2560
GELU_K = 2.0 * math.sqrt(2.0 / math.pi)


class Buf:
    __slots__ = ("name", "w", "r", "t")

    def __init__(self, name, t=None):
        self.name = name
        self.w = None
        self.r = {}
        self.t = t

    def __getitem__(self, k):
        return self.t[k]


class Sched:
    ENG = ("pe", "act", "dve", "pool", "sp")

    def __init__(self, nc, es):
        self.nc = nc
        self.es = es
        self.E = {"pe": nc.tensor, "act": nc.scalar, "dve": nc.vector, "pool": nc.gpsimd, "sp": nc.sync}
        self.sems = {}
        self.cnt = {}
        self.seen = {e: {} for e in self.ENG}
        for e in self.ENG:
            self.sems[e] = es.enter_context(nc.semaphore("sem_" + e))
            self.cnt[e] = 0
        self.dsem = {}
        self.nwait = 0
        self.nins = 0

    def sb(self, name, shape, dt, es=None):
        self.uid = getattr(self, "uid", 0) + 1
        t = (es or self.es).enter_context(self.nc.sbuf_tensor(f"{name}_{self.uid}", list(shape), dt))
        return Buf(name, t)

    def ps(self, name, shape, dt=F32, es=None):
        t = (es or self.es).enter_context(self.nc.psum_tensor(name, list(shape), dt))
        return Buf(name, t)

    def _wait(self, eng, ev):
        if ev is None:
            return
        key, val = ev
        if self.seen[eng].get(key, 0) >= val:
            return
        self.E[eng].wait_ge(self.sems[key], val)
        self.seen[eng][key] = val
        self.nwait += 1

    def _deps(self, eng, reads, writes, skip_self=False):
        for b in reads:
            if b.w is not None:
                self._wait(eng, b.w)
        for b in writes:
            if b.w is not None and not (skip_self and b.w[0] == eng):
                self._wait(eng, b.w)
            for k, v in b.r.items():
                if not (skip_self and k == eng):
                    self._wait(eng, (k, v))

    def _mark(self, ev, reads, writes):
        k, v = ev
        for b in reads:
            if b.r.get(k, 0) < v:
                b.r[k] = v
        for b in writes:
            b.w = ev
            b.r = {}

    def op(self, eng, fn, reads=(), writes=(), skip_self=False):
        self._deps(eng, reads, writes, skip_self)
        ins = fn(self.E[eng])
        self.cnt[eng] += 1
        ins.then_inc(self.sems[eng], 1)
        self._mark((eng, self.cnt[eng]), reads, writes)
        self.nins += 1
        return ins

    def mm(self, fns, reads=(), writes=()):
        eng = "pe"
        self._deps(eng, reads, writes, skip_self=True)
        ins = None
        for fn in fns:
            ins = fn(self.E[eng])
            self.nins += 1
        self.cnt[eng] += 1
        ins.then_inc(self.sems[eng], 1)
        self._mark((eng, self.cnt[eng]), reads, writes)
        return ins

    def dma(self, q, out, in_, reads=(), writes=(), sembuf=None, **kw):
        self._deps(q, reads, writes)
        name = sembuf.name
        if name not in self.dsem:
            sem = self.es.enter_context(self.nc.semaphore("ds_" + name))
            self.dsem[name] = [sem, 0]
            self.sems[("d", name)] = sem
        ent = self.dsem[name]
        ins = self.E[q].dma_start(out=out, in_=in_, **kw)
        ins.then_inc(ent[0], 16)
        ent[1] += 16
        self._mark((("d", name), ent[1]), reads, writes)
        self.nins += 1
        return ins

    def dma_group(self, q, pairs, reads=(), writes=(), sembuf=None, **kw):
        self._deps(q, reads, writes)
        name = sembuf.name
        if name not in self.dsem:
            sem = self.es.enter_context(self.nc.semaphore("ds_" + name))
            self.dsem[name] = [sem, 0]
            self.sems[("d", name)] = sem
        ent = self.dsem[name]
        for (o, i) in pairs:
            self.E[q].dma_start(out=o, in_=i, **kw).then_inc(ent[0], 16)
            ent[1] += 16
            self.nins += 1
        self._mark((("d", name), ent[1]), reads, writes)

    def barrier(self):
        for e in self.ENG:
            for e2 in self.ENG:
                if e2 != e and self.cnt[e2] > 0:
                    self._wait(e, (e2, self.cnt[e2]))
            for name, ent in self.dsem.items():
                if ent[1] > 0:
                    self._wait(e, (("d", name), ent[1]))


class Kern:
    def __init__(self, NQ, debug=None):
        self.NQ = NQ
        self.T = NQ + NCTX
        self.TK = SEQ + NCTX
        self.debug = debug or set()
        self.tiles = [(i * 512, 512, 0) for i in range(NQ // 512)] + [(NQ, NCTX, 1)]
        self.supers = []
        cur = []
        tot = 0
        for t in self.tiles:
            if tot + t[1] > 2304:
                self.supers.append(cur)
                cur, tot = [], 0
            cur.append(t)
            tot += t[1]
        self.supers.append(cur)
        self.STMAX = max(sum(t[1] for t in s) for s in self.supers)

    def build(self):
        nc = bass.Bass("TRN2", target_bir_lowering=False)
        self.nc = nc
        NQ, T, TK = self.NQ, self.T, self.TK

        def inp(name, shape, dt=F32):
            return nc.dram_tensor(name, list(shape), dt, kind="ExternalInput").ap()

        def scratch(name, shape, dt):
            return nc.dram_tensor(name, list(shape), dt, kind="Internal").ap()

        self.x = inp("x", [NQ, D])
        self.ctx = inp("ctx", [NCTX, D])
        self.cvec = inp("cvec", [2, D])
        self.ropec = inp("ropec", [128, NQ])
        self.ropes = inp("ropes", [128, NQ])
        self.w_ada = inp("w_ada", [DEPTH, D, NMOD * D])
        self.b_ada = inp("b_ada", [DEPTH, NMOD * D])
        self.norm_g = inp("norm_g", [DEPTH, 3, D])
        self.ffn_w_gu = [inp("ffn1_w_gu", [DEPTH, D, 2 * DFF]), inp("ffn2_w_gu", [DEPTH, D, 2 * DFF])]
        self.ffn_w_down = [inp("ffn1_w_down", [DEPTH, DFF, D]), inp("ffn2_w_down", [DEPTH, DFF, D])]
        self.w_in = inp("w_in", [DEPTH, D, INW])
        self.w_out = inp("w_out", [DEPTH, D, D])
        self.conv_w = inp("conv_w", [DEPTH, 4, 256])
        self.conv_b = inp("conv_b", [DEPTH, 256])
        self.lru_w_r = inp("lru_w_r", [DEPTH, 2, 4, 64, 64])
        self.lru_b_r = inp("lru_b_r", [DEPTH, 2, 256])
        self.lru_w_i = inp("lru_w_i", [DEPTH, 2, 4, 64, 64])
        self.lru_b_i = inp("lru_b_i", [DEPTH, 2, 256])
        self.lru_lambda = inp("lru_lambda", [DEPTH, 2, 256])
        self.swa_sink = inp("swa_sink", [DEPTH, 4])
        self.diff_lambda = inp("diff_lambda", [DEPTH, 4, 64])
        self.diff_subln_g = inp("diff_subln_g", [DEPTH, 128])
        self.final_g = inp("final_g", [D])
        self.stopping = any(k.startswith("stop_") for k in self.debug)
        self.out = nc.dram_tensor("out", [128 if self.stopping else NQ, D], F32, kind="ExternalOutput").ap()

        self.hs = scratch("hs", [NCH, 128, T], F32)
        self.lxs = scratch("lxs", [2, 128, TK], F32)
        self.lgs = scratch("lgs", [2, 128, T], F32)
        self.sqs = scratch("sqs", [2, 128, T], BF16)
        self.sks = scratch("sks", [2, 128, TK], BF16)
        self.dqs = scratch("dqs", [4, 128, T], BF16)
        self.dks = scratch("dks", [4, 128, TK], BF16)
        self.svs = scratch("svs", [TK, 128], BF16)
        self.dvs = scratch("dvs", [TK, 512], BF16)
        self.yms = scratch("yms", [NCH, 128, T], BF16)

        with ExitStack() as es:
            self.es = es
            S = Sched(nc, es)
            self.S = S
            self.PB = [S.ps(f"pb{i}", [128, 512], F32) for i in range(8)]
            self.consts()
            self.mods()
            self.phase0()
            if "modT" in self.debug:
                md = nc.dram_tensor("modT_o", [128, DEPTH * 72 * 2], F32, kind="ExternalOutput").ap()
                S.dma("sp", md[:, :], self.modT[:].rearrange("p l i j -> p (l i j)"), reads=[self.modT], sembuf=self.modT)
                S.barrier()
            for l in range(DEPTH):
                if "stop_p0" in self.debug:
                    break
                last = l == DEPTH - 1
                self.ffn(l, 0, self.tiles)
                if "h1" in self.debug and l == 0:
                    self.dump_ct("h1", self.hs, F32)
                if "stop_ffn1" in self.debug and l == 0:
                    break
                self.proj(l)
                if "proj" in self.debug and l == 0:
                    for nm, tn, dt in (("lxs", self.lxs, F32), ("lgs", self.lgs, F32), ("sqs", self.sqs, BF16), ("sks", self.sks, BF16), ("dqs", self.dqs, BF16), ("dks", self.dks, BF16)):
                        self.dump_ct(nm, tn, dt)
                    self.dump_tm("svs", self.svs, BF16)
                    self.dump_tm("dvs", self.dvs, BF16)
                if "stop_proj" in self.debug and l == 0:
                    break
                with ExitStack() as esm:
                    ym = S.sb("ymix", [128, NCH, T], BF16, esm)
                    if "no_lru" not in self.debug:
                        self.lru(l, ym, not last)
                    if "no_swa" not in self.debug:
                        self.swa(l, ym, not last)
                    if "no_diff" not in self.debug:
                        self.diff(l, ym, not last)
                    S.barrier()
                    if "yms" in self.debug and l == 0:
                        S.dma("sp", self.hsv(self.yms, 0, T), ym[:, :, :], reads=[ym], sembuf=ym)
                        S.barrier()
                        self.dump_ct("yms", self.yms, BF16)
                    tl = self.tiles if not last else [t for t in self.tiles if t[2] == 0]
                    self.down(l, ym, NCH, self.w_out[l], 5, 1.0, tl, [t[0] for t in tl])
                if "h2" in self.debug and l == 0:
                    self.dump_ct("h2", self.hs, F32)
                if "stop_mix" in self.debug and l == 0:
                    break
                tl = self.tiles if not last else [t for t in self.tiles if t[2] == 0]
                self.ffn(l, 2, tl)
                if "h3" in self.debug and l == 0:
                    self.dump_ct("h3", self.hs, F32)
                if "stop_l0" in self.debug and l == 0:
                    break
            if not self.stopping:
                self.final()
            S.barrier()
            print("instructions", S.nins, "waits", S.nwait, "sems", len(S.dsem) + 5)
        return nc

    def dump_ct(self, name, ten, dt):
        S, nc = self.S, self.nc
        C = ten.shape[0]
        o = nc.dram_tensor("d_" + name, [C, 128, 512], dt, kind="ExternalOutput").ap()
        b = Buf("dbgdump")
        S.dma_group("sp", [(o[:, :, d0:d0 + n], ten[:, :, a:a + n]) for (a, n, d0) in ((0, 128, 0), (self.NQ - 128, 128, 128), (self.NQ, 256, 256))], sembuf=b)
        S.barrier()

    def dump_tm(self, name, ten, dt):
        S, nc = self.S, self.nc
        E = ten.shape[1]
        o = nc.dram_tensor("d_" + name, [512, E], dt, kind="ExternalOutput").ap()
        b = Buf("dbgdump")
        S.dma_group("sp", [(o[d0:d0 + n, :], ten[a:a + n, :]) for (a, n, d0) in ((0, 128, 0), (SEQ - 128, 128, 128), (SEQ, 256, 256))], sembuf=b)
        S.barrier()

    def hsv(self, ten, off, n):
        return ten[:, :, off:off + n].rearrange("c p t -> p c t")

    def consts(self):
        S, nc, es = self.S, self.nc, self.es
        K = dict(allow_slow_non_contiguous=True)
        self.cbuf = Buf("cbuf")

        cbufs = []

        def cl(name, shape, src, dt=F32):
            b = S.sb(name, shape, dt)
            S.dma("sp", b[:], src, sembuf=self.cbuf, **K)
            cbufs.append(b)
            return b

        def cs(name, shape):
            return S.sb(name, shape, F32)

        def ld(b, dst, src):
            S.dma("sp", dst, src, sembuf=self.cbuf, **K)
            cbufs.append(b)

        self.sT = cs("sT", [128, NCH, 2])
        for j in range(2):
            ld(self.sT, self.sT[:, :, j], self.cvec[j].rearrange("(c p) -> p c", p=128))
        self.bT = cs("bT", [128, DEPTH, 72])
        self.NG = cs("NG", [128, DEPTH, 3, NCH])
        self.CW = cs("CW", [128, DEPTH, 4, 2])
        self.CB = cs("CB", [128, DEPTH, 2])
        self.LBR = cs("LBR", [128, DEPTH, 2, 2])
        self.LBI = cs("LBI", [128, DEPTH, 2, 2])
        self.LAM = cs("LAM", [128, DEPTH, 2, 2])
        self.SUBG = cs("SUBG", [128, DEPTH])
        for l in range(DEPTH):
            ld(self.bT, self.bT[:, l, :], self.b_ada[l].rearrange("(i p) -> p i", p=128))
            for k in range(3):
                ld(self.NG, self.NG[:, l, k, :], self.norm_g[l, k].rearrange("(c p) -> p c", p=128))
            for k in range(4):
                ld(self.CW, self.CW[:, l, k, :], self.conv_w[l, k].rearrange("(i p) -> p i", p=128))
            ld(self.CB, self.CB[:, l, :], self.conv_b[l].rearrange("(i p) -> p i", p=128))
            for d in range(2):
                ld(self.LBR, self.LBR[:, l, d, :], self.lru_b_r[l, d].rearrange("(i p) -> p i", p=128))
                ld(self.LBI, self.LBI[:, l, d, :], self.lru_b_i[l, d].rearrange("(i p) -> p i", p=128))
                ld(self.LAM, self.LAM[:, l, d, :], self.lru_lambda[l, d].rearrange("(i p) -> p i", p=128))
            ld(self.SUBG, self.SUBG[:, l:l + 1], self.diff_subln_g[l].rearrange("(p o) -> p o", o=1))
        self.SINK = cl("SINK", [128, DEPTH * 4], self.swa_sink.rearrange("l h -> (l h)").partition_broadcast(128))
        self.DLAM = cl("DLAM", [128, DEPTH * 4 * 64], self.diff_lambda.rearrange("l a d -> (l a d)").partition_broadcast(128))
        self.FG = cl("FG", [128, D], self.final_g.partition_broadcast(128))
        for b in cbufs:
            b.w = (("d", "cbuf"), S.dsem["cbuf"][1])

        self.ident = S.sb("ident", [128, 128], F32)
        S.op("pool", lambda e: e.memset(self.ident[:], 0.0), writes=[self.ident])
        S.op("pool", lambda e: e.affine_select(out=self.ident[:], in_=self.ident[:], pattern=[[-1, 128]], compare_op=ALU.not_equal, fill=1.0, base=0, channel_multiplier=1), reads=[self.ident], writes=[self.ident])
        self.ones_b = S.sb("ones_b", [128, 128], BF16)
        S.op("pool", lambda e: e.memset(self.ones_b[:], 1.0), writes=[self.ones_b])
        self.ones_f = S.sb("ones_f", [128, 128], F32)
        S.op("pool", lambda e: e.memset(self.ones_f[:], 1.0), writes=[self.ones_f])
        self.onesL = S.sb("onesL", [128, 128], BF16)
        self.onesR = S.sb("onesR", [128, 128], BF16)
        S.op("pool", lambda e: e.memset(self.onesL[:], 0.0), writes=[self.onesL])
        S.op("pool", lambda e: e.memset(self.onesL[:, 0:64], 1.0), writes=[self.onesL])
        S.op("pool", lambda e: e.memset(self.onesR[:], 0.0), writes=[self.onesR])
        S.op("pool", lambda e: e.memset(self.onesR[:, 64:128], 1.0), writes=[self.onesR])
        rt = S.sb("rm_tmp", [128, 2, 128], F32)
        S.op("pool", lambda e: e.memset(rt[:], 0.0), writes=[rt])
        S.op("pool", lambda e: e.affine_select(out=rt[:, 0, :], in_=rt[:, 0, :], pattern=[[-1, 128]], compare_op=ALU.not_equal, fill=1.0, base=32, channel_multiplier=1), reads=[rt], writes=[rt])
        S.op("pool", lambda e: e.affine_select(out=rt[:, 1, :], in_=rt[:, 1, :], pattern=[[-1, 128]], compare_op=ALU.not_equal, fill=-1.0, base=-32, channel_multiplier=1), reads=[rt], writes=[rt])
        for a in (0, 64):
            S.op("pool", lambda e: e.memset(rt[:, 0, a:a + 32], 0.0), reads=[rt], writes=[rt])
            S.op("pool", lambda e: e.memset(rt[:, 1, a + 32:a + 64], 0.0), reads=[rt], writes=[rt])
        self.Rm = S.sb("Rm", [128, 128], BF16)
        S.op("pool", lambda e: e.tensor_tensor(out=self.Rm[:], in0=rt[:, 0, :], in1=rt[:, 1, :], op=ALU.add), reads=[rt], writes=[self.Rm])
        mt = S.sb("mask_tmp", [128, 2, 128], F32)
        S.op("pool", lambda e: e.memset(mt[:], 1.0), writes=[mt])
        S.op("pool", lambda e: e.affine_select(out=mt[:, 0, :], in_=mt[:, 0, :], pattern=[[-1, 128]], compare_op=ALU.is_ge, fill=0.0, base=0, channel_multiplier=1), reads=[mt], writes=[mt])
        S.op("pool", lambda e: e.affine_select(out=mt[:, 1, :], in_=mt[:, 1, :], pattern=[[1, 128]], compare_op=ALU.is_ge, fill=0.0, base=0, channel_multiplier=-1), reads=[mt], writes=[mt])
        self.masks = S.sb("masks", [128, 2, 128], BF16)
        S.op("pool", lambda e: e.tensor_copy(out=self.masks[:], in_=mt[:]), reads=[mt], writes=[self.masks])
        self.eps_t = S.sb("eps_t", [128, 1], F32)
        S.op("pool", lambda e: e.memset(self.eps_t[:], EPS), writes=[self.eps_t])
        self.one_t = S.sb("one_t", [128, 1], F32)
        S.op("pool", lambda e: e.memset(self.one_t[:], 1.0), writes=[self.one_t])

    def mods(self):
        S = self.S
        S.op("act", lambda e: e.activation(out=self.sT[:], in_=self.sT[:], func=AF.Silu), reads=[self.sT], writes=[self.sT])
        self.modT = S.sb("modT", [128, DEPTH, 72, 2], F32)
        self.GS = S.sb("GS", [128, DEPTH, 3, NCH, 2], F32)
        with ExitStack() as esl:
            wa = [S.sb(f"wa{i}", [128, NCH, 512], F32, esl) for i in range(2)]
            for l in range(DEPTH):
                pm = self.PB[l]
                pmv = pm[:, 0:144].rearrange("p (i j) -> p i j", j=2)
                for cb in range(18):
                    w = wa[cb % 2]
                    S.dma("sp", w[:], self.w_ada[l][:, cb * 512:(cb + 1) * 512].rearrange("(c p) n -> p c n", p=128), writes=[w], sembuf=w)
                    fns = []
                    for sub in range(4):
                        idx = cb * 4 + sub
                        for c in range(NCH):
                            fns.append(lambda e, idx=idx, sub=sub, c=c: e.matmul(pmv[:, idx, :], w[:, c, sub * 128:(sub + 1) * 128], self.sT[:, c, :], start=(c == 0), stop=(c == NCH - 1)))
                    S.mm(fns, reads=[w, self.sT], writes=[pm])
                S.op("dve", lambda e: e.tensor_tensor(out=self.modT[:, l], in0=pmv, in1=self.bT[:, l, :].unsqueeze(2).to_broadcast([128, 72, 2]), op=ALU.add), reads=[pm, self.bT], writes=[self.modT])
                for k in range(3):
                    sc = self.modT[:, l, (3 * k + 1) * 8:(3 * k + 2) * 8, :]
                    S.op("dve", lambda e: e.scalar_tensor_tensor(out=self.GS[:, l, k], in0=sc, scalar=1.0, in1=self.NG[:, l, k, :].unsqueeze(2).to_broadcast([128, NCH, 2]), op0=ALU.add, op1=ALU.mult), reads=[self.modT, self.NG], writes=[self.GS])
            S.barrier()

    def mod_ap(self, l, m, c, j):
        return self.modT[:, l, m * 8 + c, j:j + 1]

    def phase0(self):
        S = self.S
        with ExitStack() as esl:
            xin = [S.sb(f"xin{i}", [128, D], F32, esl) for i in range(3)]
            hst = [S.sb(f"hst{i}", [128, NCH, 512], F32, esl) for i in range(2)]
            cnt = 0
            for ti, (off, n, j) in enumerate(self.tiles):
                hb = hst[ti % 2]
                for sub in range(n // 128):
                    xb = xin[cnt % 3]
                    src = self.x[off + sub * 128: off + (sub + 1) * 128, :] if j == 0 else self.ctx[sub * 128:(sub + 1) * 128, :]
                    S.dma("sp", xb[:], src, writes=[xb], sembuf=xb)
                    for half in range(2):
                        pt = self.PB[(cnt * 2 + half) % 8]
                        ptv = pt[:, :].rearrange("p (c t) -> p c t", c=4)
                        S.mm([lambda e, cc=cc: e.transpose(ptv[:, cc, :], xb[:, (half * 4 + cc) * 128:(half * 4 + cc + 1) * 128], self.ident[:]) for cc in range(4)], reads=[xb, self.ident], writes=[pt])
                        eng = "act" if half == 0 else "dve"
                        dst = hb[:, half * 4:(half + 1) * 4, sub * 128:(sub + 1) * 128]
                        if eng == "act":
                            S.op("act", lambda e: e.activation(out=dst, in_=ptv, func=AF.Copy), reads=[pt], writes=[hb])
                        else:
                            S.op("dve", lambda e: e.tensor_copy(out=dst, in_=ptv), reads=[pt], writes=[hb])
                    cnt += 1
                S.dma("sp", self.hsv(self.hs, off, n), hb[:, :, :n], reads=[hb], sembuf=hb)
            S.barrier()

    def norm_pass(self, l, k, tiles, offs, hn, hn_b):
        S = self.S
        with ExitStack() as esl:
            hbs = [S.sb(f"hb{i}", [128, NCH, 512], F32, esl) for i in range(2)]
            sq = S.sb("sq", [128, NCH, 512], BF16, esl)
            rs = [S.sb(f"rs{i}", [128, 512], F32, esl) for i in range(2)]
            for ti, (off, n, j) in enumerate(tiles):
                b = hbs[ti % 2]
                r = rs[ti % 2]
                S.dma("sp", b[:, :, :n], self.hsv(self.hs, off, n), writes=[b], sembuf=b)
                S.op("act", lambda e: e.activation(out=sq[:, :, :n], in_=b[:, :, :n], func=AF.Square), reads=[b], writes=[sq])
                ss = self.PB[4 + ti % 2]
                S.mm([lambda e, c=c: e.matmul(ss[:, :n], self.ones_b[:], sq[:, c, :n], start=(c == 0), stop=(c == NCH - 1)) for c in range(NCH)], reads=[sq, self.ones_b], writes=[ss])
                S.op("act", lambda e: e.activation(out=r[:, :n], in_=ss[:, :n], func=AF.Sqrt, scale=1.0 / D, bias=self.eps_t[:]), reads=[ss, self.eps_t], writes=[r])
                S.op("dve", lambda e: e.reciprocal(out=r[:, :n], in_=r[:, :n]), reads=[r], writes=[r])
                S.op("dve", lambda e: e.tensor_tensor(out=b[:, :, :n], in0=b[:, :, :n], in1=r[:, :n].unsqueeze(1).to_broadcast([128, NCH, n]), op=ALU.mult), reads=[b, r], writes=[b])
                o = offs[ti]
                for c in range(NCH):
                    S.op("act", lambda e, c=c: e.activation(out=hn[:, c, o:o + n], in_=b[:, c, :n], func=AF.Identity, scale=self.GS[:, l, k, c, j:j + 1], bias=self.mod_ap(l, 3 * k, c, j)), reads=[b, self.GS, self.modT], writes=[hn_b[ti]])
            S.barrier()

    def down(self, l, act, nk, wdram, gate_m, coef, tiles, offs, act_b=None):
        S = self.S
        with ExitStack() as esl:
            wd = [S.sb(f"wd{i}", [128, nk, 512], BF16, esl) for i in range(2)]
            hbs = [S.sb(f"hbd{i}", [128, NCH, 512], F32, esl) for i in range(2)]
            gt = S.sb("gt", [128, NCH, 2], F32, esl)
            S.op("pool", lambda e: e.tensor_scalar(out=gt[:], in0=self.modT[:, l, gate_m * 8:(gate_m + 1) * 8, :], scalar1=float(coef), scalar2=0.0, op0=ALU.mult, op1=ALU.add), reads=[self.modT], writes=[gt])
            for half in range(2):
                S.dma("pool", wd[half][:], wdram[:, half * 512:(half + 1) * 512].rearrange("(j p) n -> p j n", p=128), writes=[wd[half]], sembuf=wd[half])
            for ti, (off, n, j) in enumerate(tiles):
                b = hbs[ti % 2]
                o = offs[ti]
                rd = [act_b[ti]] if act_b is not None else [act]
                S.dma("sp", b[:, :, :n], self.hsv(self.hs, off, n), writes=[b], sembuf=b)
                for oc in range(NCH):
                    py = self.PB[4 + oc % 4]
                    w = wd[oc // 4]
                    cs = (oc % 4) * 128
                    S.mm([lambda e, jj=jj: e.matmul(py[:, :n], w[:, jj, cs:cs + 128], act[:, jj, o:o + n], start=(jj == 0), stop=(jj == nk - 1)) for jj in range(nk)], reads=[w] + rd, writes=[py])
                    S.op("dve", lambda e: e.scalar_tensor_tensor(out=b[:, oc, :n], in0=py[:, :n], scalar=gt[:, oc, j:j + 1], in1=b[:, oc, :n], op0=ALU.mult, op1=ALU.add), reads=[py, gt] + ([b] if oc == 0 else []), writes=[b], skip_self=True)
                S.dma("sp", self.hsv(self.hs, off, n), b[:, :, :n], reads=[b], sembuf=b)
            S.barrier()

    def ffn(self, l, k, tiles):
        S = self.S
        wgu = self.ffn_w_gu[0 if k == 0 else 1][l]
        wdn = self.ffn_w_down[0 if k == 0 else 1][l]
        supers = []
        cur, tot = [], 0
        for t in tiles:
            if tot + t[1] > 2304:
                supers.append(cur)
                cur, tot = [], 0
            cur.append(t)
            tot += t[1]
        supers.append(cur)
        for st in supers:
            offs = []
            o = 0
            for t in st:
                offs.append(o)
                o += t[1]
            with ExitStack() as esf:
                act = S.sb("act", [128, NJ, self.STMAX], BF16, esf)
                act_b = [Buf(f"act_b{i}") for i in range(len(st))]
                with ExitStack() as esa:
                    hn = S.sb("hn", [128, NCH, self.STMAX], BF16, esa)
                    hn_b = [Buf(f"hn_b{i}") for i in range(len(st))]
                    self.norm_pass(l, k, st, offs, hn, hn_b)
                    with ExitStack() as es2:
                        wg = [S.sb(f"wg{i}", [128, NCH, 256], BF16, es2) for i in range(2)]
                        wu = [S.sb(f"wu{i}", [128, NCH, 256], BF16, es2) for i in range(2)]
                        sg = [S.sb(f"sg{i}", [128, 512], BF16, es2) for i in range(2)]
                        cnt = 0
                        for jg in range(NJ // 2):
                            s = jg % 2
                            S.dma("pool", wg[s][:], wgu[:, jg * 256:(jg + 1) * 256].rearrange("(c p) n -> p c n", p=128), writes=[wg[s]], sembuf=wg[s])
                            S.dma("pool", wu[s][:], wgu[:, DFF + jg * 256:DFF + (jg + 1) * 256].rearrange("(c p) n -> p c n", p=128), writes=[wu[s]], sembuf=wu[s])
                            for jj in range(2):
                                jx = jg * 2 + jj
                                for ti, (off, n, j) in enumerate(st):
                                    o = offs[ti]
                                    pg = self.PB[(cnt % 2) * 2]
                                    pu = self.PB[(cnt % 2) * 2 + 1]
                                    sgb = sg[cnt % 2]
                                    S.mm([lambda e, c=c: e.matmul(pg[:, :n], wg[s][:, c, jj * 128:(jj + 1) * 128], hn[:, c, o:o + n], start=(c == 0), stop=(c == NCH - 1)) for c in range(NCH)], reads=[wg[s], hn_b[ti]], writes=[pg])
                                    S.mm([lambda e, c=c: e.matmul(pu[:, :n], wu[s][:, c, jj * 128:(jj + 1) * 128], hn[:, c, o:o + n], start=(c == 0), stop=(c == NCH - 1)) for c in range(NCH)], reads=[wu[s], hn_b[ti]], writes=[pu])
                                    S.op("act", lambda e: e.activation(out=sgb[:, :n], in_=pg[:, :n], func=AF.Silu), reads=[pg], writes=[sgb])
                                    S.op("dve", lambda e: e.tensor_tensor(out=act[:, jx, o:o + n], in0=sgb[:, :n], in1=pu[:, :n], op=ALU.mult), reads=[sgb, pu], writes=[act_b[ti]])
                                    cnt += 1
                    S.barrier()
                self.down(l, act, NJ, wdn, 3 * k + 2, 0.5, st, offs, act_b)

    def proj(self, l):
        S = self.S
        NQ = self.NQ
        for st in self.supers:
            offs = []
            o = 0
            for t in st:
                offs.append(o)
                o += t[1]
            with ExitStack() as esa:
                hn = S.sb("hn", [128, NCH, self.STMAX], BF16, esa)
                hn_b = [Buf(f"hn_b{i}") for i in range(len(st))]
                self.norm_pass(l, 1, st, offs, hn, hn_b)
                win = S.sb("win", [128, NCH, INW], BF16, esa)
                wkd = S.sb("wkd", [128, NCH, 256], BF16, esa)
                S.dma_group("pool", [(win[:, :, i * 512:(i + 1) * 512], self.w_in[l][:, i * 512:(i + 1) * 512].rearrange("(c p) n -> p c n", p=128)) for i in range(5)], writes=[win], sembuf=win)
                S.dma_group("pool", [(wkd[:, :, g * 128 + r * 64:g * 128 + r * 64 + 64], self.w_in[l][:, O_SK + 64 * g:O_SK + 64 * g + 64].rearrange("(c p) n -> p c n", p=128)) for g in range(2) for r in range(2)], writes=[wkd], sembuf=wkd)
                ct = [S.sb(f"ropc{i}", [128, 512], F32, esa) for i in range(2)]
                stt = [S.sb(f"rops{i}", [128, 512], F32, esa) for i in range(2)]
                f32st = [S.sb(f"pst{i}", [128, 512], F32, esa) for i in range(3)]
                bfst = [S.sb(f"pbs{i}", [128, 512], BF16, esa) for i in range(4)]
                qb = [S.sb(f"qb{i}", [128, 512], BF16, esa) for i in range(2)]
                t1 = [S.sb(f"rt1_{i}", [128, 512], F32, esa) for i in range(2)]
                t2 = [S.sb(f"rt2_{i}", [128, 512], F32, esa) for i in range(2)]
                vst = [S.sb(f"vst{i}", [128, 640], BF16, esa) for i in range(2)]
                c32 = cbf = crp = cv = 0
                chunks = []
                for i in range(2):
                    chunks.append(("f32", win, O_LX + 128 * i, self.lxs, i, True))
                    chunks.append(("f32", win, O_LG + 128 * i, self.lgs, i, False))
                for g in range(2):
                    chunks.append(("rope", win, O_SQ + 128 * g, self.sqs, g, False))
                    chunks.append(("rope", wkd, 128 * g, self.sks, g, True))
                for h in range(4):
                    chunks.append(("rope", win, O_DQ + 128 * h, self.dqs, h, False))
                    chunks.append(("rope", win, O_DK + 128 * h, self.dks, h, True))
                for ti, (off, n, j) in enumerate(st):
                    o = offs[ti]
                    koff = off if j == 0 else SEQ
                    if j == 0:
                        rc, rsn = ct[ti % 2], stt[ti % 2]
                        S.dma("sp", rc[:, :n], self.ropec[:, off:off + n], writes=[rc], sembuf=rc)
                        S.dma("sp", rsn[:, :n], self.ropes[:, off:off + n], writes=[rsn], sembuf=rsn)
                    def pmm(ci):
                        kind, wt, c0, dst, di, keysp = chunks[ci]
                        pp = self.PB[ci % 4]
                        S.mm([lambda e, c=c: e.matmul(pp[:, :n], wt[:, c, c0:c0 + 128], hn[:, c, o:o + n], start=(c == 0), stop=(c == NCH - 1)) for c in range(NCH)], reads=[wt, hn_b[ti]], writes=[pp])

                    pmm(0)
                    for ci, (kind, wt, c0, dst, di, keysp) in enumerate(chunks):
                        pp = self.PB[ci % 4]
                        if ci + 1 < len(chunks):
                            pmm(ci + 1)
                        dcol = koff if keysp else off
                        if kind == "f32":
                            sb_ = f32st[c32 % 3]
                            c32 += 1
                            S.op("act", lambda e: e.activation(out=sb_[:, :n], in_=pp[:, :n], func=AF.Copy), reads=[pp], writes=[sb_])
                            S.dma("sp", dst[di, :, dcol:dcol + n], sb_[:, :n], reads=[sb_], sembuf=sb_)
                        elif j == 1:
                            sb_ = bfst[cbf % 4]
                            cbf += 1
                            S.op("act", lambda e: e.activation(out=sb_[:, :n], in_=pp[:, :n], func=AF.Copy), reads=[pp], writes=[sb_])
                            S.dma("sp", dst[di, :, dcol:dcol + n], sb_[:, :n], reads=[sb_], sembuf=sb_)
                        else:
                            q_ = qb[crp % 2]
                            a1 = t1[crp % 2]
                            a2 = t2[crp % 2]
                            pr = self.PB[4 + crp % 2]
                            crp += 1
                            sb_ = bfst[cbf % 4]
                            cbf += 1
                            S.op("act", lambda e: e.activation(out=q_[:, :n], in_=pp[:, :n], func=AF.Copy), reads=[pp], writes=[q_])
                            S.mm([lambda e: e.matmul(pr[:, :n], self.Rm[:], q_[:, :n], start=True, stop=True)], reads=[self.Rm, q_], writes=[pr])
                            S.op("pool", lambda e: e.tensor_tensor(out=a1[:, :n], in0=q_[:, :n], in1=rc[:, :n], op=ALU.mult), reads=[q_, rc], writes=[a1])
                            S.op("dve", lambda e: e.tensor_tensor(out=a2[:, :n], in0=pr[:, :n], in1=rsn[:, :n], op=ALU.mult), reads=[pr, rsn], writes=[a2])
                            S.op("dve", lambda e: e.tensor_tensor(out=sb_[:, :n], in0=a1[:, :n], in1=a2[:, :n], op=ALU.add), reads=[a1, a2], writes=[sb_])
                            S.dma("sp", dst[di, :, dcol:dcol + n], sb_[:, :n], reads=[sb_], sembuf=sb_)
                    for sub in range(n // 128):
                        pv1 = self.PB[6]
                        pv2 = self.PB[7]
                        vs_ = vst[cv % 2]
                        cv += 1
                        S.mm([lambda e, c=c: e.matmul(pv1[:, :512], hn[:, c, o + sub * 128:o + (sub + 1) * 128], win[:, c, O_DV:O_DV + 512], start=(c == 0), stop=(c == NCH - 1)) for c in range(NCH)], reads=[win, hn_b[ti]], writes=[pv1])
                        S.mm([lambda e, c=c: e.matmul(pv2[:, :128], hn[:, c, o + sub * 128:o + (sub + 1) * 128], win[:, c, O_SV:O_SV + 128], start=(c == 0), stop=(c == NCH - 1)) for c in range(NCH)], reads=[win, hn_b[ti]], writes=[pv2])
                        S.op("act", lambda e: e.activation(out=vs_[:, 0:512], in_=pv1[:, :512], func=AF.Copy), reads=[pv1], writes=[vs_])
                        S.op("dve", lambda e: e.tensor_copy(out=vs_[:, 512:640], in_=pv2[:, :128]), reads=[pv2], writes=[vs_])
                        r0 = koff + sub * 128
                        S.dma_group("sp", [(self.dvs[r0:r0 + 128, :], vs_[:, 0:512]), (self.svs[r0:r0 + 128, :], vs_[:, 512:640])], reads=[vs_], sembuf=vs_)
                S.barrier()

    def lru(self, l, ym, with_ctx):
        S = self.S
        NQ, T, TK = self.NQ, self.T, self.TK
        K = dict(allow_slow_non_contiguous=True)
        with ExitStack() as esl:
            XH = S.sb("lru_xh", [128, TK], F32, esl)
            U = S.sb("lru_u", [128, TK], F32, esl)
            A = S.sb("lru_a", [128, TK], F32, esl)
            Bv = S.sb("lru_b", [128, TK], F32, esl)
            M = S.sb("lru_m", [128, TK], F32, esl)
            HS = S.sb("lru_hs", [128, TK], F32, esl)
            LG = S.sb("lru_lg", [128, T], F32, esl)
            WR = S.sb("lru_wr", [128, 2, 2, 128], F32, esl)
            WI = S.sb("lru_wi", [128, 2, 2, 128], F32, esl)
            c8 = S.sb("lru_c8", [128, 2, 2], F32, esl)
            c16 = S.sb("lru_c16", [128, 2, 2], F32, esl)
            S.op("pool", lambda e: e.memset(WR[:], 0.0), writes=[WR])
            S.op("pool", lambda e: e.memset(WI[:], 0.0), writes=[WI])
            idx = [(d, i, bb) for d in range(2) for i in range(2) for bb in range(2)]
            S.dma_group("sp", [(WR[bb * 64:(bb + 1) * 64, d, i, bb * 64:(bb + 1) * 64], self.lru_w_r[l, d, 2 * i + bb]) for (d, i, bb) in idx], writes=[WR], sembuf=WR)
            S.dma_group("sp", [(WI[bb * 64:(bb + 1) * 64, d, i, bb * 64:(bb + 1) * 64], self.lru_w_i[l, d, 2 * i + bb]) for (d, i, bb) in idx], writes=[WI], sembuf=WI)
            S.op("act", lambda e: e.activation(out=c8[:], in_=self.LAM[:, l], func=AF.Exp, scale=-1.0), reads=[self.LAM], writes=[c8])
            S.op("act", lambda e: e.activation(out=c8[:], in_=c8[:], func=AF.Ln, scale=1.0, bias=self.one_t[:]), reads=[c8, self.one_t], writes=[c8])
            S.op("dve", lambda e: e.tensor_scalar(out=c16[:], in0=c8[:], scalar1=-16.0, scalar2=0.0, op0=ALU.mult, op1=ALU.add), reads=[c8], writes=[c16])
            S.op("dve", lambda e: e.tensor_scalar(out=c8[:], in0=c8[:], scalar1=-8.0, scalar2=0.0, op0=ALU.mult, op1=ALU.add), reads=[c8], writes=[c8])
            segs = [(0, SEQ), (SEQ, NCTX)]
            for i in range(2):
                S.dma("sp", XH[:, :], self.lxs[i, :, :], writes=[XH], sembuf=XH)
                S.dma("sp", LG[:, :], self.lgs[i, :, :], writes=[LG], sembuf=LG)
                for (s0, sn) in segs:
                    S.op("dve", lambda e: e.tensor_scalar(out=U[:, s0:s0 + sn], in0=XH[:, s0:s0 + sn], scalar1=self.CW[:, l, 2, i:i + 1], scalar2=self.CB[:, l, i:i + 1], op0=ALU.mult, op1=ALU.add), reads=[XH, self.CW, self.CB], writes=[U])
                    for (kk, sh) in ((0, -2), (1, -1), (3, 1)):
                        if sh < 0:
                            oa, ob, ia, ib = s0 - sh, s0 + sn, s0, s0 + sn + sh
                        else:
                            oa, ob, ia, ib = s0, s0 + sn - sh, s0 + sh, s0 + sn
                        S.op("dve", lambda e: e.scalar_tensor_tensor(out=U[:, oa:ob], in0=XH[:, ia:ib], scalar=self.CW[:, l, kk, i:i + 1], in1=U[:, oa:ob], op0=ALU.mult, op1=ALU.add), reads=[XH, U, self.CW], writes=[U])
                for d in range(2):
                    nblk = (TK + 511) // 512
                    for bk in range(nblk):
                        b0 = bk * 512
                        bn = min(512, TK - b0)
                        pr = self.PB[(bk % 2) * 2]
                        pi = self.PB[(bk % 2) * 2 + 1]
                        S.mm([lambda e: e.matmul(pr[:, :bn], WR[:, d, i, :], U[:, b0:b0 + bn], start=True, stop=True)], reads=[WR, U], writes=[pr])
                        S.mm([lambda e: e.matmul(pi[:, :bn], WI[:, d, i, :], U[:, b0:b0 + bn], start=True, stop=True)], reads=[WI, U], writes=[pi])
                        S.op("act", lambda e: e.activation(out=A[:, b0:b0 + bn], in_=pr[:, :bn], func=AF.Sigmoid, bias=self.LBR[:, l, d, i:i + 1], scale=1.0), reads=[pr, self.LBR], writes=[A])
                        S.op("act", lambda e: e.activation(out=Bv[:, b0:b0 + bn], in_=pi[:, :bn], func=AF.Sigmoid, bias=self.LBI[:, l, d, i:i + 1], scale=1.0), reads=[pi, self.LBI], writes=[Bv])
                    S.op("act", lambda e: e.activation(out=M[:], in_=A[:], func=AF.Exp, scale=c16[:, d, i:i + 1]), reads=[A, c16], writes=[M])
                    S.op("act", lambda e: e.activation(out=A[:], in_=A[:], func=AF.Exp, scale=c8[:, d, i:i + 1]), reads=[A, c8], writes=[A])
                    S.op("act", lambda e: e.activation(out=M[:], in_=M[:], func=AF.Sqrt, scale=-1.0, bias=self.one_t[:]), reads=[M, self.one_t], writes=[M])
                    S.op("pool", lambda e: e.tensor_tensor(out=Bv[:], in0=Bv[:], in1=U[:], op=ALU.mult), reads=[Bv, U], writes=[Bv])
                    S.op("dve", lambda e: e.tensor_tensor(out=Bv[:], in0=Bv[:], in1=M[:], op=ALU.mult), reads=[Bv, M], writes=[Bv])
                    if d == 0:
                        S.op("dve", lambda e: e.tensor_tensor_scan(out=XH[:, SEQ:TK], data0=A[:, SEQ:TK], data1=Bv[:, SEQ:TK], initial=0.0, op0=ALU.mult, op1=ALU.add), reads=[A, Bv], writes=[XH])
                        S.op("dve", lambda e: e.tensor_tensor_scan(out=XH[:, 0:SEQ], data0=A[:, 0:SEQ], data1=Bv[:, 0:SEQ], initial=XH[:, TK - 1:TK], op0=ALU.mult, op1=ALU.add), reads=[A, Bv, XH], writes=[XH])
                        S.op("pool", lambda e: e.tensor_copy(out=HS[:], in_=XH[:]), reads=[XH], writes=[HS])
                    else:
                        S.op("dve", lambda e: e.tensor_tensor_scan(out=XH[:, SEQ:TK][:, ::-1], data0=A[:, SEQ:TK][:, ::-1], data1=Bv[:, SEQ:TK][:, ::-1], initial=0.0, op0=ALU.mult, op1=ALU.add), reads=[A, Bv], writes=[XH])
                        S.op("dve", lambda e: e.tensor_tensor_scan(out=XH[:, 0:SEQ][:, ::-1], data0=A[:, 0:SEQ][:, ::-1], data1=Bv[:, 0:SEQ][:, ::-1], initial=XH[:, SEQ:SEQ + 1], op0=ALU.mult, op1=ALU.add), reads=[A, Bv, XH], writes=[XH])
                        S.op("pool", lambda e: e.tensor_tensor(out=HS[:], in0=HS[:], in1=XH[:], op=ALU.add), reads=[HS, XH], writes=[HS])
                nt = T if with_ctx else NQ
                S.op("pool", lambda e: e.tensor_tensor(out=M[:, :nt], in0=LG[:, :nt], in1=LG[:, :nt], op=ALU.mult), reads=[LG], writes=[M])
                S.op("dve", lambda e: e.tensor_scalar(out=M[:, :nt], in0=M[:, :nt], scalar1=0.044715, scalar2=1.0, op0=ALU.mult, op1=ALU.add), reads=[M], writes=[M])
                S.op("pool", lambda e: e.tensor_tensor(out=M[:, :nt], in0=M[:, :nt], in1=LG[:, :nt], op=ALU.mult), reads=[M, LG], writes=[M])
                S.op("act", lambda e: e.activation(out=M[:, :nt], in_=M[:, :nt], func=AF.Sigmoid, scale=GELU_K), reads=[M], writes=[M])
                S.op("dve", lambda e: e.tensor_tensor(out=M[:, :nt], in0=M[:, :nt], in1=LG[:, :nt], op=ALU.mult), reads=[M, LG], writes=[M])
                S.op("dve", lambda e: e.tensor_tensor(out=ym[:, i, 0:NQ], in0=M[:, 0:NQ], in1=HS[:, 0:NQ], op=ALU.mult), reads=[M, HS], writes=[ym], skip_self=True)
                if with_ctx:
                    S.op("dve", lambda e: e.tensor_tensor(out=ym[:, i, NQ:T], in0=M[:, NQ:T], in1=HS[:, SEQ:TK], op=ALU.mult), reads=[M, HS], writes=[ym], skip_self=True)
            S.barrier()

    def swa(self, l, ym, with_ctx):
        S = self.S
        NQ, T, TK = self.NQ, self.T, self.TK
        nkt = TK // 128
        with ExitStack() as esl:
            SK = S.sb("swa_k", [128, TK], BF16, esl)
            SQ = S.sb("swa_q", [128, T], BF16, esl)
            VS = S.sb("swa_v", [128, nkt, 128], BF16, esl)
            VP = [S.sb(f"swa_vp{i}", [128, nkt, 128], BF16, esl) for i in range(2)]
            PT = [S.sb(f"swa_pt{i}", [128, 256], BF16, esl) for i in range(3)]
            es_ = S.sb("swa_es", [128, 4], F32, esl)
            zz = [S.sb(f"swa_zz{i}", [128, 128], F32, esl) for i in range(2)]
            S.op("act", lambda e: e.activation(out=es_[:], in_=self.SINK[:, l * 4:(l + 1) * 4], func=AF.Exp), reads=[self.SINK], writes=[es_])
            S.dma("sp", VS[:], self.svs.rearrange("(k p) e -> p k e", p=128), writes=[VS], sembuf=VS)
            cnt = 0
            qn = 0
            for g in range(2):
                esc = S.sb(f"swa_esc{g}", [128, 1], F32, esl)
                S.op("dve", lambda e: e.tensor_copy(out=esc[0:64, :], in_=es_[0:64, 2 * g:2 * g + 1]), reads=[es_], writes=[esc])
                S.op("dve", lambda e: e.tensor_copy(out=esc[64:128, :], in_=es_[64:128, 2 * g + 1:2 * g + 2]), reads=[es_], writes=[esc])
                S.dma("sp", SK[:], self.sks[g], writes=[SK], sembuf=SK)
                S.dma("sp", SQ[:], self.sqs[g], writes=[SQ], sembuf=SQ)
                S.op("pool", lambda e: e.memset(VP[0][:], 0.0), writes=[VP[0]])
                S.op("pool", lambda e: e.memset(VP[1][:], 0.0), writes=[VP[1]])
                S.op("pool", lambda e: e.tensor_copy(out=VP[0][:, :, 0:64], in_=VS[:, :, 64 * g:64 * g + 64]), reads=[VS], writes=[VP[0]])
                S.op("pool", lambda e: e.tensor_copy(out=VP[1][:, :, 64:128], in_=VS[:, :, 64 * g:64 * g + 64]), reads=[VS], writes=[VP[1]])
                nbl = NQ // 128
                qblocks = [(n, n * 128) for n in range(nbl)]
                if with_ctx:
                    qblocks += [(-1, NQ), (-1, NQ + 128)]
                units = []
                for (n, qc) in qblocks:
                    if n >= 0:
                        kts = ([(n - 1, 0)] if n > 0 else []) + [(n, None)] + ([(n + 1, 1)] if n < SEQ // 128 - 1 else []) + [(SEQ // 128, None), (SEQ // 128 + 1, None)]
                    else:
                        kts = [(SEQ // 128, None), (SEQ // 128 + 1, None)]
                    for ki, (kt, m) in enumerate(kts):
                        units.append((qc, kt, m, ki == 0, ki == len(kts) - 1, qn))
                    qn += 1

                def scores(u, idx):
                    qc, kt, m, first, lastk, q_ = u
                    pscs = (self.PB[(idx % 2) * 2], self.PB[(idx % 2) * 2 + 1])
                    for hh in range(2):
                        S.mm([lambda e: e.matmul(pscs[hh][:, 0:128], SK[hh * 64:(hh + 1) * 64, kt * 128:(kt + 1) * 128], SQ[hh * 64:(hh + 1) * 64, qc:qc + 128], start=True, stop=True)], reads=[SK, SQ], writes=[pscs[hh]])

                def rest(u, idx):
                    qc, kt, m, first, lastk, q_ = u
                    pscs = (self.PB[(idx % 2) * 2], self.PB[(idx % 2) * 2 + 1])
                    pt = PT[idx % 3]
                    po = self.PB[4 + (q_ % 2) * 2]
                    pz = self.PB[5 + (q_ % 2) * 2]
                    z = zz[q_ % 2]
                    S.op("act", lambda e: e.activation(out=pt[:, 0:128], in_=pscs[0][:, 0:128], func=AF.Exp, scale=0.125), reads=[pscs[0]], writes=[pt])
                    S.op("act", lambda e: e.activation(out=pt[:, 128:256], in_=pscs[1][:, 0:128], func=AF.Exp, scale=0.125), reads=[pscs[1]], writes=[pt], skip_self=True)
                    if m is not None:
                        S.op("pool", lambda e: e.tensor_tensor(out=pt[:].rearrange("p (h q) -> p h q", h=2), in0=pt[:].rearrange("p (h q) -> p h q", h=2), in1=self.masks[:, m, :].unsqueeze(1).to_broadcast([128, 2, 128]), op=ALU.mult), reads=[pt, self.masks], writes=[pt])
                    S.mm([lambda e: e.matmul(po[:, 0:128], VP[0][:, kt, :], pt[:, 0:128], start=first, stop=False),
                          lambda e: e.matmul(po[:, 0:128], VP[1][:, kt, :], pt[:, 128:256], start=False, stop=lastk)], reads=[VP[0], VP[1], pt], writes=[po])
                    S.mm([lambda e: e.matmul(pz[:, 0:128], self.onesL[:], pt[:, 0:128], start=first, stop=False),
                          lambda e: e.matmul(pz[:, 0:128], self.onesR[:], pt[:, 128:256], start=False, stop=lastk)], reads=[self.onesL, self.onesR, pt], writes=[pz])
                    if lastk:
                        S.op("dve", lambda e: e.tensor_scalar(out=z[:], in0=pz[:, 0:128], scalar1=esc[:, 0:1], scalar2=0.0, op0=ALU.add, op1=ALU.add), reads=[pz, esc], writes=[z])
                        S.op("dve", lambda e: e.reciprocal(out=z[:], in_=z[:]), reads=[z], writes=[z])
                        S.op("dve", lambda e: e.tensor_tensor(out=ym[:, 2 + g, qc:qc + 128], in0=po[:, 0:128], in1=z[:], op=ALU.mult), reads=[po, z], writes=[ym], skip_self=True)

                scores(units[0], cnt)
                for ui, u in enumerate(units):
                    if ui + 1 < len(units):
                        scores(units[ui + 1], cnt + ui + 1)
                    rest(u, cnt + ui)
                cnt += len(units)
            S.barrier()

    def diff(self, l, ym, with_ctx):
        S = self.S
        NQ, T, TK = self.NQ, self.T, self.TK
        nkt = TK // 128
        lam_init = 0.8 - 0.6 * math.exp(-0.3 * l)
        with ExitStack() as esl:
            DK = [S.sb(f"df_k{i}", [128, TK], BF16, esl) for i in range(2)]
            DQ = [S.sb(f"df_q{i}", [128, T], BF16, esl) for i in range(2)]
            DV = [S.sb(f"df_v{i}", [128, nkt, 128], BF16, esl) for i in range(2)]
            PT = [S.sb(f"df_pt{i}", [128, 1024], BF16, esl) for i in range(3)]
            r1 = S.sb("df_r1", [128, 512], F32, esl)
            r2 = S.sb("df_r2", [128, 512], F32, esl)
            o1 = S.sb("df_o1", [128, 512], F32, esl)
            o2 = S.sb("df_o2", [128, 512], F32, esl)
            osq = S.sb("df_osq", [128, 512], F32, esl)
            lt = S.sb("df_lt", [128, 2, 64], F32, esl)
            ls = S.sb("df_ls", [128, 2], F32, esl)
            nlam = S.sb("df_nlam", [128, 1], F32, esl)
            gsl = S.sb("df_gsl", [128, 1], F32, esl)
            dl = self.DLAM[:, l * 256:(l + 1) * 256].rearrange("p (a d) -> p a d", a=4)
            S.op("dve", lambda e: e.tensor_tensor(out=lt[:, 0, :], in0=dl[:, 0, :], in1=dl[:, 1, :], op=ALU.mult), reads=[self.DLAM], writes=[lt])
            S.op("dve", lambda e: e.tensor_tensor(out=lt[:, 1, :], in0=dl[:, 2, :], in1=dl[:, 3, :], op=ALU.mult), reads=[self.DLAM], writes=[lt], skip_self=True)
            S.op("dve", lambda e: e.tensor_reduce(out=ls[:], in_=lt[:], axis=mybir.AxisListType.X, op=ALU.add), reads=[lt], writes=[ls])
            S.op("act", lambda e: e.activation(out=ls[:], in_=ls[:], func=AF.Exp), reads=[ls], writes=[ls])
            S.op("dve", lambda e: e.tensor_tensor(out=nlam[:], in0=ls[:, 1:2], in1=ls[:, 0:1], op=ALU.subtract), reads=[ls], writes=[nlam])
            S.op("dve", lambda e: e.tensor_scalar(out=nlam[:], in0=nlam[:], scalar1=-lam_init, scalar2=0.0, op0=ALU.add, op1=ALU.add), reads=[nlam], writes=[nlam])
            S.op("dve", lambda e: e.tensor_scalar(out=gsl[:], in0=self.SUBG[:, l:l + 1], scalar1=1.0 - lam_init, scalar2=0.0, op0=ALU.mult, op1=ALU.add), reads=[self.SUBG], writes=[gsl])
            qtiles = [(i * 512, 512, list(range(nkt))) for i in range(NQ // 512)]
            if with_ctx:
                qtiles.append((NQ, NCTX, [SEQ // 128, SEQ // 128 + 1]))
            O1, O2, Z1, Z2 = self.PB[4], self.PB[5], self.PB[6], self.PB[7]

            def load(h):
                dk, dq, dv = DK[h % 2], DQ[h % 2], DV[h % 2]
                S.dma("sp", dk[:], self.dks[h], writes=[dk], sembuf=dk)
                S.dma("sp", dq[:], self.dqs[h], writes=[dq], sembuf=dq)
                S.dma("sp", dv[:], self.dvs[:, 128 * h:128 * (h + 1)].rearrange("(k p) e -> p k e", p=128), writes=[dv], sembuf=dv)

            units = []
            for h in range(4):
                for (qc, n, kts) in qtiles:
                    for ki, kt in enumerate(kts):
                        units.append((h, qc, n, kt, ki == 0, ki == len(kts) - 1))

            def scores(u, idx):
                h, qc, n, kt, first, lastk = u
                dk, dq = DK[h % 2], DQ[h % 2]
                pa = self.PB[(idx % 2) * 2]
                pb = self.PB[(idx % 2) * 2 + 1]
                S.mm([lambda e: e.matmul(pa[:, :n], dk[0:64, kt * 128:(kt + 1) * 128], dq[0:64, qc:qc + n], start=True, stop=True)], reads=[dk, dq], writes=[pa])
                S.mm([lambda e: e.matmul(pb[:, :n], dk[64:128, kt * 128:(kt + 1) * 128], dq[64:128, qc:qc + n], start=True, stop=True)], reads=[dk, dq], writes=[pb])

            def rest(u, idx):
                h, qc, n, kt, first, lastk = u
                dv = DV[h % 2]
                pa = self.PB[(idx % 2) * 2]
                pb = self.PB[(idx % 2) * 2 + 1]
                pt = PT[idx % 3]
                S.op("act", lambda e: e.activation(out=pt[:, 0:n], in_=pa[:, :n], func=AF.Exp, scale=0.125), reads=[pa], writes=[pt])
                S.op("act", lambda e: e.activation(out=pt[:, 512:512 + n], in_=pb[:, :n], func=AF.Exp, scale=0.125), reads=[pb], writes=[pt], skip_self=True)
                S.mm([lambda e: e.matmul(O1[:, :n], dv[:, kt, :], pt[:, 0:n], start=first, stop=lastk)], reads=[dv, pt], writes=[O1])
                S.mm([lambda e: e.matmul(O2[:, :n], dv[:, kt, :], pt[:, 512:512 + n], start=first, stop=lastk)], reads=[dv, pt], writes=[O2])
                S.mm([lambda e: e.matmul(Z1[:, :n], self.ones_b[:], pt[:, 0:n], start=first, stop=lastk)], reads=[self.ones_b, pt], writes=[Z1])
                S.mm([lambda e: e.matmul(Z2[:, :n], self.ones_b[:], pt[:, 512:512 + n], start=first, stop=lastk)], reads=[self.ones_b, pt], writes=[Z2])

            def finalize(u):
                h, qc, n, kt, first, lastk = u
                S.op("dve", lambda e: e.reciprocal(out=r1[:, :n], in_=Z1[:, :n]), reads=[Z1], writes=[r1])
                S.op("dve", lambda e: e.reciprocal(out=r2[:, :n], in_=Z2[:, :n]), reads=[Z2], writes=[r2])
                S.op("dve", lambda e: e.tensor_tensor(out=o1[:, :n], in0=O1[:, :n], in1=r1[:, :n], op=ALU.mult), reads=[O1, r1], writes=[o1])
                S.op("dve", lambda e: e.tensor_tensor(out=o2[:, :n], in0=O2[:, :n], in1=r2[:, :n], op=ALU.mult), reads=[O2, r2], writes=[o2])
                S.op("dve", lambda e: e.scalar_tensor_tensor(out=o1[:, :n], in0=o2[:, :n], scalar=nlam[:, 0:1], in1=o1[:, :n], op0=ALU.mult, op1=ALU.add), reads=[o1, o2, nlam], writes=[o1])
                S.op("act", lambda e: e.activation(out=osq[:, :n], in_=o1[:, :n], func=AF.Square), reads=[o1], writes=[osq])
                S.mm([lambda e: e.matmul(Z1[:, :n], self.ones_f[:], osq[:, :n], start=True, stop=True)], reads=[self.ones_f, osq], writes=[Z1])
                S.op("act", lambda e: e.activation(out=r1[:, :n], in_=Z1[:, :n], func=AF.Sqrt, scale=1.0 / 128, bias=self.eps_t[:]), reads=[Z1, self.eps_t], writes=[r1])
                S.op("dve", lambda e: e.reciprocal(out=r1[:, :n], in_=r1[:, :n]), reads=[r1], writes=[r1])
                S.op("dve", lambda e: e.scalar_tensor_tensor(out=ym[:, 4 + h, qc:qc + n], in0=o1[:, :n], scalar=gsl[:, 0:1], in1=r1[:, :n], op0=ALU.mult, op1=ALU.mult), reads=[o1, r1, gsl], writes=[ym], skip_self=True)

            load(0)
            load(1)
            scores(units[0], 0)
            for idx, u in enumerate(units):
                if idx + 1 < len(units):
                    scores(units[idx + 1], idx + 1)
                rest(u, idx)
                if u[5]:
                    finalize(u)
                    if (idx + 1 == len(units) or units[idx + 1][0] != u[0]) and u[0] + 2 < 4:
                        load(u[0] + 2)
            S.barrier()

    def final(self):
        S = self.S
        with ExitStack() as esl:
            hbs = [S.sb(f"hbf{i}", [128, NCH, 512], F32, esl) for i in range(2)]
            junk = S.sb("fjunk", [128, D], BF16, esl)
            ssq = [S.sb(f"fssq{i}", [128, 1], F32, esl) for i in range(2)]
            ot = [S.sb(f"fot{i}", [128, D], F32, esl) for i in range(2)]
            ss2 = [S.sb(f"fss2_{i}", [128, 2], F32, esl) for i in range(2)]
            cnt = 0
            for ti, (off, n, j) in enumerate(self.tiles):
                if j == 1:
                    continue
                b = hbs[ti % 2]
                S.dma("sp", b[:, :, :n], self.hsv(self.hs, off, n), writes=[b], sembuf=b)
                for sub in range(n // 128):
                    pa = self.PB[(cnt % 2) * 2]
                    pb = self.PB[(cnt % 2) * 2 + 1]
                    s_ = ssq[cnt % 2]
                    o_ = ot[cnt % 2]
                    cnt += 1
                    for half, pp in enumerate((pa, pb)):
                        ppv = pp[:, :].rearrange("p (c t) -> p c t", c=4)
                        S.mm([lambda e, cc=cc: e.transpose(ppv[:, cc, :], b[:, half * 4 + cc, sub * 128:(sub + 1) * 128], self.ident[:]) for cc in range(4)], reads=[b, self.ident], writes=[pp])
                    s2 = ss2[cnt % 2]
                    S.op("act", lambda e: e.activation(out=junk[:, 0:512], in_=pa[:, :], func=AF.Square, accum_out=s2[:, 0:1]), reads=[pa], writes=[junk, s2])
                    S.op("act", lambda e: e.activation(out=junk[:, 512:1024], in_=pb[:, :], func=AF.Square, accum_out=s2[:, 1:2]), reads=[pb], writes=[junk, s2])
                    S.op("dve", lambda e: e.tensor_tensor(out=s_[:], in0=s2[:, 0:1], in1=s2[:, 1:2], op=ALU.add), reads=[s2], writes=[s_])
                    S.op("act", lambda e: e.activation(out=s_[:], in_=s_[:], func=AF.Sqrt, scale=1.0 / D, bias=self.eps_t[:]), reads=[s_, self.eps_t], writes=[s_])
                    S.op("dve", lambda e: e.reciprocal(out=s_[:], in_=s_[:]), reads=[s_], writes=[s_])
                    S.op("dve", lambda e: e.scalar_tensor_tensor(out=o_[:, 0:512], in0=pa[:, :], scalar=s_[:, 0:1], in1=self.FG[:, 0:512], op0=ALU.mult, op1=ALU.mult), reads=[pa, s_, self.FG], writes=[o_])
                    S.op("dve", lambda e: e.scalar_tensor_tensor(out=o_[:, 512:1024], in0=pb[:, :], scalar=s_[:, 0:1], in1=self.FG[:, 512:1024], op0=ALU.mult, op1=ALU.mult), reads=[pb, s_, self.FG], writes=[o_], skip_self=True)
                    r0 = off + sub * 128
                    S.dma("sp", self.out[r0:r0 + 128, :], o_[:], reads=[o_], sembuf=o_)


def rope_tables(n0, n):
    t = np.arange(n0, n0 + n)
    row = (t // GRID_W).astype(np.float32)
    col = (t % GRID_W).astype(np.float32)
    nf = 16
    inv = (10000.0 ** (-np.arange(nf, dtype=np.float32) / nf)).astype(np.float32)
    ang = np.concatenate([row[:, None] * inv, col[:, None] * inv], axis=-1).astype(np.float32)
    idx = np.arange(128) % 32
    c = np.cos(ang).astype(np.float32)[:, idx].T
    s = np.sin(ang).astype(np.float32)[:, idx].T
    return np.ascontiguousarray(c), np.ascontiguousarray(s)


_NC_CACHE = {}


def kernel(**inputs):
    NQ = SEQ
    key = ("main", NQ)
    if key not in _NC_CACHE:
        _NC_CACHE[key] = Kern(NQ).build()
    nc = _NC_CACHE[key]
    f = lambda a: np.ascontiguousarray(np.asarray(a, dtype=np.float32))
    rc, rs = rope_tables(0, NQ)
    shared = {k: f(inputs[k]) for k in ("w_ada", "b_ada", "norm_g", "ffn1_w_gu", "ffn1_w_down", "ffn2_w_gu", "ffn2_w_down",
                                        "w_in", "w_out", "conv_w", "conv_b", "lru_w_r", "lru_b_r", "lru_w_i", "lru_b_i",
                                        "lru_lambda", "swa_sink", "diff_lambda", "diff_subln_g", "final_g")}
    x = f(inputs["x"])
    ctx = f(inputs["ctx"])
    c = f(inputs["c"])
    cc = f(inputs["c_ctx"])
    busy = {0: 0, 1: 1, 4: 2, 5: 3}
    b_ada = shared.pop("b_ada")
    in_maps = []
    for core in range(8):
        m = dict(shared)
        if core in busy:
            b = busy[core]
            m["x"] = x[b]
            m["ctx"] = ctx[b]
            m["cvec"] = np.ascontiguousarray(np.stack([c[b], cc], axis=0))
            m["b_ada"] = b_ada
        else:
            m["x"] = np.zeros_like(x[0])
            m["ctx"] = np.zeros_like(ctx[0])
            m["cvec"] = np.zeros((2, D), np.float32)
            m["b_ada"] = np.zeros_like(b_ada)
        m["ropec"] = rc
        m["ropes"] = rs
        in_maps.append(m)
    res = run_bass_kernel_spmd(nc, in_maps, core_ids=list(range(8)))
    cores = {b: core for core, b in busy.items()}
    out = np.stack([np.asarray(res.results[cores[b]]["out"], dtype=np.float32) for b in range(4)], axis=0)
    return out
```

```python
import math
import numpy as np
import concourse.bass as bass
import concourse.mybir as mybir
from concourse.bass_utils import run_bass_kernel_spmd
from contextlib import ExitStack

F32 = mybir.dt.float32
BF16 = mybir.dt.bfloat16
AF = mybir.ActivationFunctionType
ALU = mybir.AluOpType

D = 1024
NCH = 8
DFF = 2816
NJ = 22
SEQ = 4096
NCTX = 256
DEPTH = 2
EPS = 1e-6
NMOD = 9
GRID_W = 64
O_LX, O_LG, O_SQ, O_SK, O_SV, O_DQ, O_DK, O_DV = 0, 256, 512, 768, 896, 1024, 1536, 2048
INW = 2560
GELU_K = 2.0 * math.sqrt(2.0 / math.pi)


class Buf:
    __slots__ = ("name", "w", "r", "t")

    def __init__(self, name, t=None):
        self.name = name
        self.w = None
        self.r = {}
        self.t = t

    def __getitem__(self, k):
        return self.t[k]


class Sched:
    ENG = ("pe", "act", "dve", "pool", "sp")

    def __init__(self, nc, es):
        self.nc = nc
        self.es = es
        self.E = {"pe": nc.tensor, "act": nc.scalar, "dve": nc.vector, "pool": nc.gpsimd, "sp": nc.sync}
        self.sems = {}
        self.cnt = {}
        self.seen = {e: {} for e in self.ENG}
        for e in self.ENG:
            self.sems[e] = es.enter_context(nc.semaphore("sem_" + e))
            self.cnt[e] = 0
        self.dsem = {}
        self.nwait = 0
        self.nins = 0

    def sb(self, name, shape, dt, es=None):
        self.uid = getattr(self, "uid", 0) + 1
        t = (es or self.es).enter_context(self.nc.sbuf_tensor(f"{name}_{self.uid}", list(shape), dt))
        return Buf(name, t)

    def ps(self, name, shape, dt=F32, es=None):
        t = (es or self.es).enter_context(self.nc.psum_tensor(name, list(shape), dt))
        return Buf(name, t)

    def _wait(self, eng, ev):
        if ev is None:
            return
        key, val = ev
        if self.seen[eng].get(key, 0) >= val:
            return
        self.E[eng].wait_ge(self.sems[key], val)
        self.seen[eng][key] = val
        self.nwait += 1

    def _deps(self, eng, reads, writes, skip_self=False):
        for b in reads:
            if b.w is not None:
                self._wait(eng, b.w)
        for b in writes:
            if b.w is not None and not (skip_self and b.w[0] == eng):
                self._wait(eng, b.w)
            for k, v in b.r.items():
                if not (skip_self and k == eng):
                    self._wait(eng, (k, v))

    def _mark(self, ev, reads, writes):
        k, v = ev
        for b in reads:
            if b.r.get(k, 0) < v:
                b.r[k] = v
        for b in writes:
            b.w = ev
            b.r = {}

    def op(self, eng, fn, reads=(), writes=(), skip_self=False):
        self._deps(eng, reads, writes, skip_self)
        ins = fn(self.E[eng])
        self.cnt[eng] += 1
        ins.then_inc(self.sems[eng], 1)
        self._mark((eng, self.cnt[eng]), reads, writes)
        self.nins += 1
        return ins

    def mm(self, fns, reads=(), writes=()):
        eng = "pe"
        self._deps(eng, reads, writes, skip_self=True)
        ins = None
        for fn in fns:
            ins = fn(self.E[eng])
            self.nins += 1
        self.cnt[eng] += 1
        ins.then_inc(self.sems[eng], 1)
        self._mark((eng, self.cnt[eng]), reads, writes)
        return ins

    def dma(self, q, out, in_, reads=(), writes=(), sembuf=None, **kw):
        self._deps(q, reads, writes)
        name = sembuf.name
        if name not in self.dsem:
            sem = self.es.enter_context(self.nc.semaphore("ds_" + name))
            self.dsem[name] = [sem, 0]
            self.sems[("d", name)] = sem
        ent = self.dsem[name]
        ins = self.E[q].dma_start(out=out, in_=in_, **kw)
        ins.then_inc(ent[0], 16)
        ent[1] += 16
        self._mark((("d", name), ent[1]), reads, writes)
        self.nins += 1
        return ins

    def dma_group(self, q, pairs, reads=(), writes=(), sembuf=None, **kw):
        self._deps(q, reads, writes)
        name = sembuf.name
        if name not in self.dsem:
            sem = self.es.enter_context(self.nc.semaphore("ds_" + name))
            self.dsem[name] = [sem, 0]
            self.sems[("d", name)] = sem
        ent = self.dsem[name]
        for (o, i) in pairs:
            self.E[q].dma_start(out=o, in_=i, **kw).then_inc(ent[0], 16)
            ent[1] += 16
            self.nins += 1
        self._mark((("d", name), ent[1]), reads, writes)

    def barrier(self):
        for e in self.ENG:
            for e2 in self.ENG:
                if e2 != e and self.cnt[e2] > 0:
                    self._wait(e, (e2, self.cnt[e2]))
            for name, ent in self.dsem.items():
                if ent[1] > 0:
                    self._wait(e, (("d", name), ent[1]))


class Kern:
    def __init__(self, NQ, debug=None):
        self.NQ = NQ
        self.T = NQ + NCTX
        self.TK = SEQ + NCTX
        self.debug = debug or set()
        self.tiles = [(i * 512, 512, 0) for i in range(NQ // 512)] + [(NQ, NCTX, 1)]
        self.supers = []
        cur = []
        tot = 0
        for t in self.tiles:
            if tot + t[1] > 2304:
                self.supers.append(cur)
                cur, tot = [], 0
            cur.append(t)
            tot += t[1]
        self.supers.append(cur)
        self.STMAX = max(sum(t[1] for t in s) for s in self.supers)

    def build(self):
        nc = bass.Bass("TRN2", target_bir_lowering=False)
        self.nc = nc
        NQ, T, TK = self.NQ, self.T, self.TK

        def inp(name, shape, dt=F32):
            return nc.dram_tensor(name, list(shape), dt, kind="ExternalInput").ap()

        def scratch(name, shape, dt):
            return nc.dram_tensor(name, list(shape), dt, kind="Internal").ap()

        self.x = inp("x", [NQ, D])
        self.ctx = inp("ctx", [NCTX, D])
        self.cvec = inp("cvec", [2, D])
        self.ropec = inp("ropec", [128, NQ])
        self.ropes = inp("ropes", [128, NQ])
        self.w_ada = inp("w_ada", [DEPTH, D, NMOD * D])
        self.b_ada = inp("b_ada", [DEPTH, NMOD * D])
        self.norm_g = inp("norm_g", [DEPTH, 3, D])
        self.ffn_w_gu = [inp("ffn1_w_gu", [DEPTH, D, 2 * DFF]), inp("ffn2_w_gu", [DEPTH, D, 2 * DFF])]
        self.ffn_w_down = [inp("ffn1_w_down", [DEPTH, DFF, D]), inp("ffn2_w_down", [DEPTH, DFF, D])]
        self.w_in = inp("w_in", [DEPTH, D, INW])
        self.w_out = inp("w_out", [DEPTH, D, D])
        self.conv_w = inp("conv_w", [DEPTH, 4, 256])
        self.conv_b = inp("conv_b", [DEPTH, 256])
        self.lru_w_r = inp("lru_w_r", [DEPTH, 2, 4, 64, 64])
        self.lru_b_r = inp("lru_b_r", [DEPTH, 2, 256])
        self.lru_w_i = inp("lru_w_i", [DEPTH, 2, 4, 64, 64])
        self.lru_b_i = inp("lru_b_i", [DEPTH, 2, 256])
        self.lru_lambda = inp("lru_lambda", [DEPTH, 2, 256])
        self.swa_sink = inp("swa_sink", [DEPTH, 4])
        self.diff_lambda = inp("diff_lambda", [DEPTH, 4, 64])
        self.diff_subln_g = inp("diff_subln_g", [DEPTH, 128])
        self.final_g = inp("final_g", [D])
        self.stopping = any(k.startswith("stop_") for k in self.debug)
        self.out = nc.dram_tensor("out", [128 if self.stopping else NQ, D], F32, kind="ExternalOutput").ap()

        self.hs = scratch("hs", [NCH, 128, T], F32)
        self.lxs = scratch("lxs", [2, 128, TK], F32)
        self.lgs = scratch("lgs", [2, 128, T], F32)
        self.sqs = scratch("sqs", [2, 128, T], BF16)
        self.sks = scratch("sks", [2, 128, TK], BF16)
        self.dqs = scratch("dqs", [4, 128, T], BF16)
        self.dks = scratch("dks", [4, 128, TK], BF16)
        self.svs = scratch("svs", [TK, 128], BF16)
        self.dvs = scratch("dvs", [TK, 512], BF16)
        self.yms = scratch("yms", [NCH, 128, T], BF16)

        with ExitStack() as es:
            self.es = es
            S = Sched(nc, es)
            self.S = S
            self.PB = [S.ps(f"pb{i}", [128, 512], F32) for i in range(8)]
            self.consts()
            self.mods()
            self.phase0()
            if "modT" in self.debug:
                md = nc.dram_tensor("modT_o", [128, DEPTH * 72 * 2], F32, kind="ExternalOutput").ap()
                S.dma("sp", md[:, :], self.modT[:].rearrange("p l i j -> p (l i j)"), reads=[self.modT], sembuf=self.modT)
                S.barrier()
            for l in range(DEPTH):
                if "stop_p0" in self.debug:
                    break
                last = l == DEPTH - 1
                self.ffn(l, 0, self.tiles)
                if "h1" in self.debug and l == 0:
                    self.dump_ct("h1", self.hs, F32)
                if "stop_ffn1" in self.debug and l == 0:
                    break
                self.proj(l)
                if "proj" in self.debug and l == 0:
                    for nm, tn, dt in (("lxs", self.lxs, F32), ("lgs", self.lgs, F32), ("sqs", self.sqs, BF16), ("sks", self.sks, BF16), ("dqs", self.dqs, BF16), ("dks", self.dks, BF16)):
                        self.dump_ct(nm, tn, dt)
                    self.dump_tm("svs", self.svs, BF16)
                    self.dump_tm("dvs", self.dvs, BF16)
                if "stop_proj" in self.debug and l == 0:
                    break
                with ExitStack() as esm:
                    ym = S.sb("ymix", [128, NCH, T], BF16, esm)
                    if "no_lru" not in self.debug:
                        self.lru(l, ym, not last)
                    if "no_swa" not in self.debug:
                        self.swa(l, ym, not last)
                    if "no_diff" not in self.debug:
                        self.diff(l, ym, not last)
                    S.barrier()
                    if "yms" in self.debug and l == 0:
                        S.dma("sp", self.hsv(self.yms, 0, T), ym[:, :, :], reads=[ym], sembuf=ym)
                        S.barrier()
                        self.dump_ct("yms", self.yms, BF16)
                    tl = self.tiles if not last else [t for t in self.tiles if t[2] == 0]
                    self.down(l, ym, NCH, self.w_out[l], 5, 1.0, tl, [t[0] for t in tl])
                if "h2" in self.debug and l == 0:
                    self.dump_ct("h2", self.hs, F32)
                if "stop_mix" in self.debug and l == 0:
                    break
                tl = self.tiles if not last else [t for t in self.tiles if t[2] == 0]
                self.ffn(l, 2, tl)
                if "h3" in self.debug and l == 0:
                    self.dump_ct("h3", self.hs, F32)
                if "stop_l0" in self.debug and l == 0:
                    break
            if not self.stopping:
                self.final()
            S.barrier()
            print("instructions", S.nins, "waits", S.nwait, "sems", len(S.dsem) + 5)
        return nc

    def dump_ct(self, name, ten, dt):
        S, nc = self.S, self.nc
        C = ten.shape[0]
        o = nc.dram_tensor("d_" + name, [C, 128, 512], dt, kind="ExternalOutput").ap()
        b = Buf("dbgdump")
        S.dma_group("sp", [(o[:, :, d0:d0 + n], ten[:, :, a:a + n]) for (a, n, d0) in ((0, 128, 0), (self.NQ - 128, 128, 128), (self.NQ, 256, 256))], sembuf=b)
        S.barrier()

    def dump_tm(self, name, ten, dt):
        S, nc = self.S, self.nc
        E = ten.shape[1]
        o = nc.dram_tensor("d_" + name, [512, E], dt, kind="ExternalOutput").ap()
        b = Buf("dbgdump")
        S.dma_group("sp", [(o[d0:d0 + n, :], ten[a:a + n, :]) for (a, n, d0) in ((0, 128, 0), (SEQ - 128, 128, 128), (SEQ, 256, 256))], sembuf=b)
        S.barrier()

    def hsv(self, ten, off, n):
        return ten[:, :, off:off + n].rearrange("c p t -> p c t")

    def consts(self):
        S, nc, es = self.S, self.nc, self.es
        K = dict(allow_slow_non_contiguous=True)
        self.cbuf = Buf("cbuf")

        cbufs = []

        def cl(name, shape, src, dt=F32):
            b = S.sb(name, shape, dt)
            S.dma("sp", b[:], src, sembuf=self.cbuf, **K)
            cbufs.append(b)
            return b

        def cs(name, shape):
            return S.sb(name, shape, F32)

        def ld(b, dst, src):
            S.dma("sp", dst, src, sembuf=self.cbuf, **K)
            cbufs.append(b)

        self.sT = cs("sT", [128, NCH, 2])
        for j in range(2):
            ld(self.sT, self.sT[:, :, j], self.cvec[j].rearrange("(c p) -> p c", p=128))
        self.bT = cs("bT", [128, DEPTH, 72])
        self.NG = cs("NG", [128, DEPTH, 3, NCH])
        self.CW = cs("CW", [128, DEPTH, 4, 2])
        self.CB = cs("CB", [128, DEPTH, 2])
        self.LBR = cs("LBR", [128, DEPTH, 2, 2])
        self.LBI = cs("LBI", [128, DEPTH, 2, 2])
        self.LAM = cs("LAM", [128, DEPTH, 2, 2])
        self.SUBG = cs("SUBG", [128, DEPTH])
        for l in range(DEPTH):
            ld(self.bT, self.bT[:, l, :], self.b_ada[l].rearrange("(i p) -> p i", p=128))
            for k in range(3):
                ld(self.NG, self.NG[:, l, k, :], self.norm_g[l, k].rearrange("(c p) -> p c", p=128))
            for k in range(4):
                ld(self.CW, self.CW[:, l, k, :], self.conv_w[l, k].rearrange("(i p) -> p i", p=128))
            ld(self.CB, self.CB[:, l, :], self.conv_b[l].rearrange("(i p) -> p i", p=128))
            for d in range(2):
                ld(self.LBR, self.LBR[:, l, d, :], self.lru_b_r[l, d].rearrange("(i p) -> p i", p=128))
                ld(self.LBI, self.LBI[:, l, d, :], self.lru_b_i[l, d].rearrange("(i p) -> p i", p=128))
                ld(self.LAM, self.LAM[:, l, d, :], self.lru_lambda[l, d].rearrange("(i p) -> p i", p=128))
            ld(self.SUBG, self.SUBG[:, l:l + 1], self.diff_subln_g[l].rearrange("(p o) -> p o", o=1))
        self.SINK = cl("SINK", [128, DEPTH * 4], self.swa_sink.rearrange("l h -> (l h)").partition_broadcast(128))
        self.DLAM = cl("DLAM", [128, DEPTH * 4 * 64], self.diff_lambda.rearrange("l a d -> (l a d)").partition_broadcast(128))
        self.FG = cl("FG", [128, D], self.final_g.partition_broadcast(128))
        for b in cbufs:
            b.w = (("d", "cbuf"), S.dsem["cbuf"][1])

        self.ident = S.sb("ident", [128, 128], F32)
        S.op("pool", lambda e: e.memset(self.ident[:], 0.0), writes=[self.ident])
        S.op("pool", lambda e: e.affine_select(out=self.ident[:], in_=self.ident[:], pattern=[[-1, 128]], compare_op=ALU.not_equal, fill=1.0, base=0, channel_multiplier=1), reads=[self.ident], writes=[self.ident])
        self.ones_b = S.sb("ones_b", [128, 128], BF16)
        S.op("pool", lambda e: e.memset(self.ones_b[:], 1.0), writes=[self.ones_b])
        self.ones_f = S.sb("ones_f", [128, 128], F32)
        S.op("pool", lambda e: e.memset(self.ones_f[:], 1.0), writes=[self.ones_f])
        self.onesL = S.sb("onesL", [128, 128], BF16)
        self.onesR = S.sb("onesR", [128, 128], BF16)
        S.op("pool", lambda e: e.memset(self.onesL[:], 0.0), writes=[self.onesL])
        S.op("pool", lambda e: e.memset(self.onesL[:, 0:64], 1.0), writes=[self.onesL])
        S.op("pool", lambda e: e.memset(self.onesR[:], 0.0), writes=[self.onesR])
        S.op("pool", lambda e: e.memset(self.onesR[:, 64:128], 1.0), writes=[self.onesR])
        rt = S.sb("rm_tmp", [128, 2, 128], F32)
        S.op("pool", lambda e: e.memset(rt[:], 0.0), writes=[rt])
        S.op("pool", lambda e: e.affine_select(out=rt[:, 0, :], in_=rt[:, 0, :], pattern=[[-1, 128]], compare_op=ALU.not_equal, fill=1.0, base=32, channel_multiplier=1), reads=[rt], writes=[rt])
        S.op("pool", lambda e: e.affine_select(out=rt[:, 1, :], in_=rt[:, 1, :], pattern=[[-1, 128]], compare_op=ALU.not_equal, fill=-1.0, base=-32, channel_multiplier=1), reads=[rt], writes=[rt])
        for a in (0, 64):
            S.op("pool", lambda e: e.memset(rt[:, 0, a:a + 32], 0.0), reads=[rt], writes=[rt])
            S.op("pool", lambda e: e.memset(rt[:, 1, a + 32:a + 64], 0.0), reads=[rt], writes=[rt])
        self.Rm = S.sb("Rm", [128, 128], BF16)
        S.op("pool", lambda e: e.tensor_tensor(out=self.Rm[:], in0=rt[:, 0, :], in1=rt[:, 1, :], op=ALU.add), reads=[rt], writes=[self.Rm])
        mt = S.sb("mask_tmp", [128, 2, 128], F32)
        S.op("pool", lambda e: e.memset(mt[:], 1.0), writes=[mt])
        S.op("pool", lambda e: e.affine_select(out=mt[:, 0, :], in_=mt[:, 0, :], pattern=[[-1, 128]], compare_op=ALU.is_ge, fill=0.0, base=0, channel_multiplier=1), reads=[mt], writes=[mt])
        S.op("pool", lambda e: e.affine_select(out=mt[:, 1, :], in_=mt[:, 1, :], pattern=[[1, 128]], compare_op=ALU.is_ge, fill=0.0, base=0, channel_multiplier=-1), reads=[mt], writes=[mt])
        self.masks = S.sb("masks", [128, 2, 128], BF16)
        S.op("pool", lambda e: e.tensor_copy(out=self.masks[:], in_=mt[:]), reads=[mt], writes=[self.masks])
        self.eps_t = S.sb("eps_t", [128, 1], F32)
        S.op("pool", lambda e: e.memset(self.eps_t[:], EPS), writes=[self.eps_t])
        self.one_t = S.sb("one_t", [128, 1], F32)
        S.op("pool", lambda e: e.memset(self.one_t[:], 1.0), writes=[self.one_t])

    def mods(self):
        S = self.S
        S.op("act", lambda e: e.activation(out=self.sT[:], in_=self.sT[:], func=AF.Silu), reads=[self.sT], writes=[self.sT])
        self.modT = S.sb("modT", [128, DEPTH, 72, 2], F32)
        self.GS = S.sb("GS", [128, DEPTH, 3, NCH, 2], F32)
        with ExitStack() as esl:
            wa = [S.sb(f"wa{i}", [128, NCH, 512], F32, esl) for i in range(2)]
            for l in range(DEPTH):
                pm = self.PB[l]
                pmv = pm[:, 0:144].rearrange("p (i j) -> p i j", j=2)
                for cb in range(18):
                    w = wa[cb % 2]
                    S.dma("sp", w[:], self.w_ada[l][:, cb * 512:(cb + 1) * 512].rearrange("(c p) n -> p c n", p=128), writes=[w], sembuf=w)
                    fns = []
                    for sub in range(4):
                        idx = cb * 4 + sub
                        for c in range(NCH):
                            fns.append(lambda e, idx=idx, sub=sub, c=c: e.matmul(pmv[:, idx, :], w[:, c, sub * 128:(sub + 1) * 128], self.sT[:, c, :], start=(c == 0), stop=(c == NCH - 1)))
                    S.mm(fns, reads=[w, self.sT], writes=[pm])
                S.op("dve", lambda e: e.tensor_tensor(out=self.modT[:, l], in0=pmv, in1=self.bT[:, l, :].unsqueeze(2).to_broadcast([128, 72, 2]), op=ALU.add), reads=[pm, self.bT], writes=[self.modT])
                for k in range(3):
                    sc = self.modT[:, l, (3 * k + 1) * 8:(3 * k + 2) * 8, :]
                    S.op("dve", lambda e: e.scalar_tensor_tensor(out=self.GS[:, l, k], in0=sc, scalar=1.0, in1=self.NG[:, l, k, :].unsqueeze(2).to_broadcast([128, NCH, 2]), op0=ALU.add, op1=ALU.mult), reads=[self.modT, self.NG], writes=[self.GS])
            S.barrier()

    def mod_ap(self, l, m, c, j):
        return self.modT[:, l, m * 8 + c, j:j + 1]

    def phase0(self):
        S = self.S
        with ExitStack() as esl:
            xin = [S.sb(f"xin{i}", [128, D], F32, esl) for i in range(3)]
            hst = [S.sb(f"hst{i}", [128, NCH, 512], F32, esl) for i in range(2)]
            cnt = 0
            for ti, (off, n, j) in enumerate(self.tiles):
                hb = hst[ti % 2]
                for sub in range(n // 128):
                    xb = xin[cnt % 3]
                    src = self.x[off + sub * 128: off + (sub + 1) * 128, :] if j == 0 else self.ctx[sub * 128:(sub + 1) * 128, :]
                    S.dma("sp", xb[:], src, writes=[xb], sembuf=xb)
                    for half in range(2):
                        pt = self.PB[(cnt * 2 + half) % 8]
                        ptv = pt[:, :].rearrange("p (c t) -> p c t", c=4)
                        S.mm([lambda e, cc=cc: e.transpose(ptv[:, cc, :], xb[:, (half * 4 + cc) * 128:(half * 4 + cc + 1) * 128], self.ident[:]) for cc in range(4)], reads=[xb, self.ident], writes=[pt])
                        eng = "act" if half == 0 else "dve"
                        dst = hb[:, half * 4:(half + 1) * 4, sub * 128:(sub + 1) * 128]
                        if eng == "act":
                            S.op("act", lambda e: e.activation(out=dst, in_=ptv, func=AF.Copy), reads=[pt], writes=[hb])
                        else:
                            S.op("dve", lambda e: e.tensor_copy(out=dst, in_=ptv), reads=[pt], writes=[hb])
                    cnt += 1
                S.dma("sp", self.hsv(self.hs, off, n), hb[:, :, :n], reads=[hb], sembuf=hb)
            S.barrier()

    def norm_pass(self, l, k, tiles, offs, hn, hn_b, bufs=None, barrier=True):
        S = self.S
        with ExitStack() as esl:
            if bufs is None:
                hbs = [S.sb(f"hb{i}", [128, NCH, 512], F32, esl) for i in range(2)]
                sq = S.sb("sq", [128, NCH, 512], BF16, esl)
                rs = [S.sb(f"rs{i}", [128, 512], F32, esl) for i in range(2)]
            else:
                hbs, sq, rs = bufs
            for ti, (off, n, j) in enumerate(tiles):
                b = hbs[ti % 2]
                r = rs[ti % 2]
                S.dma("sp", b[:, :, :n], self.hsv(self.hs, off, n), writes=[b], sembuf=b)
                S.op("act", lambda e: e.activation(out=sq[:, :, :n], in_=b[:, :, :n], func=AF.Square), reads=[b], writes=[sq])
                ss = self.PB[4 + ti % 2]
                S.mm([lambda e, c=c: e.matmul(ss[:, :n], self.ones_b[:], sq[:, c, :n], start=(c == 0), stop=(c == NCH - 1)) for c in range(NCH)], reads=[sq, self.ones_b], writes=[ss])
                S.op("act", lambda e: e.activation(out=r[:, :n], in_=ss[:, :n], func=AF.Sqrt, scale=1.0 / D, bias=self.eps_t[:]), reads=[ss, self.eps_t], writes=[r])
                S.op("dve", lambda e: e.reciprocal(out=r[:, :n], in_=r[:, :n]), reads=[r], writes=[r])
                S.op("dve", lambda e: e.tensor_tensor(out=b[:, :, :n], in0=b[:, :, :n], in1=r[:, :n].unsqueeze(1).to_broadcast([128, NCH, n]), op=ALU.mult), reads=[b, r], writes=[b])
                o = offs[ti]
                for c in range(NCH):
                    S.op("act", lambda e, c=c: e.activation(out=hn[:, c, o:o + n], in_=b[:, c, :n], func=AF.Identity, scale=self.GS[:, l, k, c, j:j + 1], bias=self.mod_ap(l, 3 * k, c, j)), reads=[b, self.GS, self.modT], writes=[hn_b[ti]], skip_self=(c > 0))
            if barrier:
                S.barrier()

    def norm_steps(self, l, k, tiles, offs, hn, hn_b, bufs):
        S = self.S
        nhb, sq, rs = bufs
        steps = []
        cnt = [0]
        for ti, (off, n, j) in enumerate(tiles):
            for sub in range(n // 256):
                def step(ti=ti, off=off, j=j, sub=sub):
                    i = cnt[0]
                    cnt[0] += 1
                    b = nhb[i % 2]
                    r = rs[i % 2]
                    m = 256
                    a0 = off + sub * 256
                    S.dma("act", b[:, :, :], self.hsv(self.hs, a0, m), writes=[b], sembuf=b)
                    S.op("pool", lambda e: e.tensor_tensor(out=sq[:, :, :], in0=b[:, :, :], in1=b[:, :, :], op=ALU.mult), reads=[b], writes=[sq])
                    ss = self.PB[i % 2]
                    S.mm([lambda e, c=c: e.matmul(ss[:, :m], self.ones_b[:], sq[:, c, :], start=(c == 0), stop=(c == NCH - 1)) for c in range(NCH)], reads=[sq, self.ones_b], writes=[ss])
                    S.op("act", lambda e: e.activation(out=r[:, :], in_=ss[:, :m], func=AF.Sqrt, scale=1.0 / D, bias=self.eps_t[:]), reads=[ss, self.eps_t], writes=[r])
                    S.op("dve", lambda e: e.reciprocal(out=r[:, :], in_=r[:, :]), reads=[r], writes=[r])
                    S.op("dve", lambda e: e.tensor_tensor(out=b[:, :, :], in0=b[:, :, :], in1=r[:, :].unsqueeze(1).to_broadcast([128, NCH, m]), op=ALU.mult), reads=[b, r], writes=[b])
                    o = offs[ti] + sub * 256
                    for c in range(NCH):
                        S.op("act", lambda e, c=c: e.activation(out=hn[:, c, o:o + m], in_=b[:, c, :], func=AF.Identity, scale=self.GS[:, l, k, c, j:j + 1], bias=self.mod_ap(l, 3 * k, c, j)), reads=[b, self.GS, self.modT], writes=[hn_b[ti]], skip_self=(c > 0 or sub > 0))
                steps.append(step)
        return steps

    def down(self, l, act, nk, wdram, gate_m, coef, tiles, offs, act_b=None):
        S = self.S
        with ExitStack() as esl:
            wd = [S.sb(f"wd{i}", [128, nk, 512], BF16, esl) for i in range(2)]
            hbs = [S.sb(f"hbd{i}", [128, NCH, 512], F32, esl) for i in range(2)]
            gt = S.sb("gt", [128, NCH, 2], F32, esl)
            S.op("pool", lambda e: e.tensor_scalar(out=gt[:], in0=self.modT[:, l, gate_m * 8:(gate_m + 1) * 8, :], scalar1=float(coef), scalar2=0.0, op0=ALU.mult, op1=ALU.add), reads=[self.modT], writes=[gt])
            for half in range(2):
                S.dma("pool", wd[half][:], wdram[:, half * 512:(half + 1) * 512].rearrange("(j p) n -> p j n", p=128), writes=[wd[half]], sembuf=wd[half])
            for ti, (off, n, j) in enumerate(tiles):
                b = hbs[ti % 2]
                o = offs[ti]
                rd = [act_b[ti]] if act_b is not None else [act]
                S.dma("sp", b[:, :, :n], self.hsv(self.hs, off, n), writes=[b], sembuf=b)
                for oc in range(NCH):
                    py = self.PB[4 + oc % 4]
                    w = wd[oc // 4]
                    cs = (oc % 4) * 128
                    S.mm([lambda e, jj=jj: e.matmul(py[:, :n], w[:, jj, cs:cs + 128], act[:, jj, o:o + n], start=(jj == 0), stop=(jj == nk - 1)) for jj in range(nk)], reads=[w] + rd, writes=[py])
                    S.op("dve", lambda e: e.scalar_tensor_tensor(out=b[:, oc, :n], in0=py[:, :n], scalar=gt[:, oc, j:j + 1], in1=b[:, oc, :n], op0=ALU.mult, op1=ALU.add), reads=[py, gt] + ([b] if oc == 0 else []), writes=[b], skip_self=True)
                S.dma("sp", self.hsv(self.hs, off, n), b[:, :, :n], reads=[b], sembuf=b)
            S.barrier()

    def ffn(self, l, k, tiles):
        S = self.S
        wgu = self.ffn_w_gu[0 if k == 0 else 1][l]
        wdn = self.ffn_w_down[0 if k == 0 else 1][l]
        supers = []
        cur, tot = [], 0
        for t in tiles:
            if tot + t[1] > 2304:
                supers.append(cur)
                cur, tot = [], 0
            cur.append(t)
            tot += t[1]
        supers.append(cur)
        HALVES = [(0, 11), (11, 11)]
        NA = 11
        with ExitStack() as esf:
            act = S.sb("act", [128, NA, self.STMAX], BF16, esf)
            hn = S.sb("hn", [128, NCH, self.STMAX], BF16, esf)
            hbs = [S.sb(f"hb{i}", [128, NCH, 512], F32, esf) for i in range(2)]
            nhb = [S.sb(f"nhb{i}", [128, NCH, 256], F32, esf) for i in range(2)]
            sq = S.sb("sq", [128, NCH, 256], BF16, esf)
            rs = [S.sb(f"rs{i}", [128, 256], F32, esf) for i in range(2)]
            wg = [S.sb(f"wg{i}", [128, NCH, 256], BF16, esf) for i in range(2)]
            wu = [S.sb(f"wu{i}", [128, NCH, 256], BF16, esf) for i in range(2)]
            sg = [S.sb(f"sg{i}", [128, 512], BF16, esf) for i in range(2)]
            wdh = [S.sb(f"wd{i}", [128, NA, 512], BF16, esf) for i in range(2)]
            gt = S.sb("gt", [128, NCH, 2], F32, esf)
            S.op("pool", lambda e: e.tensor_scalar(out=gt[:], in0=self.modT[:, l, (3 * k + 2) * 8:(3 * k + 3) * 8, :], scalar1=0.5, scalar2=0.0, op0=ALU.mult, op1=ALU.add), reads=[self.modT], writes=[gt])
            maxt = max(len(st) for st in supers)
            hn_b = [Buf(f"hn_b{i}") for i in range(maxt)]
            act_b = [Buf(f"act_b{i}") for i in range(maxt)]
            hsd = {t[0]: Buf(f"hsd{t[0]}") for t in tiles}
            cnt = 0
            grp = 0
            hcnt = 0

            def offs_of(st):
                offs, o = [], 0
                for t in st:
                    offs.append(o)
                    o += t[1]
                return offs

            pending = self.norm_steps(l, k, supers[0], offs_of(supers[0]), hn, hn_b, (nhb, sq, rs))
            for si, st in enumerate(supers):
                offs = offs_of(st)
                for f in pending:
                    f()
                pending = []
                nxt = self.norm_steps(l, k, supers[si + 1], offs_of(supers[si + 1]), hn, hn_b, (nhb, sq, rs)) if si + 1 < len(supers) else []
                for hi, (j0, nj) in enumerate(HALVES):
                    for gi, g0 in enumerate(range(0, nj, 2)):
                        nb = min(2, nj - g0)
                        s_ = grp % 2
                        grp += 1
                        c0 = (j0 + g0) * 128
                        S.dma("pool", wg[s_][:, :, :nb * 128], wgu[:, c0:c0 + nb * 128].rearrange("(c p) n -> p c n", p=128), writes=[wg[s_]], sembuf=wg[s_])
                        S.dma("pool", wu[s_][:, :, :nb * 128], wgu[:, DFF + c0:DFF + c0 + nb * 128].rearrange("(c p) n -> p c n", p=128), writes=[wu[s_]], sembuf=wu[s_])
                        if gi == 1:
                            for ch in range(2):
                                S.dma("pool", wdh[ch][:, :nj, :], wdn[j0 * 128:(j0 + nj) * 128, ch * 512:(ch + 1) * 512].rearrange("(j p) n -> p j n", p=128), writes=[wdh[ch]], sembuf=wdh[ch])
                        for jj in range(nb):
                            jx = g0 + jj
                            for ti, (off, n, j) in enumerate(st):
                                o = offs[ti]
                                pg = self.PB[(cnt % 2) * 2]
                                pu = self.PB[(cnt % 2) * 2 + 1]
                                sgb = sg[cnt % 2]
                                cnt += 1
                                S.mm([lambda e, c=c: e.matmul(pg[:, :n], wg[s_][:, c, jj * 128:(jj + 1) * 128], hn[:, c, o:o + n], start=(c == 0), stop=(c == NCH - 1)) for c in range(NCH)], reads=[wg[s_], hn_b[ti]], writes=[pg])
                                S.mm([lambda e, c=c: e.matmul(pu[:, :n], wu[s_][:, c, jj * 128:(jj + 1) * 128], hn[:, c, o:o + n], start=(c == 0), stop=(c == NCH - 1)) for c in range(NCH)], reads=[wu[s_], hn_b[ti]], writes=[pu])
                                S.op("act", lambda e: e.activation(out=sgb[:, :n], in_=pg[:, :n], func=AF.Silu), reads=[pg], writes=[sgb])
                                S.op("dve", lambda e: e.tensor_tensor(out=act[:, jx, o:o + n], in0=sgb[:, :n], in1=pu[:, :n], op=ALU.mult), reads=[sgb, pu], writes=[act_b[ti]], skip_self=True)
                    last_half = hi == len(HALVES) - 1
                    per_tile = -(-len(nxt) // len(st)) if (last_half and nxt) else 0
                    for ti, (off, n, j) in enumerate(st):
                        o = offs[ti]
                        b = hbs[hcnt % 2]
                        hcnt += 1
                        S.dma("sp", b[:, :, :n], self.hsv(self.hs, off, n), reads=[hsd[off]], writes=[b], sembuf=b)
                        issued = 0
                        for oc in range(NCH):
                            py = self.PB[4 + oc % 4]
                            w = wdh[oc // 4]
                            cs = (oc % 4) * 128
                            S.mm([lambda e, jj=jj: e.matmul(py[:, :n], w[:, jj, cs:cs + 128], act[:, jj, o:o + n], start=(jj == 0), stop=(jj == nj - 1)) for jj in range(nj)], reads=[w, act_b[ti]], writes=[py])
                            S.op("dve", lambda e: e.scalar_tensor_tensor(out=b[:, oc, :n], in0=py[:, :n], scalar=gt[:, oc, j:j + 1], in1=b[:, oc, :n], op0=ALU.mult, op1=ALU.add), reads=[py, gt] + ([b] if oc == 0 else []), writes=[b], skip_self=True)
                            if last_half and nxt and issued < per_tile and oc % 2 == 1:
                                nxt.pop(0)()
                                issued += 1
                        S.dma("sp", self.hsv(self.hs, off, n), b[:, :, :n], reads=[b], writes=[hsd[off]], sembuf=b)
                pending = nxt
            S.barrier()

    def proj(self, l):
        S = self.S
        NQ = self.NQ
        for st in self.supers:
            offs = []
            o = 0
            for t in st:
                offs.append(o)
                o += t[1]
            with ExitStack() as esa:
                hn = S.sb("hn", [128, NCH, self.STMAX], BF16, esa)
                hn_b = [Buf(f"hn_b{i}") for i in range(len(st))]
                self.norm_pass(l, 1, st, offs, hn, hn_b)
                win = S.sb("win", [128, NCH, INW], BF16, esa)
                wkd = S.sb("wkd", [128, NCH, 256], BF16, esa)
                S.dma_group("pool", [(win[:, :, i * 512:(i + 1) * 512], self.w_in[l][:, i * 512:(i + 1) * 512].rearrange("(c p) n -> p c n", p=128)) for i in range(5)], writes=[win], sembuf=win)
                S.dma_group("pool", [(wkd[:, :, g * 128 + r * 64:g * 128 + r * 64 + 64], self.w_in[l][:, O_SK + 64 * g:O_SK + 64 * g + 64].rearrange("(c p) n -> p c n", p=128)) for g in range(2) for r in range(2)], writes=[wkd], sembuf=wkd)
                ct = [S.sb(f"ropc{i}", [128, 512], F32, esa) for i in range(2)]
                stt = [S.sb(f"rops{i}", [128, 512], F32, esa) for i in range(2)]
                f32st = [S.sb(f"pst{i}", [128, 512], F32, esa) for i in range(3)]
                bfst = [S.sb(f"pbs{i}", [128, 512], BF16, esa) for i in range(4)]
                qb = [S.sb(f"qb{i}", [128, 512], BF16, esa) for i in range(2)]
                t1 = [S.sb(f"rt1_{i}", [128, 512], F32, esa) for i in range(2)]
                t2 = [S.sb(f"rt2_{i}", [128, 512], F32, esa) for i in range(2)]
                vst = [S.sb(f"vst{i}", [128, 640], BF16, esa) for i in range(2)]
                c32 = cbf = crp = cv = 0
                chunks = []
                for i in range(2):
                    chunks.append(("f32", win, O_LX + 128 * i, self.lxs, i, True))
                    chunks.append(("f32", win, O_LG + 128 * i, self.lgs, i, False))
                for g in range(2):
                    chunks.append(("rope", win, O_SQ + 128 * g, self.sqs, g, False))
                    chunks.append(("rope", wkd, 128 * g, self.sks, g, True))
                for h in range(4):
                    chunks.append(("rope", win, O_DQ + 128 * h, self.dqs, h, False))
                    chunks.append(("rope", win, O_DK + 128 * h, self.dks, h, True))
                for ti, (off, n, j) in enumerate(st):
                    o = offs[ti]
                    koff = off if j == 0 else SEQ
                    if j == 0:
                        rc, rsn = ct[ti % 2], stt[ti % 2]
                        S.dma("sp", rc[:, :n], self.ropec[:, off:off + n], writes=[rc], sembuf=rc)
                        S.dma("sp", rsn[:, :n], self.ropes[:, off:off + n], writes=[rsn], sembuf=rsn)
                    def pmm(ci):
                        kind, wt, c0, dst, di, keysp = chunks[ci]
                        pp = self.PB[ci % 4]
                        S.mm([lambda e, c=c: e.matmul(pp[:, :n], wt[:, c, c0:c0 + 128], hn[:, c, o:o + n], start=(c == 0), stop=(c == NCH - 1)) for c in range(NCH)], reads=[wt, hn_b[ti]], writes=[pp])

                    pmm(0)
                    for ci, (kind, wt, c0, dst, di, keysp) in enumerate(chunks):
                        pp = self.PB[ci % 4]
                        if ci + 1 < len(chunks):
                            pmm(ci + 1)
                        dcol = koff if keysp else off
                        if kind == "f32":
                            sb_ = f32st[c32 % 3]
                            c32 += 1
                            S.op("act", lambda e: e.activation(out=sb_[:, :n], in_=pp[:, :n], func=AF.Copy), reads=[pp], writes=[sb_])
                            S.dma("sp", dst[di, :, dcol:dcol + n], sb_[:, :n], reads=[sb_], sembuf=sb_)
                        elif j == 1:
                            sb_ = bfst[cbf % 4]
                            cbf += 1
                            S.op("act", lambda e: e.activation(out=sb_[:, :n], in_=pp[:, :n], func=AF.Copy), reads=[pp], writes=[sb_])
                            S.dma("sp", dst[di, :, dcol:dcol + n], sb_[:, :n], reads=[sb_], sembuf=sb_)
                        else:
                            q_ = qb[crp % 2]
                            a1 = t1[crp % 2]
                            a2 = t2[crp % 2]
                            pr = self.PB[4 + crp % 2]
                            crp += 1
                            sb_ = bfst[cbf % 4]
                            cbf += 1
                            S.op("act", lambda e: e.activation(out=q_[:, :n], in_=pp[:, :n], func=AF.Copy), reads=[pp], writes=[q_])
                            S.mm([lambda e: e.matmul(pr[:, :n], self.Rm[:], q_[:, :n], start=True, stop=True)], reads=[self.Rm, q_], writes=[pr])
                            S.op("pool", lambda e: e.tensor_tensor(out=a1[:, :n], in0=q_[:, :n], in1=rc[:, :n], op=ALU.mult), reads=[q_, rc], writes=[a1])
                            S.op("dve", lambda e: e.tensor_tensor(out=a2[:, :n], in0=pr[:, :n], in1=rsn[:, :n], op=ALU.mult), reads=[pr, rsn], writes=[a2])
                            S.op("dve", lambda e: e.tensor_tensor(out=sb_[:, :n], in0=a1[:, :n], in1=a2[:, :n], op=ALU.add), reads=[a1, a2], writes=[sb_])
                            S.dma("sp", dst[di, :, dcol:dcol + n], sb_[:, :n], reads=[sb_], sembuf=sb_)
                    for sub in range(n // 128):
                        pv1 = self.PB[6]
                        pv2 = self.PB[7]
                        vs_ = vst[cv % 2]
                        cv += 1
                        S.mm([lambda e, c=c: e.matmul(pv1[:, :512], hn[:, c, o + sub * 128:o + (sub + 1) * 128], win[:, c, O_DV:O_DV + 512], start=(c == 0), stop=(c == NCH - 1)) for c in range(NCH)], reads=[win, hn_b[ti]], writes=[pv1])
                        S.mm([lambda e, c=c: e.matmul(pv2[:, :128], hn[:, c, o + sub * 128:o + (sub + 1) * 128], win[:, c, O_SV:O_SV + 128], start=(c == 0), stop=(c == NCH - 1)) for c in range(NCH)], reads=[win, hn_b[ti]], writes=[pv2])
                        S.op("act", lambda e: e.activation(out=vs_[:, 0:512], in_=pv1[:, :512], func=AF.Copy), reads=[pv1], writes=[vs_])
                        S.op("dve", lambda e: e.tensor_copy(out=vs_[:, 512:640], in_=pv2[:, :128]), reads=[pv2], writes=[vs_])
                        r0 = koff + sub * 128
                        S.dma_group("sp", [(self.dvs[r0:r0 + 128, :], vs_[:, 0:512]), (self.svs[r0:r0 + 128, :], vs_[:, 512:640])], reads=[vs_], sembuf=vs_)
                S.barrier()

    def lru(self, l, ym, with_ctx):
        S = self.S
        NQ, T, TK = self.NQ, self.T, self.TK
        K = dict(allow_slow_non_contiguous=True)
        with ExitStack() as esl:
            XH = S.sb("lru_xh", [128, TK], F32, esl)
            U = S.sb("lru_u", [128, TK], F32, esl)
            A = S.sb("lru_a", [128, TK], F32, esl)
            Bv = S.sb("lru_b", [128, TK], F32, esl)
            M = S.sb("lru_m", [128, TK], F32, esl)
            HS = S.sb("lru_hs", [128, TK], F32, esl)
            LG = S.sb("lru_lg", [128, T], F32, esl)
            WR = S.sb("lru_wr", [128, 2, 2, 128], F32, esl)
            WI = S.sb("lru_wi", [128, 2, 2, 128], F32, esl)
            c8 = S.sb("lru_c8", [128, 2, 2], F32, esl)
            c16 = S.sb("lru_c16", [128, 2, 2], F32, esl)
            S.op("pool", lambda e: e.memset(WR[:], 0.0), writes=[WR])
            S.op("pool", lambda e: e.memset(WI[:], 0.0), writes=[WI])
            idx = [(d, i, bb) for d in range(2) for i in range(2) for bb in range(2)]
            S.dma_group("sp", [(WR[bb * 64:(bb + 1) * 64, d, i, bb * 64:(bb + 1) * 64], self.lru_w_r[l, d, 2 * i + bb]) for (d, i, bb) in idx], writes=[WR], sembuf=WR)
            S.dma_group("sp", [(WI[bb * 64:(bb + 1) * 64, d, i, bb * 64:(bb + 1) * 64], self.lru_w_i[l, d, 2 * i + bb]) for (d, i, bb) in idx], writes=[WI], sembuf=WI)
            S.op("act", lambda e: e.activation(out=c8[:], in_=self.LAM[:, l], func=AF.Exp, scale=-1.0), reads=[self.LAM], writes=[c8])
            S.op("act", lambda e: e.activation(out=c8[:], in_=c8[:], func=AF.Ln, scale=1.0, bias=self.one_t[:]), reads=[c8, self.one_t], writes=[c8])
            S.op("dve", lambda e: e.tensor_scalar(out=c16[:], in0=c8[:], scalar1=-16.0, scalar2=0.0, op0=ALU.mult, op1=ALU.add), reads=[c8], writes=[c16])
            S.op("dve", lambda e: e.tensor_scalar(out=c8[:], in0=c8[:], scalar1=-8.0, scalar2=0.0, op0=ALU.mult, op1=ALU.add), reads=[c8], writes=[c8])
            segs = [(0, SEQ), (SEQ, NCTX)]
            for i in range(2):
                S.dma("sp", XH[:, :], self.lxs[i, :, :], writes=[XH], sembuf=XH)
                S.dma("sp", LG[:, :], self.lgs[i, :, :], writes=[LG], sembuf=LG)
                for (s0, sn) in segs:
                    S.op("dve", lambda e: e.tensor_scalar(out=U[:, s0:s0 + sn], in0=XH[:, s0:s0 + sn], scalar1=self.CW[:, l, 2, i:i + 1], scalar2=self.CB[:, l, i:i + 1], op0=ALU.mult, op1=ALU.add), reads=[XH, self.CW, self.CB], writes=[U])
                    for (kk, sh) in ((0, -2), (1, -1), (3, 1)):
                        if sh < 0:
                            oa, ob, ia, ib = s0 - sh, s0 + sn, s0, s0 + sn + sh
                        else:
                            oa, ob, ia, ib = s0, s0 + sn - sh, s0 + sh, s0 + sn
                        S.op("dve", lambda e: e.scalar_tensor_tensor(out=U[:, oa:ob], in0=XH[:, ia:ib], scalar=self.CW[:, l, kk, i:i + 1], in1=U[:, oa:ob], op0=ALU.mult, op1=ALU.add), reads=[XH, U, self.CW], writes=[U])
                for d in range(2):
                    nblk = (TK + 511) // 512
                    for bk in range(nblk):
                        b0 = bk * 512
                        bn = min(512, TK - b0)
                        pr = self.PB[(bk % 2) * 2]
                        pi = self.PB[(bk % 2) * 2 + 1]
                        S.mm([lambda e: e.matmul(pr[:, :bn], WR[:, d, i, :], U[:, b0:b0 + bn], start=True, stop=True)], reads=[WR, U], writes=[pr])
                        S.mm([lambda e: e.matmul(pi[:, :bn], WI[:, d, i, :], U[:, b0:b0 + bn], start=True, stop=True)], reads=[WI, U], writes=[pi])
                        S.op("act", lambda e: e.activation(out=A[:, b0:b0 + bn], in_=pr[:, :bn], func=AF.Sigmoid, bias=self.LBR[:, l, d, i:i + 1], scale=1.0), reads=[pr, self.LBR], writes=[A])
                        S.op("act", lambda e: e.activation(out=Bv[:, b0:b0 + bn], in_=pi[:, :bn], func=AF.Sigmoid, bias=self.LBI[:, l, d, i:i + 1], scale=1.0), reads=[pi, self.LBI], writes=[Bv])
                    S.op("act", lambda e: e.activation(out=M[:], in_=A[:], func=AF.Exp, scale=c16[:, d, i:i + 1]), reads=[A, c16], writes=[M])
                    S.op("act", lambda e: e.activation(out=A[:], in_=A[:], func=AF.Exp, scale=c8[:, d, i:i + 1]), reads=[A, c8], writes=[A])
                    S.op("act", lambda e: e.activation(out=M[:], in_=M[:], func=AF.Sqrt, scale=-1.0, bias=self.one_t[:]), reads=[M, self.one_t], writes=[M])
                    S.op("pool", lambda e: e.tensor_tensor(out=Bv[:], in0=Bv[:], in1=U[:], op=ALU.mult), reads=[Bv, U], writes=[Bv])
                    S.op("dve", lambda e: e.tensor_tensor(out=Bv[:], in0=Bv[:], in1=M[:], op=ALU.mult), reads=[Bv, M], writes=[Bv])
                    if d == 0:
                        S.op("dve", lambda e: e.tensor_tensor_scan(out=XH[:, SEQ:TK], data0=A[:, SEQ:TK], data1=Bv[:, SEQ:TK], initial=0.0, op0=ALU.mult, op1=ALU.add), reads=[A, Bv], writes=[XH])
                        S.op("dve", lambda e: e.tensor_tensor_scan(out=XH[:, 0:SEQ], data0=A[:, 0:SEQ], data1=Bv[:, 0:SEQ], initial=XH[:, TK - 1:TK], op0=ALU.mult, op1=ALU.add), reads=[A, Bv, XH], writes=[XH])
                        S.op("pool", lambda e: e.tensor_copy(out=HS[:], in_=XH[:]), reads=[XH], writes=[HS])
                    else:
                        S.op("dve", lambda e: e.tensor_tensor_scan(out=XH[:, SEQ:TK][:, ::-1], data0=A[:, SEQ:TK][:, ::-1], data1=Bv[:, SEQ:TK][:, ::-1], initial=0.0, op0=ALU.mult, op1=ALU.add), reads=[A, Bv], writes=[XH])
                        S.op("dve", lambda e: e.tensor_tensor_scan(out=XH[:, 0:SEQ][:, ::-1], data0=A[:, 0:SEQ][:, ::-1], data1=Bv[:, 0:SEQ][:, ::-1], initial=XH[:, SEQ:SEQ + 1], op0=ALU.mult, op1=ALU.add), reads=[A, Bv, XH], writes=[XH])
                        S.op("pool", lambda e: e.tensor_tensor(out=HS[:], in0=HS[:], in1=XH[:], op=ALU.add), reads=[HS, XH], writes=[HS])
                nt = T if with_ctx else NQ
                S.op("pool", lambda e: e.tensor_tensor(out=M[:, :nt], in0=LG[:, :nt], in1=LG[:, :nt], op=ALU.mult), reads=[LG], writes=[M])
                S.op("dve", lambda e: e.tensor_scalar(out=M[:, :nt], in0=M[:, :nt], scalar1=0.044715, scalar2=1.0, op0=ALU.mult, op1=ALU.add), reads=[M], writes=[M])
                S.op("pool", lambda e: e.tensor_tensor(out=M[:, :nt], in0=M[:, :nt], in1=LG[:, :nt], op=ALU.mult), reads=[M, LG], writes=[M])
                S.op("act", lambda e: e.activation(out=M[:, :nt], in_=M[:, :nt], func=AF.Sigmoid, scale=GELU_K), reads=[M], writes=[M])
                S.op("dve", lambda e: e.tensor_tensor(out=M[:, :nt], in0=M[:, :nt], in1=LG[:, :nt], op=ALU.mult), reads=[M, LG], writes=[M])
                S.op("dve", lambda e: e.tensor_tensor(out=ym[:, i, 0:NQ], in0=M[:, 0:NQ], in1=HS[:, 0:NQ], op=ALU.mult), reads=[M, HS], writes=[ym], skip_self=True)
                if with_ctx:
                    S.op("dve", lambda e: e.tensor_tensor(out=ym[:, i, NQ:T], in0=M[:, NQ:T], in1=HS[:, SEQ:TK], op=ALU.mult), reads=[M, HS], writes=[ym], skip_self=True)
            S.barrier()

    def swa(self, l, ym, with_ctx):
        S = self.S
        NQ, T, TK = self.NQ, self.T, self.TK
        nkt = TK // 128
        with ExitStack() as esl:
            SK = S.sb("swa_k", [128, TK], BF16, esl)
            SQ = S.sb("swa_q", [128, T], BF16, esl)
            VS = S.sb("swa_v", [128, nkt, 128], BF16, esl)
            VP = [S.sb(f"swa_vp{i}", [128, nkt, 128], BF16, esl) for i in range(2)]
            PT = [S.sb(f"swa_pt{i}", [128, 256], BF16, esl) for i in range(3)]
            es_ = S.sb("swa_es", [128, 4], F32, esl)
            zz = [S.sb(f"swa_zz{i}", [128, 128], F32, esl) for i in range(2)]
            S.op("act", lambda e: e.activation(out=es_[:], in_=self.SINK[:, l * 4:(l + 1) * 4], func=AF.Exp), reads=[self.SINK], writes=[es_])
            S.dma("sp", VS[:], self.svs.rearrange("(k p) e -> p k e", p=128), writes=[VS], sembuf=VS)
            cnt = 0
            qn = 0
            for g in range(2):
                esc = S.sb(f"swa_esc{g}", [128, 1], F32, esl)
                S.op("dve", lambda e: e.tensor_copy(out=esc[0:64, :], in_=es_[0:64, 2 * g:2 * g + 1]), reads=[es_], writes=[esc])
                S.op("dve", lambda e: e.tensor_copy(out=esc[64:128, :], in_=es_[64:128, 2 * g + 1:2 * g + 2]), reads=[es_], writes=[esc])
                S.dma("sp", SK[:], self.sks[g], writes=[SK], sembuf=SK)
                S.dma("sp", SQ[:], self.sqs[g], writes=[SQ], sembuf=SQ)
                S.op("pool", lambda e: e.memset(VP[0][:], 0.0), writes=[VP[0]])
                S.op("pool", lambda e: e.memset(VP[1][:], 0.0), writes=[VP[1]])
                S.op("pool", lambda e: e.tensor_copy(out=VP[0][:, :, 0:64], in_=VS[:, :, 64 * g:64 * g + 64]), reads=[VS], writes=[VP[0]])
                S.op("pool", lambda e: e.tensor_copy(out=VP[1][:, :, 64:128], in_=VS[:, :, 64 * g:64 * g + 64]), reads=[VS], writes=[VP[1]])
                nbl = NQ // 128
                qblocks = [(n, n * 128) for n in range(nbl)]
                if with_ctx:
                    qblocks += [(-1, NQ), (-1, NQ + 128)]
                units = []
                for (n, qc) in qblocks:
                    if n >= 0:
                        kts = ([(n - 1, 0)] if n > 0 else []) + [(n, None)] + ([(n + 1, 1)] if n < SEQ // 128 - 1 else []) + [(SEQ // 128, None), (SEQ // 128 + 1, None)]
                    else:
                        kts = [(SEQ // 128, None), (SEQ // 128 + 1, None)]
                    for ki, (kt, m) in enumerate(kts):
                        units.append((qc, kt, m, ki == 0, ki == len(kts) - 1, qn))
                    qn += 1

                def scores(u, idx):
                    qc, kt, m, first, lastk, q_ = u
                    pscs = (self.PB[(idx % 2) * 2], self.PB[(idx % 2) * 2 + 1])
                    for hh in range(2):
                        S.mm([lambda e: e.matmul(pscs[hh][:, 0:128], SK[hh * 64:(hh + 1) * 64, kt * 128:(kt + 1) * 128], SQ[hh * 64:(hh + 1) * 64, qc:qc + 128], start=True, stop=True)], reads=[SK, SQ], writes=[pscs[hh]])

                def rest(u, idx):
                    qc, kt, m, first, lastk, q_ = u
                    pscs = (self.PB[(idx % 2) * 2], self.PB[(idx % 2) * 2 + 1])
                    pt = PT[idx % 3]
                    po = self.PB[4 + (q_ % 2) * 2]
                    pz = self.PB[5 + (q_ % 2) * 2]
                    z = zz[q_ % 2]
                    S.op("act", lambda e: e.activation(out=pt[:, 0:128], in_=pscs[0][:, 0:128], func=AF.Exp, scale=0.125), reads=[pscs[0]], writes=[pt])
                    S.op("act", lambda e: e.activation(out=pt[:, 128:256], in_=pscs[1][:, 0:128], func=AF.Exp, scale=0.125), reads=[pscs[1]], writes=[pt], skip_self=True)
                    if m is not None:
                        S.op("pool", lambda e: e.tensor_tensor(out=pt[:].rearrange("p (h q) -> p h q", h=2), in0=pt[:].rearrange("p (h q) -> p h q", h=2), in1=self.masks[:, m, :].unsqueeze(1).to_broadcast([128, 2, 128]), op=ALU.mult), reads=[pt, self.masks], writes=[pt])
                    S.mm([lambda e: e.matmul(po[:, 0:128], VP[0][:, kt, :], pt[:, 0:128], start=first, stop=False),
                          lambda e: e.matmul(po[:, 0:128], VP[1][:, kt, :], pt[:, 128:256], start=False, stop=lastk)], reads=[VP[0], VP[1], pt], writes=[po])
                    S.mm([lambda e: e.matmul(pz[:, 0:128], self.onesL[:], pt[:, 0:128], start=first, stop=False),
                          lambda e: e.matmul(pz[:, 0:128], self.onesR[:], pt[:, 128:256], start=False, stop=lastk)], reads=[self.onesL, self.onesR, pt], writes=[pz])
                    if lastk:
                        S.op("dve", lambda e: e.tensor_scalar(out=z[:], in0=pz[:, 0:128], scalar1=esc[:, 0:1], scalar2=0.0, op0=ALU.add, op1=ALU.add), reads=[pz, esc], writes=[z])
                        S.op("dve", lambda e: e.reciprocal(out=z[:], in_=z[:]), reads=[z], writes=[z])
                        S.op("dve", lambda e: e.tensor_tensor(out=ym[:, 2 + g, qc:qc + 128], in0=po[:, 0:128], in1=z[:], op=ALU.mult), reads=[po, z], writes=[ym], skip_self=True)

                scores(units[0], cnt)
                for ui, u in enumerate(units):
                    if ui + 1 < len(units):
                        scores(units[ui + 1], cnt + ui + 1)
                    rest(u, cnt + ui)
                cnt += len(units)
            S.barrier()

    def diff(self, l, ym, with_ctx):
        S = self.S
        NQ, T, TK = self.NQ, self.T, self.TK
        nkt = TK // 128
        lam_init = 0.8 - 0.6 * math.exp(-0.3 * l)
        with ExitStack() as esl:
            DK = [S.sb(f"df_k{i}", [128, TK], BF16, esl) for i in range(2)]
            DQ = [S.sb(f"df_q{i}", [128, T], BF16, esl) for i in range(2)]
            DV = [S.sb(f"df_v{i}", [128, nkt, 128], BF16, esl) for i in range(2)]
            PT = [S.sb(f"df_pt{i}", [128, 1024], BF16, esl) for i in range(3)]
            r1 = S.sb("df_r1", [128, 512], F32, esl)
            r2 = S.sb("df_r2", [128, 512], F32, esl)
            o1 = S.sb("df_o1", [128, 512], F32, esl)
            o2 = S.sb("df_o2", [128, 512], F32, esl)
            osq = S.sb("df_osq", [128, 512], F32, esl)
            lt = S.sb("df_lt", [128, 2, 64], F32, esl)
            ls = S.sb("df_ls", [128, 2], F32, esl)
            nlam = S.sb("df_nlam", [128, 1], F32, esl)
            gsl = S.sb("df_gsl", [128, 1], F32, esl)
            dl = self.DLAM[:, l * 256:(l + 1) * 256].rearrange("p (a d) -> p a d", a=4)
            S.op("dve", lambda e: e.tensor_tensor(out=lt[:, 0, :], in0=dl[:, 0, :], in1=dl[:, 1, :], op=ALU.mult), reads=[self.DLAM], writes=[lt])
            S.op("dve", lambda e: e.tensor_tensor(out=lt[:, 1, :], in0=dl[:, 2, :], in1=dl[:, 3, :], op=ALU.mult), reads=[self.DLAM], writes=[lt], skip_self=True)
            S.op("dve", lambda e: e.tensor_reduce(out=ls[:], in_=lt[:], axis=mybir.AxisListType.X, op=ALU.add), reads=[lt], writes=[ls])
            S.op("act", lambda e: e.activation(out=ls[:], in_=ls[:], func=AF.Exp), reads=[ls], writes=[ls])
            S.op("dve", lambda e: e.tensor_tensor(out=nlam[:], in0=ls[:, 1:2], in1=ls[:, 0:1], op=ALU.subtract), reads=[ls], writes=[nlam])
            S.op("dve", lambda e: e.tensor_scalar(out=nlam[:], in0=nlam[:], scalar1=-lam_init, scalar2=0.0, op0=ALU.add, op1=ALU.add), reads=[nlam], writes=[nlam])
            S.op("dve", lambda e: e.tensor_scalar(out=gsl[:], in0=self.SUBG[:, l:l + 1], scalar1=1.0 - lam_init, scalar2=0.0, op0=ALU.mult, op1=ALU.add), reads=[self.SUBG], writes=[gsl])
            qtiles = [(i * 512, 512, list(range(nkt))) for i in range(NQ // 512)]
            if with_ctx:
                qtiles.append((NQ, NCTX, [SEQ // 128, SEQ // 128 + 1]))
            O1, O2, Z1, Z2 = self.PB[4], self.PB[5], self.PB[6], self.PB[7]

            def load(h):
                dk, dq, dv = DK[h % 2], DQ[h % 2], DV[h % 2]
                S.dma("sp", dk[:], self.dks[h], writes=[dk], sembuf=dk)
                S.dma("sp", dq[:], self.dqs[h], writes=[dq], sembuf=dq)
                S.dma("sp", dv[:], self.dvs[:, 128 * h:128 * (h + 1)].rearrange("(k p) e -> p k e", p=128), writes=[dv], sembuf=dv)

            units = []
            for h in range(4):
                for (qc, n, kts) in qtiles:
                    for ki, kt in enumerate(kts):
                        units.append((h, qc, n, kt, ki == 0, ki == len(kts) - 1))

            def scores(u, idx):
                h, qc, n, kt, first, lastk = u
                dk, dq = DK[h % 2], DQ[h % 2]
                pa = self.PB[(idx % 2) * 2]
                pb = self.PB[(idx % 2) * 2 + 1]
                S.mm([lambda e: e.matmul(pa[:, :n], dk[0:64, kt * 128:(kt + 1) * 128], dq[0:64, qc:qc + n], start=True, stop=True)], reads=[dk, dq], writes=[pa])
                S.mm([lambda e: e.matmul(pb[:, :n], dk[64:128, kt * 128:(kt + 1) * 128], dq[64:128, qc:qc + n], start=True, stop=True)], reads=[dk, dq], writes=[pb])

            def rest(u, idx):
                h, qc, n, kt, first, lastk = u
                dv = DV[h % 2]
                pa = self.PB[(idx % 2) * 2]
                pb = self.PB[(idx % 2) * 2 + 1]
                pt = PT[idx % 3]
                S.op("act", lambda e: e.activation(out=pt[:, 0:n], in_=pa[:, :n], func=AF.Exp, scale=0.125), reads=[pa], writes=[pt])
                S.op("act", lambda e: e.activation(out=pt[:, 512:512 + n], in_=pb[:, :n], func=AF.Exp, scale=0.125), reads=[pb], writes=[pt], skip_self=True)
                S.mm([lambda e: e.matmul(O1[:, :n], dv[:, kt, :], pt[:, 0:n], start=first, stop=lastk)], reads=[dv, pt], writes=[O1])
                S.mm([lambda e: e.matmul(O2[:, :n], dv[:, kt, :], pt[:, 512:512 + n], start=first, stop=lastk)], reads=[dv, pt], writes=[O2])
                S.mm([lambda e: e.matmul(Z1[:, :n], self.ones_b[:], pt[:, 0:n], start=first, stop=lastk)], reads=[self.ones_b, pt], writes=[Z1])
                S.mm([lambda e: e.matmul(Z2[:, :n], self.ones_b[:], pt[:, 512:512 + n], start=first, stop=lastk)], reads=[self.ones_b, pt], writes=[Z2])

            def finalize(u):
                h, qc, n, kt, first, lastk = u
                S.op("dve", lambda e: e.reciprocal(out=r1[:, :n], in_=Z1[:, :n]), reads=[Z1], writes=[r1])
                S.op("dve", lambda e: e.reciprocal(out=r2[:, :n], in_=Z2[:, :n]), reads=[Z2], writes=[r2])
                S.op("dve", lambda e: e.tensor_tensor(out=o1[:, :n], in0=O1[:, :n], in1=r1[:, :n], op=ALU.mult), reads=[O1, r1], writes=[o1])
                S.op("dve", lambda e: e.tensor_tensor(out=o2[:, :n], in0=O2[:, :n], in1=r2[:, :n], op=ALU.mult), reads=[O2, r2], writes=[o2])
                S.op("dve", lambda e: e.scalar_tensor_tensor(out=o1[:, :n], in0=o2[:, :n], scalar=nlam[:, 0:1], in1=o1[:, :n], op0=ALU.mult, op1=ALU.add), reads=[o1, o2, nlam], writes=[o1])
                S.op("act", lambda e: e.activation(out=osq[:, :n], in_=o1[:, :n], func=AF.Square), reads=[o1], writes=[osq])
                S.mm([lambda e: e.matmul(Z1[:, :n], self.ones_f[:], osq[:, :n], start=True, stop=True)], reads=[self.ones_f, osq], writes=[Z1])
                S.op("act", lambda e: e.activation(out=r1[:, :n], in_=Z1[:, :n], func=AF.Sqrt, scale=1.0 / 128, bias=self.eps_t[:]), reads=[Z1, self.eps_t], writes=[r1])
                S.op("dve", lambda e: e.reciprocal(out=r1[:, :n], in_=r1[:, :n]), reads=[r1], writes=[r1])
                S.op("dve", lambda e: e.scalar_tensor_tensor(out=ym[:, 4 + h, qc:qc + n], in0=o1[:, :n], scalar=gsl[:, 0:1], in1=r1[:, :n], op0=ALU.mult, op1=ALU.mult), reads=[o1, r1, gsl], writes=[ym], skip_self=True)

            load(0)
            load(1)
            scores(units[0], 0)
            for idx, u in enumerate(units):
                if idx + 1 < len(units):
                    scores(units[idx + 1], idx + 1)
                rest(u, idx)
                if u[5]:
                    finalize(u)
                    if (idx + 1 == len(units) or units[idx + 1][0] != u[0]) and u[0] + 2 < 4:
                        load(u[0] + 2)
            S.barrier()

    def final(self):
        S = self.S
        with ExitStack() as esl:
            hbs = [S.sb(f"hbf{i}", [128, NCH, 512], F32, esl) for i in range(2)]
            junk = S.sb("fjunk", [128, D], BF16, esl)
            ssq = [S.sb(f"fssq{i}", [128, 1], F32, esl) for i in range(2)]
            ot = [S.sb(f"fot{i}", [128, D], F32, esl) for i in range(2)]
            ss2 = [S.sb(f"fss2_{i}", [128, 2], F32, esl) for i in range(2)]
            cnt = 0
            for ti, (off, n, j) in enumerate(self.tiles):
                if j == 1:
                    continue
                b = hbs[ti % 2]
                S.dma("sp", b[:, :, :n], self.hsv(self.hs, off, n), writes=[b], sembuf=b)
                for sub in range(n // 128):
                    pa = self.PB[(cnt % 2) * 2]
                    pb = self.PB[(cnt % 2) * 2 + 1]
                    s_ = ssq[cnt % 2]
                    o_ = ot[cnt % 2]
                    cnt += 1
                    for half, pp in enumerate((pa, pb)):
                        ppv = pp[:, :].rearrange("p (c t) -> p c t", c=4)
                        S.mm([lambda e, cc=cc: e.transpose(ppv[:, cc, :], b[:, half * 4 + cc, sub * 128:(sub + 1) * 128], self.ident[:]) for cc in range(4)], reads=[b, self.ident], writes=[pp])
                    s2 = ss2[cnt % 2]
                    S.op("act", lambda e: e.activation(out=junk[:, 0:512], in_=pa[:, :], func=AF.Square, accum_out=s2[:, 0:1]), reads=[pa], writes=[junk, s2])
                    S.op("act", lambda e: e.activation(out=junk[:, 512:1024], in_=pb[:, :], func=AF.Square, accum_out=s2[:, 1:2]), reads=[pb], writes=[junk, s2])
                    S.op("dve", lambda e: e.tensor_tensor(out=s_[:], in0=s2[:, 0:1], in1=s2[:, 1:2], op=ALU.add), reads=[s2], writes=[s_])
                    S.op("act", lambda e: e.activation(out=s_[:], in_=s_[:], func=AF.Sqrt, scale=1.0 / D, bias=self.eps_t[:]), reads=[s_, self.eps_t], writes=[s_])
                    S.op("dve", lambda e: e.reciprocal(out=s_[:], in_=s_[:]), reads=[s_], writes=[s_])
                    S.op("dve", lambda e: e.scalar_tensor_tensor(out=o_[:, 0:512], in0=pa[:, :], scalar=s_[:, 0:1], in1=self.FG[:, 0:512], op0=ALU.mult, op1=ALU.mult), reads=[pa, s_, self.FG], writes=[o_])
                    S.op("dve", lambda e: e.scalar_tensor_tensor(out=o_[:, 512:1024], in0=pb[:, :], scalar=s_[:, 0:1], in1=self.FG[:, 512:1024], op0=ALU.mult, op1=ALU.mult), reads=[pb, s_, self.FG], writes=[o_], skip_self=True)
                    r0 = off + sub * 128
                    S.dma("sp", self.out[r0:r0 + 128, :], o_[:], reads=[o_], sembuf=o_)


def rope_tables(n0, n):
    t = np.arange(n0, n0 + n)
    row = (t // GRID_W).astype(np.float32)
    col = (t % GRID_W).astype(np.float32)
    nf = 16
    inv = (10000.0 ** (-np.arange(nf, dtype=np.float32) / nf)).astype(np.float32)
    ang = np.concatenate([row[:, None] * inv, col[:, None] * inv], axis=-1).astype(np.float32)
    idx = np.arange(128) % 32
    c = np.cos(ang).astype(np.float32)[:, idx].T
    s = np.sin(ang).astype(np.float32)[:, idx].T
    return np.ascontiguousarray(c), np.ascontiguousarray(s)


_NC_CACHE = {}


def kernel(**inputs):
    NQ = SEQ
    key = ("main", NQ)
    if key not in _NC_CACHE:
        _NC_CACHE[key] = Kern(NQ).build()
    nc = _NC_CACHE[key]
    f = lambda a: np.ascontiguousarray(np.asarray(a, dtype=np.float32))
    rc, rs = rope_tables(0, NQ)
    shared = {k: f(inputs[k]) for k in ("w_ada", "b_ada", "norm_g", "ffn1_w_gu", "ffn1_w_down", "ffn2_w_gu", "ffn2_w_down",
                                        "w_in", "w_out", "conv_w", "conv_b", "lru_w_r", "lru_b_r", "lru_w_i", "lru_b_i",
                                        "lru_lambda", "swa_sink", "diff_lambda", "diff_subln_g", "final_g")}
    x = f(inputs["x"])
    ctx = f(inputs["ctx"])
    c = f(inputs["c"])
    cc = f(inputs["c_ctx"])
    busy = {0: 0, 1: 1, 4: 2, 5: 3}
    b_ada = shared.pop("b_ada")
    in_maps = []
    for core in range(8):
        m = dict(shared)
        if core in busy:
            b = busy[core]
            m["x"] = x[b]
            m["ctx"] = ctx[b]
            m["cvec"] = np.ascontiguousarray(np.stack([c[b], cc], axis=0))
            m["b_ada"] = b_ada
        else:
            m["x"] = np.zeros_like(x[0])
            m["ctx"] = np.zeros_like(ctx[0])
            m["cvec"] = np.zeros((2, D), np.float32)
            m["b_ada"] = np.zeros_like(b_ada)
        m["ropec"] = rc
        m["ropes"] = rs
        in_maps.append(m)
    res = run_bass_kernel_spmd(nc, in_maps, core_ids=list(range(8)))
    cores = {b: core for core, b in busy.items()}
    out = np.stack([np.asarray(res.results[cores[b]]["out"], dtype=np.float32) for b in range(4)], axis=0)
    return out
```
